# Optimizing a Trainium2 kernel written in Bass

```python
import jax, jax.numpy as jnp
from jax import lax
import numpy as np

D_MODEL = 1024
BATCH = 16
SEQ = 2048
DEPTH = 2

CHUNK = 64
PLE_DIM = 256
D_FF = 2816
EPS = 1e-6

RNN_WIDTH = 1024
RNN_BLOCKS = 16
RNN_BLOCK = RNN_WIDTH // RNN_BLOCKS
CONV_WIDTH = 4
LRU_C = 8.0

MLA_HEADS = 16
Q_LORA = 384
KV_LORA = 256
QK_NOPE = 64
QK_ROPE = 32
V_DIM = 64
ROPE_THETA = 10000.0
Q_BLOCK = 128

SGU_WIDTH = 1024
SGU_GROUPS = 8
SGU_GROUP = SGU_WIDTH // SGU_GROUPS
SGU_LEN = 128

N_BRANCH = 3
IN_SIZES = (RNN_WIDTH, RNN_WIDTH, Q_LORA, KV_LORA, QK_ROPE, 2 * SGU_WIDTH, N_BRANCH * D_MODEL)
W_IN_COLS = RNN_WIDTH * 2 + Q_LORA + KV_LORA + QK_ROPE + 2 * SGU_WIDTH + N_BRANCH * D_MODEL

kernel_name = "hybrid_rglru_mla_sgu_macaron_block"


def _split_points(sizes):
    pts, acc = [], 0
    for s in sizes[:-1]:
        acc += s
        pts.append(acc)
    return pts


def rms_norm(x, g):
    xf = x.astype(jnp.float32)
    y = xf * lax.rsqrt(jnp.mean(xf * xf, axis=-1, keepdims=True) + EPS)
    return (y * g.astype(jnp.float32)).astype(x.dtype)


def layer_norm(x, g, b):
    xf = x.astype(jnp.float32)
    mu = jnp.mean(xf, axis=-1, keepdims=True)
    var = jnp.mean(jnp.square(xf - mu), axis=-1, keepdims=True)
    y = (xf - mu) * lax.rsqrt(var + EPS)
    return (y * g.astype(jnp.float32) + b.astype(jnp.float32)).astype(x.dtype)


def swiglu_ffn(x, norm_g, w_gu, w_down):
    h = rms_norm(x, norm_g) @ w_gu
    gate, up = jnp.split(h, 2, axis=-1)
    return (jax.nn.silu(gate) * up) @ w_down


def causal_depthwise_conv(x, w, b):
    S = x.shape[1]
    xp = jnp.pad(x, ((0, 0), (CONV_WIDTH - 1, 0), (0, 0)))
    out = b
    for k in range(CONV_WIDTH):
        out = out + xp[:, k:k + S] * w[k]
    return out


def rg_lru(x, w_a, b_a, w_x, b_x, lam):
    B, S, _ = x.shape
    xb = x.reshape(B, S, RNN_BLOCKS, RNN_BLOCK)
    r = jax.nn.sigmoid(jnp.einsum('bsnc,ncd->bsnd', xb, w_a).reshape(B, S, RNN_WIDTH) + b_a)
    i = jax.nn.sigmoid(jnp.einsum('bsnc,ncd->bsnd', xb, w_x).reshape(B, S, RNN_WIDTH) + b_x)
    log_a = -LRU_C * r.astype(jnp.float32) * jax.nn.softplus(-lam.astype(jnp.float32))
    a = jnp.exp(log_a)
    mult = jnp.sqrt(-jnp.expm1(2.0 * log_a))
    first = (jnp.arange(S) == 0)[None, :, None]
    mult = jnp.where(first, jnp.ones((), jnp.float32), mult)
    u = mult * (i * x).astype(jnp.float32)

    def combine(c1, c2):
        a1, b1 = c1
        a2, b2 = c2
        return a1 * a2, a2 * b1 + b2

    _, h = lax.associative_scan(combine, (a, u), axis=1)
    return h.astype(x.dtype)


def apply_rope(x, cos, sin):
    half = QK_ROPE // 2
    x1 = x[..., :half].astype(jnp.float32)
    x2 = x[..., half:].astype(jnp.float32)
    out = jnp.concatenate([x1 * cos - x2 * sin, x1 * sin + x2 * cos], axis=-1)
    return out.astype(x.dtype)


def mla_attention(c_q, c_kv, k_r, q_norm, w_uq, kv_norm, w_ukv):
    B, S, _ = c_q.shape
    half = QK_ROPE // 2
    inv_freq = ROPE_THETA ** (-jnp.arange(half, dtype=jnp.float32) / half)
    ang = jnp.arange(S, dtype=jnp.float32)[:, None] * inv_freq[None, :]
    cos, sin = jnp.cos(ang), jnp.sin(ang)

    q = (rms_norm(c_q, q_norm) @ w_uq).reshape(B, S, MLA_HEADS, QK_NOPE + QK_ROPE)
    q_nope = q[..., :QK_NOPE]
    q_rope = apply_rope(q[..., QK_NOPE:], cos[None, :, None, :], sin[None, :, None, :])
    kv = (rms_norm(c_kv, kv_norm) @ w_ukv).reshape(B, S, MLA_HEADS, QK_NOPE + V_DIM)
    k_nope, v = kv[..., :QK_NOPE], kv[..., QK_NOPE:]
    k_rope = apply_rope(k_r, cos[None], sin[None])
    scale = (QK_NOPE + QK_ROPE) ** -0.5

    outs = []
    for qb in range(S // Q_BLOCK):
        q0, kend = qb * Q_BLOCK, (qb + 1) * Q_BLOCK
        s = (jnp.einsum('bqhd,bkhd->bhqk', q_nope[:, q0:kend], k_nope[:, :kend])
             + jnp.einsum('bqhr,bkr->bhqk', q_rope[:, q0:kend], k_rope[:, :kend]))
        s = s.astype(jnp.float32) * scale
        qi = q0 + jnp.arange(Q_BLOCK)
        kj = jnp.arange(kend)
        mask = (kj[None, :] // CHUNK) <= (qi[:, None] // CHUNK)
        s = jnp.where(mask[None, None], s, -1e30)
        pr = jax.nn.softmax(s, axis=-1).astype(v.dtype)
        outs.append(jnp.einsum('bhqk,bkhd->bqhd', pr, v[:, :kend]))
    o = jnp.concatenate(outs, axis=1)
    return o.reshape(B, S, MLA_HEADS * V_DIM)


def spatial_gating(z, ln_g, ln_b, w_s, b_s):
    B, S, _ = z.shape
    u, v = jnp.split(z, 2, axis=-1)
    v = layer_norm(v, ln_g, ln_b)
    vb = v.reshape(B, S // SGU_LEN, SGU_LEN, SGU_GROUPS, SGU_GROUP)
    t = jnp.arange(SGU_LEN)
    mask = (t[None, :] // CHUNK) <= (t[:, None] // CHUNK)
    w = jnp.where(mask[None], w_s, jnp.zeros((), w_s.dtype))
    sp = jnp.einsum('gts,bnsgc->bntgc', w, vb) + b_s.T[None, None, :, :, None]
    return u * sp.reshape(B, S, SGU_WIDTH)


def parallel_mixer(x, mix_norm, w_in, conv_w, conv_b, lru_w_a, lru_b_a, lru_w_x, lru_b_x,
                   lru_lambda, w_read_a, mla_q_norm, mla_w_uq, mla_kv_norm, mla_w_ukv,
                   w_read_b, sgu_norm_g, sgu_norm_b, sgu_w_s, sgu_b_s, w_read_c,
                   gate_bias, w_out):
    B, S, _ = x.shape
    n = rms_norm(x, mix_norm)
    z = n @ w_in
    xa, ga, cq, ckv, kr, zc, gl = jnp.split(z, _split_points(IN_SIZES), axis=-1)
    ya = rg_lru(causal_depthwise_conv(xa, conv_w, conv_b),
                lru_w_a, lru_b_a, lru_w_x, lru_b_x, lru_lambda) * jax.nn.gelu(ga)
    ya = ya @ w_read_a
    yb = mla_attention(cq, ckv, kr, mla_q_norm, mla_w_uq, mla_kv_norm, mla_w_ukv) @ w_read_b
    yc = spatial_gating(jax.nn.gelu(zc), sgu_norm_g, sgu_norm_b, sgu_w_s, sgu_b_s) @ w_read_c
    g = jax.nn.sigmoid(gl + gate_bias).reshape(B, S, N_BRANCH, D_MODEL)
    merged = g[:, :, 0] * ya + g[:, :, 1] * yb + g[:, :, 2] * yc
    return merged @ w_out


def setup_inputs(seed: int = 0) -> dict:
    key = jax.random.key(seed)
    ks = iter(jax.random.split(key, 64))
    f32 = jnp.float32

    def dense(shape, fan_in):
        return jax.random.normal(next(ks), shape, f32) * (fan_in ** -0.5)

    def gain(shape):
        return 1.0 + 0.05 * jax.random.normal(next(ks), shape, f32)

    def bias(shape, scale=0.01):
        return scale * jax.random.normal(next(ks), shape, f32)

    L = DEPTH
    a0 = jax.random.uniform(next(ks), (L, RNN_WIDTH), f32, 0.9, 0.999)
    inputs = {
        "x": jax.random.normal(next(ks), (BATCH, SEQ, D_MODEL), f32),
        "p": jax.random.normal(next(ks), (DEPTH, BATCH, SEQ, PLE_DIM), f32),
        "ffn1_norm": gain((L, D_MODEL)),
        "ffn1_w_gu": dense((L, D_MODEL, 2 * D_FF), D_MODEL),
        "ffn1_w_down": dense((L, D_FF, D_MODEL), D_FF),
        "mix_norm": gain((L, D_MODEL)),
        "w_in": dense((L, D_MODEL, W_IN_COLS), D_MODEL),
        "conv_w": dense((L, CONV_WIDTH, RNN_WIDTH), CONV_WIDTH),
        "conv_b": bias((L, RNN_WIDTH)),
        "lru_w_a": dense((L, RNN_BLOCKS, RNN_BLOCK, RNN_BLOCK), RNN_BLOCK),
        "lru_b_a": bias((L, RNN_WIDTH)),
        "lru_w_x": dense((L, RNN_BLOCKS, RNN_BLOCK, RNN_BLOCK), RNN_BLOCK),
        "lru_b_x": bias((L, RNN_WIDTH)),
        "lru_lambda": jnp.log(a0) - jnp.log1p(-a0),
        "w_read_a": dense((L, RNN_WIDTH, D_MODEL), RNN_WIDTH),
        "mla_q_norm": gain((L, Q_LORA)),
        "mla_w_uq": dense((L, Q_LORA, MLA_HEADS * (QK_NOPE + QK_ROPE)), Q_LORA),
        "mla_kv_norm": gain((L, KV_LORA)),
        "mla_w_ukv": dense((L, KV_LORA, MLA_HEADS * (QK_NOPE + V_DIM)), KV_LORA),
        "w_read_b": dense((L, MLA_HEADS * V_DIM, D_MODEL), MLA_HEADS * V_DIM),
        "sgu_norm_g": gain((L, SGU_WIDTH)),
        "sgu_norm_b": bias((L, SGU_WIDTH)),
        "sgu_w_s": dense((L, SGU_GROUPS, SGU_LEN, SGU_LEN), SGU_LEN),
        "sgu_b_s": 1.0 + bias((L, SGU_GROUPS, SGU_LEN), 0.1),
        "w_read_c": dense((L, SGU_WIDTH, D_MODEL), SGU_WIDTH),
        "gate_bias": bias((L, N_BRANCH * D_MODEL)),
        "w_out": dense((L, D_MODEL, D_MODEL), D_MODEL),
        "ffn2_norm": gain((L, D_MODEL)),
        "ffn2_w_gu": dense((L, D_MODEL, 2 * D_FF), D_MODEL),
        "ffn2_w_down": dense((L, D_FF, D_MODEL), D_FF),
        "ple_norm": gain((L, D_MODEL)),
        "ple_w_gate": dense((L, D_MODEL, D_MODEL), D_MODEL),
        "ple_w_proj": dense((L, PLE_DIM, D_MODEL), PLE_DIM),
        "final_norm": gain((D_MODEL,)),
    }
    return inputs


def reference(x, p, ffn1_norm, ffn1_w_gu, ffn1_w_down, mix_norm, w_in, conv_w, conv_b,
              lru_w_a, lru_b_a, lru_w_x, lru_b_x, lru_lambda, w_read_a, mla_q_norm,
              mla_w_uq, mla_kv_norm, mla_w_ukv, w_read_b, sgu_norm_g, sgu_norm_b,
              sgu_w_s, sgu_b_s, w_read_c, gate_bias, w_out, ffn2_norm, ffn2_w_gu,
              ffn2_w_down, ple_norm, ple_w_gate, ple_w_proj, final_norm):
    for i in range(DEPTH):
        x = x + 0.5 * swiglu_ffn(x, ffn1_norm[i], ffn1_w_gu[i], ffn1_w_down[i])
        x = x + parallel_mixer(x, mix_norm[i], w_in[i], conv_w[i], conv_b[i], lru_w_a[i],
                               lru_b_a[i], lru_w_x[i], lru_b_x[i], lru_lambda[i], w_read_a[i],
                               mla_q_norm[i], mla_w_uq[i], mla_kv_norm[i], mla_w_ukv[i],
                               w_read_b[i], sgu_norm_g[i], sgu_norm_b[i], sgu_w_s[i],
                               sgu_b_s[i], w_read_c[i], gate_bias[i], w_out[i])
        x = x + 0.5 * swiglu_ffn(x, ffn2_norm[i], ffn2_w_gu[i], ffn2_w_down[i])
        gate = jax.nn.sigmoid(rms_norm(x, ple_norm[i]) @ ple_w_gate[i])
        x = x + (p[i] @ ple_w_proj[i]) * gate
    return rms_norm(x, final_norm)
```

```python
from contextlib import ExitStack
import numpy as np
import concourse.bass as bass
import concourse.mybir as mybir
from concourse.bass_utils import run_bass_kernel_spmd

F32 = mybir.dt.float32
BF16 = mybir.dt.bfloat16
ALU = mybir.AluOpType
AF = mybir.ActivationFunctionType

D = 1024
S = 2048
L = 2
DFF = 2816
PLE = 256
TS = 1024
TT = 512
EPS = 1e-6
NH = 16
ENGS = ("pe", "act", "dve", "pool", "sp")
EPOCH = 16000
C_XA, C_GA, C_CQ, C_CKV, C_KR, C_ZU, C_ZV, C_GL = 0, 1024, 2048, 2432, 2688, 2720, 3744, 4768
PRM_LAYOUT = [("ffn1_norm", 8), ("mix_norm", 8), ("ffn2_norm", 8), ("ple_norm", 8), ("conv_w", 32),
              ("conv_b", 8), ("lru_b_a", 8), ("lru_b_x", 8), ("lru_lambda", 8), ("mla_q_norm", 3),
              ("mla_kv_norm", 2), ("sgu_norm_g", 8), ("sgu_norm_b", 8), ("gate_bias", 24)]
PRM_OFF = {}
_o = 0
for _n, _k in PRM_LAYOUT:
    PRM_OFF[_n] = _o
    _o += _k
PRM_PER_LAYER = _o
PRM_FINAL = L * PRM_PER_LAYER
PRM_ROWS = 384
FFN_GROUPS = [(0, 6), (6, 6), (12, 5), (17, 5)]


class Buf:
    __slots__ = ("lw", "rd")

    def __init__(self):
        self.lw = None
        self.rd = {}


class Fw:
    def __init__(self, nc, stack):
        self.nc = nc
        self.stack = stack
        self.q = {e: [] for e in ENGS}
        self.sems = {}
        self.cnt = {}
        self.seen = {e: {} for e in ENGS}
        self.cur = {}
        self.last = {}
        self.pending_sp = {}
        for e in ("pe", "act", "dve", "pool"):
            self._new_epoch(e)

    def new_sem(self, key):
        self.sems[key] = self.stack.enter_context(self.nc.semaphore(key.replace("#", "_")))
        self.cnt[key] = 0

    def _new_epoch(self, e):
        k = "%s#%d" % (e, sum(1 for x in self.sems if x.startswith(e + "#")))
        self.new_sem(k)
        self.cur[e] = k

    @staticmethod
    def _add(deps, t):
        if t is not None and deps.get(t[0], 0) < t[1]:
            deps[t[0]] = t[1]

    def deps(self, reads, writes, extra=()):
        deps = {}
        for b in reads:
            self._add(deps, b.lw)
        for b in writes:
            self._add(deps, b.lw)
            for k, v in b.rd.items():
                if deps.get(k, 0) < v:
                    deps[k] = v
        for t in extra:
            self._add(deps, t)
        return deps

    def wait(self, eng, deps):
        seen = self.seen[eng]
        for k, v in deps.items():
            if eng == "pe" and k.startswith("pe#"):
                continue
            if seen.get(k, 0) >= v:
                continue
            seen[k] = v
            self.q[eng].append(("w", k, v))

    def mark(self, tok, reads, writes):
        k, v = tok
        for b in reads:
            if b.rd.get(k, 0) < v:
                b.rd[k] = v
        for b in writes:
            b.lw = tok
            b.rd = {}

    def raw(self, eng, fn):
        if self.cnt[self.cur[eng]] >= EPOCH:
            self._new_epoch(eng)
        key = self.cur[eng]
        self.cnt[key] += 1
        self.q[eng].append(("o", fn, key, 1))
        self.last[eng] = (key, self.cnt[key])
        return (key, self.cnt[key])

    def op(self, eng, fn, reads=(), writes=(), extra=()):
        self.wait(eng, self.deps(reads, writes, extra))
        tok = self.raw(eng, fn)
        self.mark(tok, reads, writes)
        return tok

    def dma_raw(self, qeng, chan, out, in_):
        if chan not in self.sems:
            self.new_sem(chan)
        self.cnt[chan] += 16
        self.q[qeng].append(("o", lambda E: E.dma_start(out=out, in_=in_), chan, 16))
        return (chan, self.cnt[chan])

    def dma(self, qeng, chan, out, in_, reads=(), writes=(), extra=()):
        self.wait(qeng, self.deps(reads, writes, extra))
        tok = self.dma_raw(qeng, chan, out, in_)
        self.mark(tok, reads, writes)
        if qeng == "sp":
            self.pending_sp[tok[0]] = tok[1]
        return tok

    def barrier(self):
        toks = {}
        for e in ("pe", "act", "dve", "pool"):
            t = self.last.get(e)
            if t is not None:
                toks[t[0]] = t[1]
        toks.update(self.pending_sp)
        self.pending_sp = {}
        for e in ("pe", "act", "dve", "pool", "sp"):
            self.wait(e, toks)

    def emit(self):
        nc = self.nc
        with nc.Block() as block:
            def run(eng):
                def body(E):
                    sems = self.sems
                    for it in self.q[eng]:
                        if it[0] == "w":
                            E.wait_ge(sems[it[1]], it[2])
                        else:
                            it[1](E).then_inc(sems[it[2]], it[3])
                return body
            block.tensor(run("pe"))
            block.scalar(run("act"))
            block.vector(run("dve"))
            block.gpsimd(run("pool"))
            block.sync(run("sp"))


class Carver:
    def __init__(self, scr, base=0):
        self.scr = scr
        self.off = base

    def take(self, shape, dtype):
        n = 1
        for s in shape:
            n *= s
        nbytes = n * (4 if dtype == F32 else 2)
        nbytes = (nbytes + 31) // 32 * 32
        o = self.off
        self.off += nbytes
        v = self.scr[:, o // 4:(o + nbytes) // 4]
        if dtype != F32:
            v = v.bitcast(dtype)
        v = v[:, 0:n]
        if len(shape) == 2:
            v = v.rearrange("p (a b) -> p a b", a=shape[0])
        elif len(shape) == 3:
            v = v.rearrange("p (a b c) -> p a b c", a=shape[0], b=shape[1])
        return v


def build(n_seq=2, n_layers=2, phases=("ffn1", "B", "A", "C", "wout", "ffn2", "ple"), n_seg=2, dbg=()):
    nc = bass.Bass("TRN2", target_bir_lowering=False, dynamic_dma_scratch_size=8192)
    dbg_done = set()
    NT = n_seq * S
    dr = lambda n, s: nc.dram_tensor(n, s, F32, kind="ExternalInput").ap()
    x_d = dr("x", [NT, D])
    p_d = dr("p", [L, NT, PLE])
    out_d = nc.dram_tensor("out", [NT, D], F32, kind="ExternalOutput").ap()
    wgu_d = [dr("ffn1_w_gu", [L, D, 2 * DFF]), dr("ffn2_w_gu", [L, D, 2 * DFF])]
    wdn_d = [dr("ffn1_w_down", [L, DFF, D]), dr("ffn2_w_down", [L, DFF, D])]
    win_d = dr("w_in", [L, D, 7840])
    wattn_d = dr("w_attn", [L, D, 768])
    wuqx_d = dr("w_uqx", [L, 384, 2048])
    wukv_d = dr("mla_w_ukv", [L, 256, 2048])
    wra_d = dr("w_read_a", [L, D, D])
    wrb_d = dr("w_read_b", [L, D, D])
    wrc_d = dr("w_read_c", [L, D, D])
    wout_d = dr("w_out", [L, D, D])
    wpg_d = dr("ple_w_gate", [L, D, D])
    wpp_d = dr("ple_w_proj", [L, PLE, D])
    lwa_d = dr("lru_w_a", [L, 16, 64, 64])
    lwx_d = dr("lru_w_x", [L, 16, 64, 64])
    swT_d = dr("sgu_wT", [L, 128, 8, 128])
    sbs_d = dr("sgu_b_s", [L, 1024])
    prm_d = dr("prm", [PRM_ROWS, 128])
    rope_d = dr("rope", [64, S])
    ident_d = dr("ident", [128, 128])

    st = ExitStack()
    fw = Fw(nc, st)
    sb = lambda n, s, d: st.enter_context(nc.sbuf_tensor(n, s, d))
    WAW = 18432
    WA = [sb("wa%d" % i, [128, WAW], BF16) for i in range(2)]
    WAb = [Buf(), Buf()]
    X = sb("X", [128, 8, TS], F32)
    Xb = [[Buf() for _ in range(2)] for _ in range(8)]
    N = sb("N", [128, 8, TS], BF16)
    Nb = [[Buf() for _ in range(2)] for _ in range(8)]
    CKV = [sb("ckv%d" % l, [128, 2, TS], BF16) for l in range(L)] + [sb("ckvc", [128, 2, TS], BF16)]
    CKVb = [Buf() for _ in range(L + 1)]
    KR = [sb("kr%d" % l, [128, TS], BF16) for l in range(L)] + [sb("krc", [128, TS], BF16)]
    KRb = [Buf() for _ in range(L + 1)]
    PRM = sb("prm_sb", [128, PRM_ROWS], F32); PRMb = Buf()
    IDENT = sb("ident_sb", [128, 128], F32); IDENTb = Buf()
    ONES = sb("ones", [128, 128], BF16); ONESb = Buf()
    CST = sb("cst", [128, 4], F32); CSTb = Buf()
    CLAM = sb("clam", [128, L, 2, 8], F32); CLAMb = Buf()
    HALO = sb("halo", [128, L, 8, 3], F32); HALOb = [[Buf() for _ in range(8)] for _ in range(L)]
    CAR = sb("car", [128, L, 8], F32); CARb = [[Buf() for _ in range(8)] for _ in range(L)]
    SQ = [sb("sq%d" % i, [128, TT], BF16) for i in range(2)]; SQb = [Buf(), Buf()]
    STD = sb("std", [128, TT], F32); STDb = Buf()
    RSTD = [sb("rstd%d" % i, [128, TT], F32) for i in range(2)]; RSTDb = [Buf(), Buf()]
    SCRW = 16384
    SCR = sb("scr", [128, SCRW], F32)
    PS = [st.enter_context(nc.psum_tensor("ps%d" % i, [128, TT], F32)) for i in range(8)]
    PSb = [Buf() for _ in range(8)]
    rot = {"a": [0, 1, 2, 3], "b": [4, 5], "c": [6, 7]}
    rotp = {"a": 0, "b": 0, "c": 0}

    def bank(role):
        lst = rot[role]
        i = lst[rotp[role] % len(lst)]
        rotp[role] += 1
        return i

    def mm(bk, out_ap, pairs, reads):
        fw.wait("pe", fw.deps(reads, [PSb[bk]]))
        n = len(pairs)
        tok = None
        for i, (lh, rh) in enumerate(pairs):
            tok = fw.raw("pe", lambda E, lh=lh, rh=rh, i=i: E.matmul(out=out_ap, lhsT=lh, rhs=rh,
                                                                    start=(i == 0), stop=(i == n - 1)))
        fw.mark(tok, reads, [PSb[bk]])
        return tok

    def prm_col(l, name, c):
        j = l * PRM_PER_LAYER + PRM_OFF[name] + c
        return PRM[:, j:j + 1]

    def tsl(tt):
        return slice(tt * TT, (tt + 1) * TT)

    eps_ap = CST[:, 0:1]
    one_ap = CST[:, 1:2]

    fw.dma("sp", "ld_id", IDENT[:], ident_d, writes=[IDENTb])
    fw.op("dve", lambda E: E.memset(ONES[:], 1.0), writes=[ONESb])
    fw.op("dve", lambda E: E.memset(CST[:, 0:1], EPS), writes=[CSTb])
    fw.op("dve", lambda E: E.memset(CST[:, 1:2], 1.0), writes=[CSTb])
    PTMP = SCR[:, 0:PRM_ROWS].rearrange("p (a b) -> p a b", a=3)
    PTMPb = Buf()
    fw.dma("sp", "ld_prm", PTMP, prm_d.rearrange("(a p) f -> p a f", p=128), writes=[PTMPb])
    bk = bank("a")
    for a in range(3):
        fw.op("pe", lambda E, a=a: E.transpose(out=PS[bk][:, a * 128:(a + 1) * 128], in_=PTMP[:, a, :],
                                               identity=IDENT[:]), reads=[PTMPb, IDENTb], writes=[PSb[bk]])
    fw.op("dve", lambda E: E.tensor_copy(out=PRM[:], in_=PS[bk][:, 0:PRM_ROWS]), reads=[PSb[bk]], writes=[PRMb])
    for l in range(n_layers):
        j = l * PRM_PER_LAYER + PRM_OFF["lru_lambda"]
        fw.op("act", lambda E, l=l, j=j: E.activation(out=CLAM[:, l, 0, :], in_=PRM[:, j:j + 8], func=AF.Exp, scale=-1.0),
              reads=[PRMb], writes=[CLAMb])
        fw.op("act", lambda E, l=l: E.activation(out=CLAM[:, l, 0, :], in_=CLAM[:, l, 0, :], func=AF.Ln, bias=one_ap),
              reads=[CLAMb, CSTb], writes=[CLAMb])
        fw.op("dve", lambda E, l=l: E.tensor_scalar(out=CLAM[:, l, 1, :], in0=CLAM[:, l, 0, :], scalar1=-16.0, scalar2=None,
                                                    op0=ALU.mult), reads=[CLAMb], writes=[CLAMb])
        fw.op("dve", lambda E, l=l: E.tensor_scalar(out=CLAM[:, l, 0, :], in0=CLAM[:, l, 0, :], scalar1=-8.0, scalar2=None,
                                                    op0=ALU.mult), reads=[CLAMb], writes=[CLAMb])

    rk = [0]

    def rmsnorm(src_aps, src_bufs, gain_aps, out_aps, out_bufs, dn, n=TT):
        nch = len(src_aps)
        bk = bank("c")
        fw.wait("pe", fw.deps([], [PSb[bk]]))
        tok = None
        for c in range(nch):
            k = c % 2
            fw.op("act", lambda E, c=c, k=k: E.activation(out=SQ[k][:, 0:n], in_=src_aps[c], func=AF.Square),
                  reads=[src_bufs[c]], writes=[SQb[k]])
            fw.wait("pe", fw.deps([SQb[k], ONESb], []))
            tok = fw.raw("pe", lambda E, c=c, k=k: E.matmul(out=PS[bk][:, 0:n], lhsT=ONES[:], rhs=SQ[k][:, 0:n],
                                                           start=(c == 0), stop=(c == nch - 1)))
            fw.mark(tok, [SQb[k], ONESb], [])
        fw.mark(tok, [], [PSb[bk]])
        fw.op("act", lambda E: E.activation(out=STD[:, 0:n], in_=PS[bk][:, 0:n], func=AF.Sqrt, scale=1.0 / dn, bias=eps_ap),
              reads=[PSb[bk], CSTb], writes=[STDb])
        r = rk[0] % 2
        rk[0] += 1
        fw.op("dve", lambda E: E.reciprocal(out=RSTD[r][:, 0:n], in_=STD[:, 0:n]), reads=[STDb], writes=[RSTDb[r]])
        for c in range(nch):
            fw.op("dve", lambda E, c=c: E.scalar_tensor_tensor(out=out_aps[c], in0=src_aps[c], scalar=gain_aps[c],
                                                               in1=RSTD[r][:, 0:n], op0=ALU.mult, op1=ALU.mult),
                  reads=[src_bufs[c], RSTDb[r], PRMb], writes=[out_bufs[c]])

    def norm_x(gname, l):
        for tt in range(2):
            rmsnorm([X[:, c, tsl(tt)] for c in range(8)], [Xb[c][tt] for c in range(8)],
                    [prm_col(l, gname, c) for c in range(8)],
                    [N[:, c, tsl(tt)] for c in range(8)], [Nb[c][tt] for c in range(8)], D)

    def wview(W, off, shape):
        n = 1
        for s_ in shape:
            n *= s_
        v = W[:, off:off + n]
        if len(shape) == 2:
            v = v.rearrange("p (a b) -> p a b", a=shape[0])
        elif len(shape) == 3:
            v = v.rearrange("p (a b c) -> p a b c", a=shape[0], b=shape[1])
        return v, off + n

    def kview(src2d):
        return src2d.rearrange("(c p) f -> p c f", p=128)

    stages = []

    def st_load_x(sq, seg):
        def f(W, Wb, Wo, Wob):
            def comp():
                fw.barrier()
                cv = Carver(SCR)
                XT = [cv.take([D], F32) for _ in range(2)]
                XTb = [Buf(), Buf()]
                for l in range(n_layers):
                    if seg == 0:
                        fw.op("dve", lambda E, l=l: E.memset(HALO[:, l, :, :], 0.0), writes=HALOb[l])
                        fw.op("dve", lambda E, l=l: E.memset(CAR[:, l, :], 0.0), writes=CARb[l])
                for tb in range(8):
                    k = tb % 2
                    r0 = sq * S + seg * TS + tb * 128
                    fw.dma("sp", "xl%d" % k, XT[k], x_d[r0:r0 + 128, :], writes=[XTb[k]])
                    for hh in range(2):
                        bk = bank("a")
                        fw.wait("pe", fw.deps([XTb[k], IDENTb], [PSb[bk]]))
                        tok = None
                        for c4 in range(4):
                            c = hh * 4 + c4
                            tok = fw.raw("pe", lambda E, c=c, c4=c4, k=k, bk=bk: E.transpose(
                                out=PS[bk][:, c4 * 128:(c4 + 1) * 128], in_=XT[k][:, c * 128:(c + 1) * 128], identity=IDENT[:]))
                        fw.mark(tok, [XTb[k], IDENTb], [PSb[bk]])
                        tt = tb // 4
                        dst = X[:, hh * 4:hh * 4 + 4, tb * 128:(tb + 1) * 128]
                        src = PS[bk][:, :].rearrange("p (a b) -> p a b", a=4)
                        wr = [Xb[c][tt] for c in range(hh * 4, hh * 4 + 4)]
                        if hh == 0:
                            fw.op("act", lambda E, dst=dst, src=src: E.activation(out=dst, in_=src, func=AF.Copy),
                                  reads=[PSb[bk]], writes=wr)
                        else:
                            fw.op("dve", lambda E, dst=dst, src=src: E.tensor_copy(out=dst, in_=src),
                                  reads=[PSb[bk]], writes=wr)
            return [], comp
        return f

    out_toks = []

    def tap(name, ap, bufs):
        if name not in dbg or name in dbg_done:
            return
        dbg_done.add(name)
        dd = nc.dram_tensor("dbg_" + name, list(ap.shape), ap.dtype, kind="ExternalOutput").ap()
        out_toks.append(fw.dma("sp", "dbg", dd, ap, reads=bufs))

    def st_store(sq, seg):
        def f(W, Wb, Wo, Wob):
            def comp():
                fw.barrier()
                cv = Carver(SCR)
                Y = cv.take([8, TT], F32); Yb = [Buf() for _ in range(8)]
                OTK = [cv.take([D], F32) for _ in range(2)]; OTKb = [Buf(), Buf()]
                for tt in range(2):
                    rmsnorm([X[:, c, tsl(tt)] for c in range(8)], [Xb[c][tt] for c in range(8)],
                            [PRM[:, PRM_FINAL + c:PRM_FINAL + c + 1] for c in range(8)],
                            [Y[:, c, :] for c in range(8)], Yb, D)
                    for blk in range(4):
                        k = blk % 2
                        for hh in range(2):
                            bk = bank("a")
                            rd = [Yb[c] for c in range(hh * 4, hh * 4 + 4)] + [IDENTb]
                            fw.wait("pe", fw.deps(rd, [PSb[bk]]))
                            tok = None
                            for c4 in range(4):
                                c = hh * 4 + c4
                                tok = fw.raw("pe", lambda E, c=c, c4=c4, bk=bk, blk=blk: E.transpose(
                                    out=PS[bk][:, c4 * 128:(c4 + 1) * 128], in_=Y[:, c, blk * 128:(blk + 1) * 128], identity=IDENT[:]))
                            fw.mark(tok, rd, [PSb[bk]])
                            dst = OTK[k][:, hh * 512:(hh + 1) * 512]
                            if hh == 0:
                                fw.op("act", lambda E, dst=dst, bk=bk: E.activation(out=dst, in_=PS[bk][:, :], func=AF.Copy),
                                      reads=[PSb[bk]], writes=[OTKb[k]])
                            else:
                                fw.op("dve", lambda E, dst=dst, bk=bk: E.tensor_copy(out=dst, in_=PS[bk][:, :]),
                                      reads=[PSb[bk]], writes=[OTKb[k]])
                        r0 = sq * S + seg * TS + tt * TT + blk * 128
                        out_toks.append(fw.dma("sp", "st%d" % k, out_d[r0:r0 + 128, :], OTK[k], reads=[OTKb[k]]))
            return [], comp
        return f

    def st_ffn(l, which, gi):
        f0, nf = FFN_GROUPS[gi]
        gname = "ffn1_norm" if which == 0 else "ffn2_norm"

        def f(W, Wb, Wo, Wob):
            wg, o = wview(W, 0, [8, nf * 128])
            wu, o = wview(W, o, [8, nf * 128])
            wd, o = wview(W, o, [nf, D])
            dmas = [(wg, kview(wgu_d[which][l, :, f0 * 128:(f0 + nf) * 128])),
                    (wu, kview(wgu_d[which][l, :, DFF + f0 * 128:DFF + (f0 + nf) * 128])),
                    (wd, wdn_d[which][l, f0 * 128:(f0 + nf) * 128, :].rearrange("(f p) d -> p f d", p=128))]

            def comp():
                cv = Carver(SCR)
                H = cv.take([2, 6, TT], BF16)
                SG = [cv.take([TT], F32) for _ in range(2)]
                if gi == 0:
                    fw.barrier()
                    st_ffn.Hb = [[Buf() for _ in range(6)] for _ in range(2)]
                    st_ffn.SGb = [Buf(), Buf()]
                    norm_x(gname, l)
                Hb, SGb = st_ffn.Hb, st_ffn.SGb
                k = 0
                for tt in range(2):
                    nrd = [Nb[c][tt] for c in range(8)] + [Wb]
                    for fi in range(nf):
                        bg, bu = bank("a"), bank("a")
                        mm(bg, PS[bg][:, :], [(wg[:, c, fi * 128:(fi + 1) * 128], N[:, c, tsl(tt)]) for c in range(8)], nrd)
                        mm(bu, PS[bu][:, :], [(wu[:, c, fi * 128:(fi + 1) * 128], N[:, c, tsl(tt)]) for c in range(8)], nrd)
                        kk = k % 2
                        k += 1
                        fw.op("act", lambda E, kk=kk, bg=bg: E.activation(out=SG[kk], in_=PS[bg][:, :], func=AF.Silu),
                              reads=[PSb[bg]], writes=[SGb[kk]])
                        fw.op("dve", lambda E, kk=kk, bu=bu, tt=tt, fi=fi: E.tensor_tensor(
                            out=H[:, tt, fi, :], in0=SG[kk], in1=PS[bu][:, :], op=ALU.mult),
                            reads=[SGb[kk], PSb[bu]], writes=[Hb[tt][fi]])
                for tt in range(2):
                    hrd = [Hb[tt][fi] for fi in range(nf)] + [Wb]
                    for d in range(8):
                        bd = bank("b")
                        mm(bd, PS[bd][:, :], [(wd[:, fi, d * 128:(d + 1) * 128], H[:, tt, fi, :]) for fi in range(nf)], hrd)
                        fw.op("dve", lambda E, d=d, tt=tt, bd=bd: E.scalar_tensor_tensor(
                            out=X[:, d, tsl(tt)], in0=PS[bd][:, :], scalar=0.5, in1=X[:, d, tsl(tt)],
                            op0=ALU.mult, op1=ALU.add), reads=[PSb[bd], Xb[d][tt]], writes=[Xb[d][tt]])
            return dmas, comp
        return f

    MGcv = Carver(SCR)
    MG = MGcv.take([8, TS], BF16)
    MGb = [[Buf() for _ in range(2)] for _ in range(8)]
    MIX_BASE = MGcv.off
    mix_first = [True]

    def gate_merge(l, br, tt, yw, Y, Yb, glw, Wbs, TMPG, TMPGb, GT, GTb):
        first = mix_first[0]
        for d in range(8):
            by, bgl = bank("a"), bank("a")
            mm(by, PS[by][:, :], [(yw[:, c, d * 128:(d + 1) * 128], Y[:, c, :]) for c in range(8)], Yb + Wbs)
            mm(bgl, PS[bgl][:, :], [(glw[:, c, d * 128:(d + 1) * 128], N[:, c, tsl(tt)]) for c in range(8)],
               [Nb[c][tt] for c in range(8)] + Wbs)
            k = d % 2
            fw.op("act", lambda E, k=k, bgl=bgl, d=d: E.activation(out=GT[k], in_=PS[bgl][:, :], func=AF.Sigmoid,
                                                                   bias=prm_col(l, "gate_bias", br * 8 + d)),
                  reads=[PSb[bgl], PRMb], writes=[GTb[k]])
            if first:
                fw.op("dve", lambda E, k=k, by=by, d=d: E.tensor_tensor(out=MG[:, d, tsl(tt)], in0=GT[k], in1=PS[by][:, :], op=ALU.mult),
                      reads=[GTb[k], PSb[by]], writes=[MGb[d][tt]])
            else:
                fw.op("dve", lambda E, k=k, by=by: E.tensor_tensor(out=TMPG[k], in0=GT[k], in1=PS[by][:, :], op=ALU.mult),
                      reads=[GTb[k], PSb[by]], writes=[TMPGb[k]])
                fw.op("pool", lambda E, k=k, d=d: E.tensor_tensor(out=MG[:, d, tsl(tt)], in0=MG[:, d, tsl(tt)], in1=TMPG[k], op=ALU.add),
                      reads=[TMPGb[k], MGb[d][tt]], writes=[MGb[d][tt]])

    def st_B1(l, sq, seg):
        def f(W, Wb, Wo, Wob):
            wat, o = wview(W, 0, [8, 768])
            wuq, o = wview(W, o, [3, 2048])
            wkv, o = wview(W, o, [2, 2048])
            dmas = [(wat, kview(wattn_d[l])), (wuq, kview(wuqx_d[l])), (wkv, kview(wukv_d[l]))]
            wrb, o2 = wview(Wo, 0, [8, D])
            wgl, o2 = wview(Wo, o2, [8, D])

            def comp():
                fw.barrier()
                cv = Carver(SCR, MIX_BASE)
                CQN = cv.take([3, TS], BF16); CQNb = [[Buf(), Buf()] for _ in range(3)]
                Q = [cv.take([TT], BF16) for _ in range(2)]; Qb = [Buf(), Buf()]
                K = [cv.take([S], BF16) for _ in range(2)]; Kb = [Buf(), Buf()]
                VE = cv.take([16, 2, 128], BF16); VEb = Buf()
                PT = [cv.take([TT], BF16) for _ in range(3)]; PTb = [Buf() for _ in range(3)]
                PTD = [cv.take([TT], BF16) for _ in range(2)]; PTDb = [Buf(), Buf()]
                OT = cv.take([8, TT], BF16); OTb = [Buf() for _ in range(8)]
                TAB = cv.take([TS], F32); TABb = Buf()
                T1 = cv.take([TT], F32); T1b = Buf()
                T2 = cv.take([TT], F32); T2b = Buf()
                RC = cv.take([TT], F32); RCb = Buf()
                assert cv.off <= SCRW * 4, cv.off
                cdst = l if seg == 0 else L
                scale = float((64 + 32) ** -0.5)
                norm_x("mix_norm", l)
                fw.dma("sp", "ld_tab", TAB[64:128, :], rope_d[:, seg * TS:(seg + 1) * TS], writes=[TABb])
                fw.op("pool", lambda E: E.memset(VE[:, :, :, 64:128], 1.0), writes=[VEb])
                for k in range(2):
                    fw.op("pool", lambda E, k=k: E.memset(PTD[k][64:128, 0:64], 0.0), writes=[PTDb[k]])
                for tt in range(2):
                    nrd = [Nb[c][tt] for c in range(8)] + [Wb]
                    bks = [bank("a") for _ in range(3)]
                    for j in range(3):
                        mm(bks[j], PS[bks[j]][:, :], [(wat[:, c, j * 128:(j + 1) * 128], N[:, c, tsl(tt)]) for c in range(8)], nrd)
                    rmsnorm([PS[bks[j]][:, :] for j in range(3)], [PSb[bks[j]] for j in range(3)],
                            [prm_col(l, "mla_q_norm", j) for j in range(3)],
                            [CQN[:, j, tsl(tt)] for j in range(3)], [CQNb[j][tt] for j in range(3)], 384)
                    bks = [bank("a") for _ in range(2)]
                    for j in range(2):
                        mm(bks[j], PS[bks[j]][:, :], [(wat[:, c, 384 + j * 128:384 + (j + 1) * 128], N[:, c, tsl(tt)]) for c in range(8)], nrd)
                    rmsnorm([PS[bks[j]][:, :] for j in range(2)], [PSb[bks[j]] for j in range(2)],
                            [prm_col(l, "mla_kv_norm", j) for j in range(2)],
                            [CKV[cdst][:, j, tsl(tt)] for j in range(2)], [CKVb[cdst], CKVb[cdst]], 256)
                    bkr = bank("a")
                    mm(bkr, PS[bkr][:, :], [(wat[:, c, 640:768], N[:, c, tsl(tt)]) for c in range(8)], nrd)
                    fw.op("dve", lambda E, bkr=bkr, tt=tt: E.tensor_tensor(out=T1[64:96, :], in0=PS[bkr][64:96, :], in1=TAB[64:96, tsl(tt)], op=ALU.mult),
                          reads=[PSb[bkr], TABb], writes=[T1b])
                    fw.op("dve", lambda E, bkr=bkr, tt=tt: E.tensor_tensor(out=T2[64:96, :], in0=PS[bkr][96:128, :], in1=TAB[96:128, tsl(tt)], op=ALU.mult),
                          reads=[PSb[bkr], TABb], writes=[T2b])
                    fw.op("pool", lambda E, tt=tt: E.tensor_tensor(out=KR[cdst][64:96, tsl(tt)], in0=T1[64:96, :], in1=T2[64:96, :], op=ALU.add),
                          reads=[T1b, T2b], writes=[KRb[cdst]])
                tap("cqn", CQN, [b for bb in CQNb for b in bb])
                tap("ckv", CKV[cdst][:], [CKVb[cdst]])
                tap("kr", KR[cdst][64:96, :], [KRb[cdst]])
                ksrc = [l] if seg == 0 else [l, L]
                nhalf = len(ksrc)
                for kb_ in range(2):
                    for hf in range(nhalf):
                        fw.op("act", lambda E, kb_=kb_, hf=hf: E.activation(out=K[kb_][64:96, hf * TS:(hf + 1) * TS],
                                                                            in_=KR[ksrc[hf]][64:96, :], func=AF.Copy),
                              reads=[KRb[ksrc[hf]]], writes=[Kb[kb_]])

                def lat(kc, k0, n):
                    hf = k0 // TS
                    return CKV[ksrc[hf]][:, kc, k0 - hf * TS:k0 - hf * TS + n], CKVb[ksrc[hf]]

                hctr = [0]
                ptc = [0]
                ptdc = [0]
                for tt in range(2):
                    gq = seg * 2 + tt
                    kend = (gq + 1) * TT
                    nkb = kend // 128
                    for hp in range(8):
                        for kb0 in range(0, nkb, 4):
                            bv = bank("c")
                            rds = [Wb]
                            fw.wait("pe", fw.deps([CKVb[i] for i in ksrc] + [Wb], [PSb[bv]]))
                            tok = None
                            for q4 in range(4):
                                kb = kb0 + q4
                                for kc in range(2):
                                    la, lab = lat(kc, kb * 128, 128)
                                    tok = fw.raw("pe", lambda E, la=la, kc=kc, q4=q4, bv=bv, hp=hp: E.matmul(
                                        out=PS[bv][:, q4 * 128:(q4 + 1) * 128], lhsT=la,
                                        rhs=wkv[:, kc, :].rearrange("p (h e) -> p h e", h=16)[:, 2 * hp:2 * hp + 2, 64:128],
                                        start=(kc == 0), stop=(kc == 1)))
                            fw.mark(tok, [CKVb[i] for i in ksrc] + [Wb], [PSb[bv]])
                            fw.op("dve", lambda E, kb0=kb0, bv=bv: E.tensor_copy(
                                out=VE[:, kb0:kb0 + 4, :, 0:64],
                                in_=PS[bv][:, :].rearrange("p (a h e) -> p a h e", a=4, h=2)), reads=[PSb[bv]], writes=[VEb])
                        for hh in range(2):
                            h = hp * 2 + hh
                            kk = hctr[0] % 2
                            hctr[0] += 1
                            for kt in range(kend // TT):
                                bkk = bank("c")
                                pairs = []
                                for kc in range(2):
                                    la, lab = lat(kc, kt * TT, TT)
                                    pairs.append((wkv[:, kc, h * 128:h * 128 + 64], la))
                                mm(bkk, PS[bkk][0:64, :], pairs, [CKVb[i] for i in ksrc] + [Wb])
                                fw.op("act", lambda E, kk=kk, kt=kt, bkk=bkk: E.activation(out=K[kk][0:64, kt * TT:(kt + 1) * TT],
                                                                                       in_=PS[bkk][0:64, :], func=AF.Copy),
                                      reads=[PSb[bkk]], writes=[Kb[kk]])
                            bq = bank("c")
                            mm(bq, PS[bq][:, :], [(wuq[:, kc, h * 128:(h + 1) * 128], CQN[:, kc, tsl(tt)]) for kc in range(3)],
                               [CQNb[kc][tt] for kc in range(3)] + [Wb])
                            tq = fw.op("act", lambda E, kk=kk, bq=bq: E.activation(out=Q[kk][0:64, :], in_=PS[bq][0:64, :], func=AF.Copy),
                                       reads=[PSb[bq]], writes=[Qb[kk]])
                            fw.op("dve", lambda E, bq=bq, tt=tt: E.tensor_tensor(out=T1[64:96, :], in0=PS[bq][64:96, :], in1=TAB[64:96, tsl(tt)], op=ALU.mult),
                                  reads=[PSb[bq], TABb], writes=[T1b], extra=[tq])
                            fw.op("dve", lambda E, bq=bq, tt=tt: E.tensor_tensor(out=T2[64:96, :], in0=PS[bq][96:128, :], in1=TAB[96:128, tsl(tt)], op=ALU.mult),
                                  reads=[PSb[bq], TABb], writes=[T2b])
                            fw.op("pool", lambda E, kk=kk: E.tensor_tensor(out=Q[kk][64:96, :], in0=T1[64:96, :], in1=T2[64:96, :], op=ALU.add),
                                  reads=[T1b, T2b], writes=[Qb[kk]])
                            tap("q0", Q[kk][0:96, :], [Qb[kk]])
                            tap("k0", K[kk][0:96, 0:TT], [Kb[kk]])
                            tap("ve0", VE[:, 0:4, :, :], [VEb])
                            bo = bank("b")
                            pend = None

                            def issue_pv(item, last):
                                kb, q_lo, pt, ptb, first = item
                                fw.wait("pe", fw.deps([ptb, VEb], [PSb[bo]] if first else []))
                                tok = fw.raw("pe", lambda E, kb=kb, q_lo=q_lo, pt=pt, hh=hh, first=first, last=last, bo=bo: E.matmul(
                                    out=PS[bo][:, q_lo:TT], lhsT=VE[:, kb, hh, :], rhs=pt[:, 0:TT - q_lo], start=first, stop=last))
                                fw.mark(tok, [ptb, VEb], [PSb[bo]])

                            for kb in range(nkb):
                                q_lo = max(0, kb * 128 - gq * TT)
                                nq = TT - q_lo
                                diag = kb * 128 >= gq * TT
                                bs = bank("a")
                                mm(bs, PS[bs][:, 0:nq], [(K[kk][0:96, kb * 128:(kb + 1) * 128], Q[kk][0:96, q_lo:TT])], [Kb[kk], Qb[kk]])
                                if diag:
                                    i = ptdc[0] % 2
                                    ptdc[0] += 1
                                    pt, ptb = PTD[i], PTDb[i]
                                    fw.op("act", lambda E, pt=pt, bs=bs, nq=nq: E.activation(out=pt[0:64, 0:nq], in_=PS[bs][0:64, 0:nq], func=AF.Exp, scale=scale),
                                          reads=[PSb[bs]], writes=[ptb])
                                    if nq > 64:
                                        fw.op("act", lambda E, pt=pt, bs=bs, nq=nq: E.activation(out=pt[64:128, 64:nq], in_=PS[bs][64:128, 64:nq], func=AF.Exp, scale=scale),
                                              reads=[PSb[bs]], writes=[ptb])
                                else:
                                    i = ptc[0] % 3
                                    ptc[0] += 1
                                    pt, ptb = PT[i], PTb[i]
                                    fw.op("act", lambda E, pt=pt, bs=bs, nq=nq: E.activation(out=pt[:, 0:nq], in_=PS[bs][:, 0:nq], func=AF.Exp, scale=scale),
                                          reads=[PSb[bs]], writes=[ptb])
                                tap("pt0", pt[:, 0:TT], [ptb])
                                if pend is not None:
                                    issue_pv(pend, False)
                                pend = (kb, q_lo, pt, ptb, kb == 0)
                            issue_pv(pend, True)
                            fw.op("act", lambda E, bo=bo: E.activation(out=RC[64:128, :], in_=PS[bo][64:128, :], func=AF.Ln),
                                  reads=[PSb[bo]], writes=[RCb])
                            fw.op("act", lambda E: E.activation(out=RC[64:128, :], in_=RC[64:128, :], func=AF.Exp, scale=-1.0),
                                  reads=[RCb], writes=[RCb])
                            fw.op("dve", lambda E: E.tensor_copy(out=RC[0:64, :], in_=RC[64:128, :]), reads=[RCb], writes=[RCb])
                            fw.op("dve", lambda E, bo=bo, hh=hh, hp=hp: E.tensor_tensor(out=OT[hh * 64:(hh + 1) * 64, hp, :], in0=PS[bo][0:64, :],
                                                                                    in1=RC[0:64, :], op=ALU.mult),
                                  reads=[PSb[bo], RCb], writes=[OTb[hp]])
                    tap("ot0", OT[:, 0, :], [OTb[0]])
                    assert mix_first[0]
                    gate_merge(l, 1, tt, wrb, OT, OTb, wgl, [Wob], None, None, [T1, T2], [T1b, T2b])
                mix_first[0] = False
            return dmas, comp
        return f

    def st_B2(l):
        def f(W, Wb, Wo, Wob):
            wrb, o = wview(W, 0, [8, D])
            wgl, o = wview(W, o, [8, D])
            dmas = [(wrb, kview(wrb_d[l])), (wgl, kview(win_d[l, :, C_GL + D:C_GL + 2 * D]))]
            return dmas, (lambda: None)
        return f

    def st_A1(l, sq, seg):
        def f(W, Wb, Wo, Wob):
            wxa, o = wview(W, 0, [8, D])
            wga, o = wview(W, o, [8, D])
            lwa, o = wview(W, o, [8, 128])
            lwx, o = wview(W, o, [8, 128])
            dmas = [(wxa, kview(win_d[l, :, C_XA:C_XA + D])), (wga, kview(win_d[l, :, C_GA:C_GA + D]))]
            for j in range(2):
                dmas.append((lwa[64 * j:64 * j + 64, :, 64 * j:64 * j + 64],
                             lwa_d[l].rearrange("(n two) c d -> two c n d", two=2)[j]))
                dmas.append((lwx[64 * j:64 * j + 64, :, 64 * j:64 * j + 64],
                             lwx_d[l].rearrange("(n two) c d -> two c n d", two=2)[j]))
            wra, o2 = wview(Wo, 0, [8, D])
            wgl, o2 = wview(Wo, o2, [8, D])
            pre = [("z", lwa[0:64, :, 64:128]), ("z", lwa[64:128, :, 0:64]), ("z", lwx[0:64, :, 64:128]), ("z", lwx[64:128, :, 0:64])]

            def comp():
                fw.barrier()
                cv = Carver(SCR, MIX_BASE)
                XA = [cv.take([TT + 3], F32) for _ in range(2)]; XAb = [Buf(), Buf()]
                XC = [cv.take([TT], F32) for _ in range(2)]; XCb = [Buf(), Buf()]
                XCH = [cv.take([TT], BF16) for _ in range(2)]; XCHb = [Buf(), Buf()]
                R = [cv.take([TT], F32) for _ in range(2)]; Rb = [Buf(), Buf()]
                I = [cv.take([TT], F32) for _ in range(2)]; Ib = [Buf(), Buf()]
                T = [cv.take([TT], F32) for _ in range(2)]; Tb = [Buf(), Buf()]
                HH = [cv.take([TT], F32) for _ in range(2)]; HHb = [Buf(), Buf()]
                GG = [cv.take([TT], F32) for _ in range(2)]; GGb = [Buf(), Buf()]
                AT = cv.take([8, TT], BF16); ATb = [Buf() for _ in range(8)]
                GT = [cv.take([TT], F32) for _ in range(2)]; GTb = [Buf(), Buf()]
                TM = [cv.take([TT], F32) for _ in range(2)]; TMb = [Buf(), Buf()]
                assert cv.off <= SCRW * 4, cv.off
                if "B" not in phases:
                    norm_x("mix_norm", l)
                for tt in range(2):
                    nrd = [Nb[c][tt] for c in range(8)] + [Wb]
                    for c in range(8):
                        k = c % 2
                        bx = bank("a")
                        mm(bx, PS[bx][:, :], [(wxa[:, kc, c * 128:(c + 1) * 128], N[:, kc, tsl(tt)]) for kc in range(8)], nrd)
                        fw.op("dve", lambda E, k=k, c=c: E.tensor_copy(out=XA[k][:, 0:3], in_=HALO[:, l, c, :]),
                              reads=[HALOb[l][c]], writes=[XAb[k]])
                        fw.op("act", lambda E, k=k, bx=bx: E.activation(out=XA[k][:, 3:TT + 3], in_=PS[bx][:, :], func=AF.Copy),
                              reads=[PSb[bx]], writes=[XAb[k]])
                        fw.op("dve", lambda E, k=k, c=c: E.tensor_copy(out=HALO[:, l, c, :], in_=XA[k][:, TT:TT + 3]),
                              reads=[XAb[k]], writes=[HALOb[l][c]])
                        cw = lambda tap, c=c: prm_col(l, "conv_w", tap * 8 + c)
                        fw.op("dve", lambda E, k=k, c=c, cw=cw: E.tensor_scalar(out=XC[k], in0=XA[k][:, 0:TT], scalar1=cw(0),
                                                                             scalar2=prm_col(l, "conv_b", c), op0=ALU.mult, op1=ALU.add),
                              reads=[XAb[k], PRMb], writes=[XCb[k]])
                        for tap in range(1, 4):
                            fw.op("dve", lambda E, k=k, tap=tap, cw=cw: E.scalar_tensor_tensor(
                                out=XC[k], in0=XA[k][:, tap:tap + TT], scalar=cw(tap), in1=XC[k], op0=ALU.mult, op1=ALU.add),
                                reads=[XAb[k], XCb[k], PRMb], writes=[XCb[k]])
                        fw.op("act", lambda E, k=k: E.activation(out=XCH[k], in_=XC[k], func=AF.Copy), reads=[XCb[k]], writes=[XCHb[k]])
                        br_, bi_ = bank("c"), bank("c")
                        mm(br_, PS[br_][:, :], [(lwa[:, c, :], XCH[k])], [XCHb[k], Wb])
                        mm(bi_, PS[bi_][:, :], [(lwx[:, c, :], XCH[k])], [XCHb[k], Wb])
                        fw.op("act", lambda E, k=k, br_=br_, c=c: E.activation(out=R[k], in_=PS[br_][:, :], func=AF.Sigmoid, bias=prm_col(l, "lru_b_a", c)),
                              reads=[PSb[br_], PRMb], writes=[Rb[k]])
                        fw.op("act", lambda E, k=k, bi_=bi_, c=c: E.activation(out=I[k], in_=PS[bi_][:, :], func=AF.Sigmoid, bias=prm_col(l, "lru_b_x", c)),
                              reads=[PSb[bi_], PRMb], writes=[Ib[k]])
                        fw.op("act", lambda E, k=k, c=c: E.activation(out=T[k], in_=R[k], func=AF.Exp, scale=CLAM[:, l, 1, c:c + 1]),
                              reads=[Rb[k], CLAMb], writes=[Tb[k]])
                        fw.op("act", lambda E, k=k, c=c: E.activation(out=R[k], in_=R[k], func=AF.Exp, scale=CLAM[:, l, 0, c:c + 1]),
                              reads=[Rb[k], CLAMb], writes=[Rb[k]])
                        fw.op("act", lambda E, k=k: E.activation(out=T[k], in_=T[k], func=AF.Sqrt, scale=-1.0, bias=one_ap),
                              reads=[Tb[k], CSTb], writes=[Tb[k]])
                        if seg == 0 and tt == 0:
                            fw.op("dve", lambda E, k=k: E.memset(T[k][:, 0:1], 1.0), reads=[], writes=[Tb[k]])
                        fw.op("dve", lambda E, k=k: E.tensor_tensor(out=I[k], in0=I[k], in1=T[k], op=ALU.mult),
                              reads=[Ib[k], Tb[k]], writes=[Ib[k]])
                        fw.op("dve", lambda E, k=k: E.tensor_tensor(out=I[k], in0=I[k], in1=XC[k], op=ALU.mult),
                              reads=[Ib[k], XCb[k]], writes=[Ib[k]])
                        fw.op("dve", lambda E, k=k, c=c: E.tensor_tensor_scan(out=HH[k], data0=R[k], data1=I[k], initial=CAR[:, l, c:c + 1],
                                                                            op0=ALU.mult, op1=ALU.add),
                              reads=[Rb[k], Ib[k], CARb[l][c]], writes=[HHb[k]])
                        fw.op("dve", lambda E, k=k, c=c: E.tensor_copy(out=CAR[:, l, c:c + 1], in_=HH[k][:, TT - 1:TT]),
                              reads=[HHb[k]], writes=[CARb[l][c]])
                        bg = bank("a")
                        mm(bg, PS[bg][:, :], [(wga[:, kc, c * 128:(c + 1) * 128], N[:, kc, tsl(tt)]) for kc in range(8)], nrd)
                        fw.op("act", lambda E, k=k, bg=bg: E.activation(out=GG[k], in_=PS[bg][:, :], func=AF.Gelu_apprx_tanh),
                              reads=[PSb[bg]], writes=[GGb[k]])
                        fw.op("dve", lambda E, k=k, c=c: E.tensor_tensor(out=AT[:, c, :], in0=HH[k], in1=GG[k], op=ALU.mult),
                              reads=[HHb[k], GGb[k]], writes=[ATb[c]])
                    gate_merge(l, 0, tt, wra, AT, ATb, wgl, [Wob], TM, TMb, GT, GTb)
                mix_first[0] = False
            return dmas, comp, pre
        return f

    def st_A2(l):
        def f(W, Wb, Wo, Wob):
            wra, o = wview(W, 0, [8, D])
            wgl, o = wview(W, o, [8, D])
            dmas = [(wra, kview(wra_d[l])), (wgl, kview(win_d[l, :, C_GL:C_GL + D]))]
            return dmas, (lambda: None)
        return f

    def st_C1(l, sq, seg):
        def f(W, Wb, Wo, Wob):
            wzu, o = wview(W, 0, [8, D])
            wzv, o = wview(W, o, [8, D])
            wmT, o = wview(W, o, [8, 128])
            dmas = [(wzu, kview(win_d[l, :, C_ZU:C_ZU + D])), (wzv, kview(win_d[l, :, C_ZV:C_ZV + D])), (wmT, swT_d[l])]
            wrc, o2 = wview(Wo, 0, [8, D])
            wgl, o2 = wview(Wo, o2, [8, D])
            post = [("z", wmT[64:128, :, 0:64])]

            def comp():
                fw.barrier()
                cv = Carver(SCR, MIX_BASE)
                UT = cv.take([8, TT], BF16); UTb = [Buf() for _ in range(8)]
                VT = [cv.take([D], F32) for _ in range(2)]; VTb = [Buf(), Buf()]
                VH = [cv.take([D], BF16) for _ in range(2)]; VHb = [Buf(), Buf()]
                CT = cv.take([8, TT], BF16); CTb = [Buf() for _ in range(8)]
                TC1 = cv.take([8, 128], F32); TC = [TC1, TC1]; TCb1 = Buf(); TCb = [TCb1, TCb1]
                B2 = cv.take([8, 128], F32); B2b = Buf()
                STA = [cv.take([16], F32) for _ in range(2)]; STAb = [Buf(), Buf()]
                GT = [cv.take([TT], F32) for _ in range(2)]; GTb = [Buf(), Buf()]
                TMall = cv.take([2, TT], F32); TM = [TMall[:, 0, :], TMall[:, 1, :]]; TMb = [Buf(), Buf()]
                BSB = TMall.rearrange("p a (b c) -> p (a b) c", c=128)
                assert cv.off <= SCRW * 4, cv.off
                if "B" not in phases and "A" not in phases:
                    norm_x("mix_norm", l)
                fw.dma("sp", "ld_bsb", BSB, sbs_d[l].rearrange("(g t) -> g t", g=8).partition_broadcast(128), writes=TMb)
                for hh in range(2):
                    bb = bank("c")
                    mm(bb, PS[bb][:, :], [(ONES[:], wmT[:, hh * 4:hh * 4 + 4, :])], [ONESb, Wb])
                    for g4 in range(4):
                        g = hh * 4 + g4
                        fw.op("dve", lambda E, g=g, g4=g4, bb=bb: E.scalar_tensor_tensor(
                            out=B2[:, g, :], in0=PS[bb][:, g4 * 128:(g4 + 1) * 128], scalar=prm_col(l, "sgu_norm_b", g),
                            in1=BSB[:, g, :], op0=ALU.mult, op1=ALU.add), reads=[PSb[bb], PRMb] + TMb, writes=[B2b])
                for tt in range(2):
                    nrd = [Nb[c][tt] for c in range(8)] + [Wb]
                    for g in range(8):
                        bu = bank("a")
                        mm(bu, PS[bu][:, :], [(wzu[:, kc, g * 128:(g + 1) * 128], N[:, kc, tsl(tt)]) for kc in range(8)], nrd)
                        fw.op("act", lambda E, g=g, bu=bu: E.activation(out=UT[:, g, :], in_=PS[bu][:, :], func=AF.Gelu_apprx_tanh),
                              reads=[PSb[bu]], writes=[UTb[g]])
                    for blk in range(4):
                        k = blk % 2
                        t0 = tt * TT + blk * 128
                        for hv in range(2):
                            bv = bank("a")
                            mm(bv, PS[bv][:, :], [(N[:, kc, t0:t0 + 128], wzv[:, kc, hv * 512:(hv + 1) * 512]) for kc in range(8)], nrd)
                            fw.op("act", lambda E, k=k, hv=hv, bv=bv: E.activation(out=VT[k][:, hv * 512:(hv + 1) * 512], in_=PS[bv][:, :],
                                                                                  func=AF.Gelu_apprx_tanh), reads=[PSb[bv]], writes=[VTb[k]])
                        for hv in range(2):
                            fw.op("dve", lambda E, k=k, hv=hv: E.bn_stats(out=STA[k][:, hv * 6:(hv + 1) * 6], in_=VT[k][:, hv * 512:(hv + 1) * 512]),
                                  reads=[VTb[k]], writes=[STAb[k]])
                        fw.op("dve", lambda E, k=k: E.bn_aggr(out=STA[k][:, 12:14], in_=STA[k][:, 0:12]), reads=[STAb[k]], writes=[STAb[k]])
                        fw.op("act", lambda E, k=k: E.activation(out=STA[k][:, 14:15], in_=STA[k][:, 13:14], func=AF.Sqrt, bias=eps_ap),
                              reads=[STAb[k], CSTb], writes=[STAb[k]])
                        fw.op("dve", lambda E, k=k: E.reciprocal(out=STA[k][:, 15:16], in_=STA[k][:, 14:15]), reads=[STAb[k]], writes=[STAb[k]])
                        fw.op("dve", lambda E, k=k: E.tensor_scalar(out=VH[k], in0=VT[k], scalar1=STA[k][:, 12:13], scalar2=STA[k][:, 15:16],
                                                                   op0=ALU.subtract, op1=ALU.mult), reads=[VTb[k], STAb[k]], writes=[VHb[k]])
                        for hg in range(2):
                            bp = bank("c")
                            fw.wait("pe", fw.deps([VHb[k], Wb], [PSb[bp]]))
                            tok = None
                            for g4 in range(4):
                                g = hg * 4 + g4
                                tok = fw.raw("pe", lambda E, k=k, g=g, g4=g4, bp=bp: E.matmul(
                                    out=PS[bp][:, g4 * 128:(g4 + 1) * 128], lhsT=VH[k][:, g * 128:(g + 1) * 128], rhs=wmT[:, g, :],
                                    start=True, stop=True))
                            fw.mark(tok, [VHb[k], Wb], [PSb[bp]])
                            for g4 in range(4):
                                g = hg * 4 + g4
                                fw.op("dve", lambda E, k=k, g=g, g4=g4, bp=bp: E.scalar_tensor_tensor(
                                    out=TC[k][:, g, :], in0=PS[bp][:, g4 * 128:(g4 + 1) * 128], scalar=prm_col(l, "sgu_norm_g", g),
                                    in1=B2[:, g, :], op0=ALU.mult, op1=ALU.add), reads=[PSb[bp], B2b, PRMb], writes=[TCb[k]])
                        fw.op("pool", lambda E, k=k, blk=blk: E.tensor_tensor(out=CT[:, :, blk * 128:(blk + 1) * 128], in0=TC[k][:],
                                                                             in1=UT[:, :, blk * 128:(blk + 1) * 128], op=ALU.mult),
                              reads=[TCb[k]] + UTb, writes=CTb)
                    gate_merge(l, 2, tt, wrc, CT, CTb, wgl, [Wob], TM, TMb, GT, GTb)
                mix_first[0] = False
            return dmas, comp, [], post
        return f

    def st_C2(l):
        def f(W, Wb, Wo, Wob):
            wrc, o = wview(W, 0, [8, D])
            wgl, o = wview(W, o, [8, D])
            dmas = [(wrc, kview(wrc_d[l])), (wgl, kview(win_d[l, :, C_GL + 2 * D:C_GL + 3 * D]))]
            return dmas, (lambda: None)
        return f

    def st_wout(l):
        def f(W, Wb, Wo, Wob):
            wo, o = wview(W, 0, [8, D])
            dmas = [(wo, kview(wout_d[l]))]

            def comp():
                for tt in range(2):
                    rd = [MGb[c][tt] for c in range(8)] + [Wb]
                    for d in range(8):
                        bd = bank("b")
                        mm(bd, PS[bd][:, :], [(wo[:, c, d * 128:(d + 1) * 128], MG[:, c, tsl(tt)]) for c in range(8)], rd)
                        fw.op("dve", lambda E, d=d, tt=tt, bd=bd: E.tensor_tensor(out=X[:, d, tsl(tt)], in0=X[:, d, tsl(tt)], in1=PS[bd][:, :], op=ALU.add),
                              reads=[PSb[bd], Xb[d][tt]], writes=[Xb[d][tt]])
                mix_first[0] = True
            return dmas, comp
        return f

    def st_ple(l, sq, seg):
        def f(W, Wb, Wo, Wob):
            wpg, o = wview(W, 0, [8, D])
            wpp, o = wview(W, o, [2, D])
            dmas = [(wpg, kview(wpg_d[l])), (wpp, kview(wpp_d[l]))]

            def comp():
                fw.barrier()
                cv = Carver(SCR)
                PTK = [cv.take([PLE], F32) for _ in range(2)]; PTKb = [Buf(), Buf()]
                PTT = cv.take([2, TS], BF16); PTTb = [Buf(), Buf()]
                GT = [cv.take([TT], F32) for _ in range(2)]; GTb = [Buf(), Buf()]
                TM = [cv.take([TT], F32) for _ in range(2)]; TMb = [Buf(), Buf()]
                norm_x("ple_norm", l)
                for tb in range(8):
                    k = tb % 2
                    r0 = sq * S + seg * TS + tb * 128
                    fw.dma("sp", "pl%d" % k, PTK[k], p_d[l, r0:r0 + 128, :], writes=[PTKb[k]])
                    bk = bank("c")
                    fw.wait("pe", fw.deps([PTKb[k], IDENTb], [PSb[bk]]))
                    tok = None
                    for j in range(2):
                        tok = fw.raw("pe", lambda E, j=j, k=k, bk=bk: E.transpose(out=PS[bk][:, j * 128:(j + 1) * 128],
                                                                              in_=PTK[k][:, j * 128:(j + 1) * 128], identity=IDENT[:]))
                    fw.mark(tok, [PTKb[k], IDENTb], [PSb[bk]])
                    fw.op("act", lambda E, tb=tb, bk=bk: E.activation(out=PTT[:, :, tb * 128:(tb + 1) * 128],
                                                                  in_=PS[bk][:, 0:256].rearrange("p (a b) -> p a b", a=2), func=AF.Copy),
                          reads=[PSb[bk]], writes=[PTTb[tb // 4]])
                for tt in range(2):
                    nrd = [Nb[c][tt] for c in range(8)] + [Wb]
                    for d in range(8):
                        bg, bp = bank("a"), bank("a")
                        mm(bg, PS[bg][:, :], [(wpg[:, c, d * 128:(d + 1) * 128], N[:, c, tsl(tt)]) for c in range(8)], nrd)
                        mm(bp, PS[bp][:, :], [(wpp[:, j, d * 128:(d + 1) * 128], PTT[:, j, tsl(tt)]) for j in range(2)], [PTTb[tt], Wb])
                        k = d % 2
                        fw.op("act", lambda E, k=k, bg=bg: E.activation(out=GT[k], in_=PS[bg][:, :], func=AF.Sigmoid),
                              reads=[PSb[bg]], writes=[GTb[k]])
                        fw.op("dve", lambda E, k=k, bp=bp: E.tensor_tensor(out=TM[k], in0=GT[k], in1=PS[bp][:, :], op=ALU.mult),
                              reads=[GTb[k], PSb[bp]], writes=[TMb[k]])
                        fw.op("pool", lambda E, k=k, d=d, tt=tt: E.tensor_tensor(out=X[:, d, tsl(tt)], in0=X[:, d, tsl(tt)], in1=TM[k], op=ALU.add),
                              reads=[TMb[k], Xb[d][tt]], writes=[Xb[d][tt]])
            return dmas, comp
        return f

    for sq in range(n_seq):
        for seg in range(n_seg):
            stages.append(st_load_x(sq, seg))
            for l in range(n_layers):
                if "ffn1" in phases:
                    for gi in range(len(FFN_GROUPS)):
                        stages.append(st_ffn(l, 0, gi))
                if "B" in phases:
                    stages.append(st_B1(l, sq, seg))
                    stages.append(st_B2(l))
                if "A" in phases:
                    stages.append(st_A1(l, sq, seg))
                    stages.append(st_A2(l))
                if "C" in phases:
                    stages.append(st_C1(l, sq, seg))
                    stages.append(st_C2(l))
                if "wout" in phases:
                    stages.append(st_wout(l))
                if "ffn2" in phases:
                    for gi in range(len(FFN_GROUPS)):
                        stages.append(st_ffn(l, 1, gi))
                if "ple" in phases:
                    stages.append(st_ple(l, sq, seg))
            stages.append(st_store(sq, seg))

    half = 0
    pending = None
    for sfn in stages:
        res = sfn(WA[half], WAb[half], WA[1 - half], WAb[1 - half])
        dmas, comp = res[0], res[1]
        pre = res[2] if len(res) > 2 else []
        post = res[3] if len(res) > 3 else []
        if dmas:
            hb = WAb[half]
            fw.wait("pool", fw.deps([], [hb]))
            tok = None
            for kind, ap in pre:
                tok = fw.raw("pool", lambda E, ap=ap: E.memset(ap, 0.0))
            if tok is not None:
                fw.wait("pool", {tok[0]: tok[1]})
            for dst, src in dmas:
                tok = fw.dma_raw("pool", "wl%d" % half, dst, src)
            hb.lw = tok
            hb.rd = {}
            for kind, ap in post:
                fw.op("pool", lambda E, ap=ap: E.memset(ap, 0.0), reads=[], writes=[hb])
            half = 1 - half
        if pending is not None:
            pending()
        pending = comp
    pending()
    fw.wait("sp", fw.deps([], [], extra=out_toks))
    fw.emit()
    st.close()
    return nc


_CACHE = {}


def _host_weights(inp):
    f = lambda k: np.ascontiguousarray(np.asarray(inp[k], dtype=np.float32))
    w_in = f("w_in")
    w = {}
    for k in ("ffn1_w_gu", "ffn2_w_gu", "ffn1_w_down", "ffn2_w_down", "mla_w_ukv", "w_read_a", "w_read_b", "w_read_c",
              "w_out", "ple_w_gate", "ple_w_proj", "lru_w_a", "lru_w_x"):
        w[k] = f(k)
    w["w_in"] = w_in
    kr = w_in[:, :, C_KR:C_KR + 32]
    kr_sw = np.concatenate([kr[:, :, 16:32], kr[:, :, 0:16]], axis=-1)
    w["w_attn"] = np.ascontiguousarray(np.concatenate(
        [w_in[:, :, C_CQ:C_CQ + 384], w_in[:, :, C_CKV:C_CKV + 256], w_in[:, :, C_CKV:C_CKV + 64], kr, kr_sw], axis=-1))
    uq = f("mla_w_uq").reshape(L, 384, NH, 96)
    uqx = np.concatenate([uq, uq[..., 80:96], uq[..., 64:80]], axis=-1)
    w["w_uqx"] = np.ascontiguousarray(uqx.reshape(L, 384, NH * 128))
    w["sgu_wT"] = np.ascontiguousarray(f("sgu_w_s").transpose(0, 3, 1, 2))
    w["sgu_b_s"] = np.ascontiguousarray(f("sgu_b_s").reshape(L, 1024))
    prm = np.zeros((PRM_ROWS, 128), np.float32)
    for l in range(L):
        for name, k in PRM_LAYOUT:
            r0 = l * PRM_PER_LAYER + PRM_OFF[name]
            prm[r0:r0 + k] = f(name)[l].reshape(k, 128)
    prm[PRM_FINAL:PRM_FINAL + 8] = f("final_norm").reshape(8, 128)
    w["prm"] = prm
    half = 16
    inv_freq = (10000.0 ** (-np.arange(half, dtype=np.float32) / half)).astype(np.float32)
    ang = np.arange(S, dtype=np.float32)[None, :] * inv_freq[:, None]
    cos, sin = np.cos(ang).astype(np.float32), np.sin(ang).astype(np.float32)
    w["rope"] = np.ascontiguousarray(np.concatenate([cos, cos, -sin, sin], axis=0))
    w["ident"] = np.eye(128, dtype=np.float32)
    return w


def kernel(**inputs):
    x = np.asarray(inputs["x"], dtype=np.float32)
    p = np.asarray(inputs["p"], dtype=np.float32)
    w = _host_weights(inputs)
    if "nc" not in _CACHE:
        _CACHE["nc"] = build()
    nc = _CACHE["nc"]
    in_maps = []
    for c in range(8):
        m = dict(w)
        m["x"] = np.ascontiguousarray(x[2 * c:2 * c + 2].reshape(2 * S, D))
        m["p"] = np.ascontiguousarray(p[:, 2 * c:2 * c + 2].reshape(L, 2 * S, PLE))
        in_maps.append(m)
    res = run_bass_kernel_spmd(nc, in_maps, core_ids=list(range(8)))
    out = np.stack([res.results[c]["out"].reshape(2, S, D) for c in range(8)], axis=0).reshape(16, S, D)
    return np.ascontiguousarray(out.astype(np.float32))
```

```python
from contextlib import ExitStack
import numpy as np
import concourse.bass as bass
import concourse.mybir as mybir
from concourse.bass_utils import run_bass_kernel_spmd

F32 = mybir.dt.float32
BF16 = mybir.dt.bfloat16
ALU = mybir.AluOpType
AF = mybir.ActivationFunctionType

D = 1024
S = 2048
L = 2
DFF = 2816
PLE = 256
TS = 1024
TT = 512
EPS = 1e-6
NH = 16
ENGS = ("pe", "act", "dve", "pool", "sp")
EPOCH = 16000
C_XA, C_GA, C_CQ, C_CKV, C_KR, C_ZU, C_ZV, C_GL = 0, 1024, 2048, 2432, 2688, 2720, 3744, 4768
PRM_LAYOUT = [("ffn1_norm", 8), ("mix_norm", 8), ("ffn2_norm", 8), ("ple_norm", 8), ("conv_w", 32),
              ("conv_b", 8), ("lru_b_a", 8), ("lru_b_x", 8), ("lru_lambda", 8), ("mla_q_norm", 3),
              ("mla_kv_norm", 2), ("sgu_norm_g", 8), ("sgu_norm_b", 8), ("gate_bias", 24)]
PRM_OFF = {}
_o = 0
for _n, _k in PRM_LAYOUT:
    PRM_OFF[_n] = _o
    _o += _k
PRM_PER_LAYER = _o
PRM_FINAL = L * PRM_PER_LAYER
PRM_ROWS = 384
FFN_GROUPS = [(0, 6), (6, 6), (12, 5), (17, 5)]


class Buf:
    __slots__ = ("lw", "rd")

    def __init__(self):
        self.lw = None
        self.rd = {}


class Fw:
    def __init__(self, nc, stack):
        self.nc = nc
        self.stack = stack
        self.q = {e: [] for e in ENGS}
        self.sems = {}
        self.cnt = {}
        self.seen = {e: {} for e in ENGS}
        self.cur = {}
        self.last = {}
        self.pending_sp = {}
        for e in ("pe", "act", "dve", "pool"):
            self._new_epoch(e)

    def new_sem(self, key):
        self.sems[key] = self.stack.enter_context(self.nc.semaphore(key.replace("#", "_")))
        self.cnt[key] = 0

    def _new_epoch(self, e):
        k = "%s#%d" % (e, sum(1 for x in self.sems if x.startswith(e + "#")))
        self.new_sem(k)
        self.cur[e] = k

    @staticmethod
    def _add(deps, t):
        if t is not None and deps.get(t[0], 0) < t[1]:
            deps[t[0]] = t[1]

    def deps(self, reads, writes, extra=()):
        deps = {}
        for b in reads:
            self._add(deps, b.lw)
        for b in writes:
            self._add(deps, b.lw)
            for k, v in b.rd.items():
                if deps.get(k, 0) < v:
                    deps[k] = v
        for t in extra:
            self._add(deps, t)
        return deps

    def wait(self, eng, deps):
        seen = self.seen[eng]
        for k, v in deps.items():
            if eng == "pe" and k.startswith("pe#"):
                continue
            if seen.get(k, 0) >= v:
                continue
            seen[k] = v
            self.q[eng].append(("w", k, v))

    def mark(self, tok, reads, writes):
        k, v = tok
        for b in reads:
            if b.rd.get(k, 0) < v:
                b.rd[k] = v
        for b in writes:
            b.lw = tok
            b.rd = {}

    def raw(self, eng, fn):
        if self.cnt[self.cur[eng]] >= EPOCH:
            self._new_epoch(eng)
        key = self.cur[eng]
        self.cnt[key] += 1
        self.q[eng].append(("o", fn, key, 1))
        self.last[eng] = (key, self.cnt[key])
        return (key, self.cnt[key])

    def op(self, eng, fn, reads=(), writes=(), extra=()):
        self.wait(eng, self.deps(reads, writes, extra))
        tok = self.raw(eng, fn)
        self.mark(tok, reads, writes)
        return tok

    def dma_raw(self, qeng, chan, out, in_):
        if chan not in self.sems:
            self.new_sem(chan)
        self.cnt[chan] += 16
        self.q[qeng].append(("o", lambda E: E.dma_start(out=out, in_=in_), chan, 16))
        return (chan, self.cnt[chan])

    def dma(self, qeng, chan, out, in_, reads=(), writes=(), extra=()):
        self.wait(qeng, self.deps(reads, writes, extra))
        tok = self.dma_raw(qeng, chan, out, in_)
        self.mark(tok, reads, writes)
        if qeng == "sp":
            self.pending_sp[tok[0]] = tok[1]
        return tok

    def barrier(self):
        toks = {}
        for e in ("pe", "act", "dve", "pool"):
            t = self.last.get(e)
            if t is not None:
                toks[t[0]] = t[1]
        toks.update(self.pending_sp)
        self.pending_sp = {}
        for e in ("pe", "act", "dve", "pool", "sp"):
            self.wait(e, toks)

    def emit(self):
        nc = self.nc
        with nc.Block() as block:
            def run(eng):
                def body(E):
                    sems = self.sems
                    for it in self.q[eng]:
                        if it[0] == "w":
                            E.wait_ge(sems[it[1]], it[2])
                        else:
                            it[1](E).then_inc(sems[it[2]], it[3])
                return body
            block.tensor(run("pe"))
            block.scalar(run("act"))
            block.vector(run("dve"))
            block.gpsimd(run("pool"))
            block.sync(run("sp"))


class Carver:
    def __init__(self, scr, base=0):
        self.scr = scr
        self.off = base

    def take(self, shape, dtype):
        n = 1
        for s in shape:
            n *= s
        nbytes = n * (4 if dtype == F32 else 2)
        nbytes = (nbytes + 31) // 32 * 32
        o = self.off
        self.off += nbytes
        v = self.scr[:, o // 4:(o + nbytes) // 4]
        if dtype != F32:
            v = v.bitcast(dtype)
        v = v[:, 0:n]
        if len(shape) == 2:
            v = v.rearrange("p (a b) -> p a b", a=shape[0])
        elif len(shape) == 3:
            v = v.rearrange("p (a b c) -> p a b c", a=shape[0], b=shape[1])
        return v


def build(n_seq=2, n_layers=2, phases=("ffn1", "B", "A", "C", "wout", "ffn2", "ple"), n_seg=2, dbg=()):
    nc = bass.Bass("TRN2", target_bir_lowering=False, dynamic_dma_scratch_size=8192)
    dbg_done = set()
    NT = n_seq * S
    dr = lambda n, s: nc.dram_tensor(n, s, F32, kind="ExternalInput").ap()
    x_d = dr("x", [NT, D])
    p_d = dr("p", [L, NT, PLE])
    out_d = nc.dram_tensor("out", [NT, D], F32, kind="ExternalOutput").ap()
    wgu_d = [dr("ffn1_w_gu", [L, D, 2 * DFF]), dr("ffn2_w_gu", [L, D, 2 * DFF])]
    wdn_d = [dr("ffn1_w_down", [L, DFF, D]), dr("ffn2_w_down", [L, DFF, D])]
    win_d = dr("w_in", [L, D, 7840])
    wattn_d = dr("w_attn", [L, D, 768])
    wuqx_d = dr("w_uqx", [L, 384, 2048])
    wukv_d = dr("mla_w_ukv", [L, 256, 2048])
    wra_d = dr("w_read_a", [L, D, D])
    wrb_d = dr("w_read_b", [L, D, D])
    wrc_d = dr("w_read_c", [L, D, D])
    wout_d = dr("w_out", [L, D, D])
    wpg_d = dr("ple_w_gate", [L, D, D])
    wpp_d = dr("ple_w_proj", [L, PLE, D])
    lwa_d = dr("lru_w_a", [L, 16, 64, 64])
    lwx_d = dr("lru_w_x", [L, 16, 64, 64])
    swT_d = dr("sgu_wT", [L, 128, 8, 128])
    sbs_d = dr("sgu_b_s", [L, 1024])
    prm_d = dr("prm", [PRM_ROWS, 128])
    rope_d = dr("rope", [64, S])
    ident_d = dr("ident", [128, 128])

    st = ExitStack()
    fw = Fw(nc, st)
    sb = lambda n, s, d: st.enter_context(nc.sbuf_tensor(n, s, d))
    WAW = 18432
    WA = [sb("wa%d" % i, [128, WAW], BF16) for i in range(2)]
    WAb = [Buf(), Buf()]
    X = sb("X", [128, 8, TS], F32)
    Xb = [[Buf() for _ in range(2)] for _ in range(8)]
    N = sb("N", [128, 8, TS], BF16)
    Nb = [[Buf() for _ in range(2)] for _ in range(8)]
    CKV = [sb("ckv%d" % l, [128, 2, TS], BF16) for l in range(L)] + [sb("ckvc", [128, 2, TS], BF16)]
    CKVb = [Buf() for _ in range(L + 1)]
    KR = [sb("kr%d" % l, [128, TS], BF16) for l in range(L)] + [sb("krc", [128, TS], BF16)]
    KRb = [Buf() for _ in range(L + 1)]
    PRM = sb("prm_sb", [128, PRM_ROWS], F32); PRMb = Buf()
    IDENT = sb("ident_sb", [128, 128], F32); IDENTb = Buf()
    ONES = sb("ones", [128, 128], BF16); ONESb = Buf()
    CST = sb("cst", [128, 4], F32); CSTb = Buf()
    CLAM = sb("clam", [128, L, 2, 8], F32); CLAMb = Buf()
    HALO = sb("halo", [128, L, 8, 3], F32); HALOb = [[Buf() for _ in range(8)] for _ in range(L)]
    CAR = sb("car", [128, L, 8], F32); CARb = [[Buf() for _ in range(8)] for _ in range(L)]
    SQ = [sb("sq%d" % i, [128, TT], BF16) for i in range(2)]; SQb = [Buf(), Buf()]
    STD = sb("std", [128, TT], F32); STDb = Buf()
    RSTD = [sb("rstd%d" % i, [128, TT], F32) for i in range(2)]; RSTDb = [Buf(), Buf()]
    SCRW = 17152
    SCR = sb("scr", [128, SCRW], F32)
    PSP = [st.enter_context(nc.psum_tensor("psp%d" % i, [128, 2 * TT], F32)) for i in range(2)]
    PS = [PSP[i // 2][:, (i % 2) * TT:(i % 2 + 1) * TT] for i in range(4)]
    PS += [st.enter_context(nc.psum_tensor("ps%d" % i, [128, TT], F32))[:, :] for i in range(4, 8)]
    PSb = [Buf() for _ in range(8)]
    rot = {"a": [0, 1, 2, 3], "b": [4, 5], "c": [6, 7]}
    rotp = {"a": 0, "b": 0, "c": 0}

    def bank(role):
        lst = rot[role]
        i = lst[rotp[role] % len(lst)]
        rotp[role] += 1
        return i

    def mm(bk, out_ap, pairs, reads):
        fw.wait("pe", fw.deps(reads, [PSb[bk]]))
        n = len(pairs)
        tok = None
        for i, (lh, rh) in enumerate(pairs):
            tok = fw.raw("pe", lambda E, lh=lh, rh=rh, i=i: E.matmul(out=out_ap, lhsT=lh, rhs=rh,
                                                                    start=(i == 0), stop=(i == n - 1)))
        fw.mark(tok, reads, [PSb[bk]])
        return tok

    def prm_col(l, name, c):
        j = l * PRM_PER_LAYER + PRM_OFF[name] + c
        return PRM[:, j:j + 1]

    def tsl(tt):
        return slice(tt * TT, (tt + 1) * TT)

    eps_ap = CST[:, 0:1]
    one_ap = CST[:, 1:2]

    fw.dma("sp", "ld_id", IDENT[:], ident_d, writes=[IDENTb])
    fw.op("dve", lambda E: E.memset(ONES[:], 1.0), writes=[ONESb])
    fw.op("dve", lambda E: E.memset(CST[:, 0:1], EPS), writes=[CSTb])
    fw.op("dve", lambda E: E.memset(CST[:, 1:2], 1.0), writes=[CSTb])
    PTMP = SCR[:, 0:PRM_ROWS].rearrange("p (a b) -> p a b", a=3)
    PTMPb = Buf()
    fw.dma("sp", "ld_prm", PTMP, prm_d.rearrange("(a p) f -> p a f", p=128), writes=[PTMPb])
    bk = bank("a")
    for a in range(3):
        fw.op("pe", lambda E, a=a: E.transpose(out=PS[bk][:, a * 128:(a + 1) * 128], in_=PTMP[:, a, :],
                                               identity=IDENT[:]), reads=[PTMPb, IDENTb], writes=[PSb[bk]])
    fw.op("dve", lambda E: E.tensor_copy(out=PRM[:], in_=PS[bk][:, 0:PRM_ROWS]), reads=[PSb[bk]], writes=[PRMb])
    for l in range(n_layers):
        j = l * PRM_PER_LAYER + PRM_OFF["lru_lambda"]
        fw.op("act", lambda E, l=l, j=j: E.activation(out=CLAM[:, l, 0, :], in_=PRM[:, j:j + 8], func=AF.Exp, scale=-1.0),
              reads=[PRMb], writes=[CLAMb])
        fw.op("act", lambda E, l=l: E.activation(out=CLAM[:, l, 0, :], in_=CLAM[:, l, 0, :], func=AF.Ln, bias=one_ap),
              reads=[CLAMb, CSTb], writes=[CLAMb])
        fw.op("dve", lambda E, l=l: E.tensor_scalar(out=CLAM[:, l, 1, :], in0=CLAM[:, l, 0, :], scalar1=-16.0, scalar2=None,
                                                    op0=ALU.mult), reads=[CLAMb], writes=[CLAMb])
        fw.op("dve", lambda E, l=l: E.tensor_scalar(out=CLAM[:, l, 0, :], in0=CLAM[:, l, 0, :], scalar1=-8.0, scalar2=None,
                                                    op0=ALU.mult), reads=[CLAMb], writes=[CLAMb])

    NEGB = sb("negb", [128, L, 2, 8], F32); NEGBb = Buf()
    for l in range(n_layers):
        for jj, nm in enumerate(("lru_b_a", "lru_b_x")):
            j = l * PRM_PER_LAYER + PRM_OFF[nm]
            fw.op("dve", lambda E, l=l, jj=jj, j=j: E.tensor_scalar(out=NEGB[:, l, jj, :], in0=PRM[:, j:j + 8], scalar1=-1.0, scalar2=None,
                                                                    op0=ALU.mult), reads=[PRMb], writes=[NEGBb])

    rk = [0]

    def rmsnorm(src_aps, src_bufs, gain_aps, out_aps, out_bufs, dn, n=TT):
        nch = len(src_aps)
        bk = bank("c")
        fw.wait("pe", fw.deps([], [PSb[bk]]))
        tok = None
        for c in range(nch):
            k = c % 2
            fw.op("act", lambda E, c=c, k=k: E.activation(out=SQ[k][:, 0:n], in_=src_aps[c], func=AF.Square),
                  reads=[src_bufs[c]], writes=[SQb[k]])
            fw.wait("pe", fw.deps([SQb[k], ONESb], []))
            tok = fw.raw("pe", lambda E, c=c, k=k: E.matmul(out=PS[bk][:, 0:n], lhsT=ONES[:], rhs=SQ[k][:, 0:n],
                                                           start=(c == 0), stop=(c == nch - 1)))
            fw.mark(tok, [SQb[k], ONESb], [])
        fw.mark(tok, [], [PSb[bk]])
        fw.op("act", lambda E: E.activation(out=STD[:, 0:n], in_=PS[bk][:, 0:n], func=AF.Sqrt, scale=1.0 / dn, bias=eps_ap),
              reads=[PSb[bk], CSTb], writes=[STDb])
        r = rk[0] % 2
        rk[0] += 1
        fw.op("dve", lambda E: E.reciprocal(out=RSTD[r][:, 0:n], in_=STD[:, 0:n]), reads=[STDb], writes=[RSTDb[r]])
        for c in range(nch):
            fw.op("dve", lambda E, c=c: E.scalar_tensor_tensor(out=out_aps[c], in0=src_aps[c], scalar=gain_aps[c],
                                                               in1=RSTD[r][:, 0:n], op0=ALU.mult, op1=ALU.mult),
                  reads=[src_bufs[c], RSTDb[r], PRMb], writes=[out_bufs[c]])

    def norm_x(gname, l):
        for tt in range(2):
            rmsnorm([X[:, c, tsl(tt)] for c in range(8)], [Xb[c][tt] for c in range(8)],
                    [prm_col(l, gname, c) for c in range(8)],
                    [N[:, c, tsl(tt)] for c in range(8)], [Nb[c][tt] for c in range(8)], D)

    def wview(W, off, shape):
        n = 1
        for s_ in shape:
            n *= s_
        v = W[:, off:off + n]
        if len(shape) == 2:
            v = v.rearrange("p (a b) -> p a b", a=shape[0])
        elif len(shape) == 3:
            v = v.rearrange("p (a b c) -> p a b c", a=shape[0], b=shape[1])
        return v, off + n

    def kview(src2d):
        return src2d.rearrange("(c p) f -> p c f", p=128)

    stages = []

    def st_load_x(sq, seg):
        def f(W, Wb, Wo, Wob):
            def comp():
                fw.barrier()
                cv = Carver(SCR)
                XT = [cv.take([D], F32) for _ in range(2)]
                XTb = [Buf(), Buf()]
                for l in range(n_layers):
                    if seg == 0:
                        fw.op("dve", lambda E, l=l: E.memset(HALO[:, l, :, :], 0.0), writes=HALOb[l])
                        fw.op("dve", lambda E, l=l: E.memset(CAR[:, l, :], 0.0), writes=CARb[l])
                for tb in range(8):
                    k = tb % 2
                    r0 = sq * S + seg * TS + tb * 128
                    fw.dma("sp", "xl%d" % k, XT[k], x_d[r0:r0 + 128, :], writes=[XTb[k]])
                    for hh in range(2):
                        bk = bank("a")
                        fw.wait("pe", fw.deps([XTb[k], IDENTb], [PSb[bk]]))
                        tok = None
                        for c4 in range(4):
                            c = hh * 4 + c4
                            tok = fw.raw("pe", lambda E, c=c, c4=c4, k=k, bk=bk: E.transpose(
                                out=PS[bk][:, c4 * 128:(c4 + 1) * 128], in_=XT[k][:, c * 128:(c + 1) * 128], identity=IDENT[:]))
                        fw.mark(tok, [XTb[k], IDENTb], [PSb[bk]])
                        tt = tb // 4
                        dst = X[:, hh * 4:hh * 4 + 4, tb * 128:(tb + 1) * 128]
                        src = PS[bk][:, :].rearrange("p (a b) -> p a b", a=4)
                        wr = [Xb[c][tt] for c in range(hh * 4, hh * 4 + 4)]
                        if hh == 0:
                            fw.op("act", lambda E, dst=dst, src=src: E.activation(out=dst, in_=src, func=AF.Copy),
                                  reads=[PSb[bk]], writes=wr)
                        else:
                            fw.op("dve", lambda E, dst=dst, src=src: E.tensor_copy(out=dst, in_=src),
                                  reads=[PSb[bk]], writes=wr)
            return [], comp
        f._name = "load_x"
        return f

    out_toks = []

    def tap(name, ap, bufs):
        if name not in dbg or name in dbg_done:
            return
        dbg_done.add(name)
        dd = nc.dram_tensor("dbg_" + name, list(ap.shape), ap.dtype, kind="ExternalOutput").ap()
        out_toks.append(fw.dma("sp", "dbg", dd, ap, reads=bufs))

    def st_store(sq, seg):
        def f(W, Wb, Wo, Wob):
            def comp():
                fw.barrier()
                cv = Carver(SCR)
                Y = cv.take([8, TT], F32); Yb = [Buf() for _ in range(8)]
                OTK = [cv.take([D], F32) for _ in range(2)]; OTKb = [Buf(), Buf()]
                for tt in range(2):
                    rmsnorm([X[:, c, tsl(tt)] for c in range(8)], [Xb[c][tt] for c in range(8)],
                            [PRM[:, PRM_FINAL + c:PRM_FINAL + c + 1] for c in range(8)],
                            [Y[:, c, :] for c in range(8)], Yb, D)
                    for blk in range(4):
                        k = blk % 2
                        for hh in range(2):
                            bk = bank("a")
                            rd = [Yb[c] for c in range(hh * 4, hh * 4 + 4)] + [IDENTb]
                            fw.wait("pe", fw.deps(rd, [PSb[bk]]))
                            tok = None
                            for c4 in range(4):
                                c = hh * 4 + c4
                                tok = fw.raw("pe", lambda E, c=c, c4=c4, bk=bk, blk=blk: E.transpose(
                                    out=PS[bk][:, c4 * 128:(c4 + 1) * 128], in_=Y[:, c, blk * 128:(blk + 1) * 128], identity=IDENT[:]))
                            fw.mark(tok, rd, [PSb[bk]])
                            dst = OTK[k][:, hh * 512:(hh + 1) * 512]
                            if hh == 0:
                                fw.op("act", lambda E, dst=dst, bk=bk: E.activation(out=dst, in_=PS[bk][:, :], func=AF.Copy),
                                      reads=[PSb[bk]], writes=[OTKb[k]])
                            else:
                                fw.op("dve", lambda E, dst=dst, bk=bk: E.tensor_copy(out=dst, in_=PS[bk][:, :]),
                                      reads=[PSb[bk]], writes=[OTKb[k]])
                        r0 = sq * S + seg * TS + tt * TT + blk * 128
                        out_toks.append(fw.dma("sp", "st%d" % k, out_d[r0:r0 + 128, :], OTK[k], reads=[OTKb[k]]))
            return [], comp
        f._name = "store"
        return f

    def st_ffn(l, which, gi):
        f0, nf = FFN_GROUPS[gi]
        gname = "ffn1_norm" if which == 0 else "ffn2_norm"

        def f(W, Wb, Wo, Wob):
            wg, o = wview(W, 0, [8, nf * 128])
            wu, o = wview(W, o, [8, nf * 128])
            wd, o = wview(W, o, [nf, D])
            dmas = [(wg, kview(wgu_d[which][l, :, f0 * 128:(f0 + nf) * 128])),
                    (wu, kview(wgu_d[which][l, :, DFF + f0 * 128:DFF + (f0 + nf) * 128])),
                    (wd, wdn_d[which][l, f0 * 128:(f0 + nf) * 128, :].rearrange("(f p) d -> p f d", p=128))]

            def comp():
                cv = Carver(SCR)
                H = cv.take([2, 6, TT], BF16)
                SG = [cv.take([TT], F32) for _ in range(2)]
                if gi == 0:
                    fw.barrier()
                    st_ffn.Hb = [[Buf() for _ in range(6)] for _ in range(2)]
                    st_ffn.SGb = [Buf(), Buf()]
                    norm_x(gname, l)
                Hb, SGb = st_ffn.Hb, st_ffn.SGb
                k = 0
                for tt in range(2):
                    nrd = [Nb[c][tt] for c in range(8)] + [Wb]
                    for fi in range(nf):
                        bg, bu = bank("a"), bank("a")
                        mm(bg, PS[bg][:, :], [(wg[:, c, fi * 128:(fi + 1) * 128], N[:, c, tsl(tt)]) for c in range(8)], nrd)
                        mm(bu, PS[bu][:, :], [(wu[:, c, fi * 128:(fi + 1) * 128], N[:, c, tsl(tt)]) for c in range(8)], nrd)
                        kk = k % 2
                        k += 1
                        fw.op("act", lambda E, kk=kk, bg=bg: E.activation(out=SG[kk], in_=PS[bg][:, :], func=AF.Silu),
                              reads=[PSb[bg]], writes=[SGb[kk]])
                        fw.op("dve", lambda E, kk=kk, bu=bu, tt=tt, fi=fi: E.tensor_tensor(
                            out=H[:, tt, fi, :], in0=SG[kk], in1=PS[bu][:, :], op=ALU.mult),
                            reads=[SGb[kk], PSb[bu]], writes=[Hb[tt][fi]])
                for tt in range(2):
                    hrd = [Hb[tt][fi] for fi in range(nf)] + [Wb]
                    for d in range(8):
                        bd = bank("b")
                        mm(bd, PS[bd][:, :], [(wd[:, fi, d * 128:(d + 1) * 128], H[:, tt, fi, :]) for fi in range(nf)], hrd)
                        fw.op("dve", lambda E, d=d, tt=tt, bd=bd: E.scalar_tensor_tensor(
                            out=X[:, d, tsl(tt)], in0=PS[bd][:, :], scalar=0.5, in1=X[:, d, tsl(tt)],
                            op0=ALU.mult, op1=ALU.add), reads=[PSb[bd], Xb[d][tt]], writes=[Xb[d][tt]])
            return dmas, comp
        f._name = "ffn"
        return f

    MGcv = Carver(SCR)
    MG = MGcv.take([8, TS], BF16)
    MGb = [[Buf() for _ in range(2)] for _ in range(8)]
    MIX_BASE = MGcv.off
    mix_first = [True]

    def gate_merge(l, br, tt, yw, Y, Yb, glw, Wbs, TMPG, TMPGb, GT, GTb):
        first = mix_first[0]
        for d in range(8):
            by, bgl = bank("a"), bank("a")
            mm(by, PS[by][:, :], [(yw[:, c, d * 128:(d + 1) * 128], Y[:, c, :]) for c in range(8)], Yb + Wbs)
            mm(bgl, PS[bgl][:, :], [(glw[:, c, d * 128:(d + 1) * 128], N[:, c, tsl(tt)]) for c in range(8)],
               [Nb[c][tt] for c in range(8)] + Wbs)
            k = d % 2
            fw.op("act", lambda E, k=k, bgl=bgl, d=d: E.activation(out=GT[k], in_=PS[bgl][:, :], func=AF.Sigmoid,
                                                                   bias=prm_col(l, "gate_bias", br * 8 + d)),
                  reads=[PSb[bgl], PRMb], writes=[GTb[k]])
            if first:
                fw.op("dve", lambda E, k=k, by=by, d=d: E.tensor_tensor(out=MG[:, d, tsl(tt)], in0=GT[k], in1=PS[by][:, :], op=ALU.mult),
                      reads=[GTb[k], PSb[by]], writes=[MGb[d][tt]])
            else:
                fw.op("dve", lambda E, k=k, by=by: E.tensor_tensor(out=TMPG[k], in0=GT[k], in1=PS[by][:, :], op=ALU.mult),
                      reads=[GTb[k], PSb[by]], writes=[TMPGb[k]])
                fw.op("pool", lambda E, k=k, d=d: E.tensor_tensor(out=MG[:, d, tsl(tt)], in0=MG[:, d, tsl(tt)], in1=TMPG[k], op=ALU.add),
                      reads=[TMPGb[k], MGb[d][tt]], writes=[MGb[d][tt]])

    def st_B1(l, sq, seg):
        def f(W, Wb, Wo, Wob):
            wat, o = wview(W, 0, [8, 768])
            wuq, o = wview(W, o, [3, 2048])
            wkv, o = wview(W, o, [2, 2048])
            dmas = [(wat, kview(wattn_d[l])), (wuq, kview(wuqx_d[l])), (wkv, kview(wukv_d[l]))]
            wrb, o2 = wview(Wo, 0, [8, D])
            wgl, o2 = wview(Wo, o2, [8, D])

            def comp():
                fw.barrier()
                cv = Carver(SCR, MIX_BASE)
                CQN = cv.take([3, TS], BF16); CQNb = [[Buf(), Buf()] for _ in range(3)]
                Q = [cv.take([TT], BF16) for _ in range(2)]; Qb = [Buf(), Buf()]
                K = [cv.take([S], BF16) for _ in range(2)]; Kb = [Buf(), Buf()]
                VE = cv.take([16, 2, 128], BF16); VEb = [Buf(), Buf()]
                PT = [cv.take([2 * TT], BF16) for _ in range(2)]; PTb = [Buf() for _ in range(2)]
                PTD = [cv.take([TT], BF16) for _ in range(2)]; PTDb = [Buf(), Buf()]
                OT = cv.take([8, TT], BF16); OTb = [Buf() for _ in range(8)]
                TAB = cv.take([TS], F32); TABb = Buf()
                T1 = cv.take([TT], F32); T1b = Buf()
                T2 = cv.take([TT], F32); T2b = Buf()
                TQ1 = [T1, T1]; TQ1b = [T1b, T1b]
                TQ2 = [T2, T2]; TQ2b = [T2b, T2b]
                RC = cv.take([TT], F32); RCb = Buf()
                assert cv.off <= SCRW * 4, cv.off
                cdst = l if seg == 0 else L
                scale = float((64 + 32) ** -0.5)
                norm_x("mix_norm", l)
                fw.dma("sp", "ld_tab", TAB[64:128, :], rope_d[:, seg * TS:(seg + 1) * TS], writes=[TABb])
                fw.op("pool", lambda E: E.memset(VE[:, :, :, 64:128], 1.0), writes=VEb)
                for k in range(2):
                    fw.op("pool", lambda E, k=k: E.memset(Q[k][64:128, :], 0.0), writes=[Qb[k]])
                    fw.op("pool", lambda E, k=k: E.memset(K[k][64:128, :], 0.0), writes=[Kb[k]])
                for k in range(2):
                    fw.op("pool", lambda E, k=k: E.memset(PTD[k][64:128, 0:64], 0.0), writes=[PTDb[k]])
                for tt in range(2):
                    nrd = [Nb[c][tt] for c in range(8)] + [Wb]
                    bks = [bank("a") for _ in range(3)]
                    for j in range(3):
                        mm(bks[j], PS[bks[j]][:, :], [(wat[:, c, j * 128:(j + 1) * 128], N[:, c, tsl(tt)]) for c in range(8)], nrd)
                    rmsnorm([PS[bks[j]][:, :] for j in range(3)], [PSb[bks[j]] for j in range(3)],
                            [prm_col(l, "mla_q_norm", j) for j in range(3)],
                            [CQN[:, j, tsl(tt)] for j in range(3)], [CQNb[j][tt] for j in range(3)], 384)
                    bks = [bank("a") for _ in range(2)]
                    for j in range(2):
                        mm(bks[j], PS[bks[j]][:, :], [(wat[:, c, 384 + j * 128:384 + (j + 1) * 128], N[:, c, tsl(tt)]) for c in range(8)], nrd)
                    rmsnorm([PS[bks[j]][:, :] for j in range(2)], [PSb[bks[j]] for j in range(2)],
                            [prm_col(l, "mla_kv_norm", j) for j in range(2)],
                            [CKV[cdst][:, j, tsl(tt)] for j in range(2)], [CKVb[cdst], CKVb[cdst]], 256)
                    bkr = bank("a")
                    mm(bkr, PS[bkr][:, :], [(wat[:, c, 640:768], N[:, c, tsl(tt)]) for c in range(8)], nrd)
                    fw.op("dve", lambda E, bkr=bkr, tt=tt: E.tensor_tensor(out=T1[64:96, :], in0=PS[bkr][64:96, :], in1=TAB[64:96, tsl(tt)], op=ALU.mult),
                          reads=[PSb[bkr], TABb], writes=[T1b])
                    fw.op("dve", lambda E, bkr=bkr, tt=tt: E.tensor_tensor(out=T2[64:96, :], in0=PS[bkr][96:128, :], in1=TAB[96:128, tsl(tt)], op=ALU.mult),
                          reads=[PSb[bkr], TABb], writes=[T2b])
                    fw.op("pool", lambda E, tt=tt: E.tensor_tensor(out=KR[cdst][64:96, tsl(tt)], in0=T1[64:96, :], in1=T2[64:96, :], op=ALU.add),
                          reads=[T1b, T2b], writes=[KRb[cdst]])
                tap("cqn", CQN, [b for bb in CQNb for b in bb])
                tap("ckv", CKV[cdst][:], [CKVb[cdst]])
                tap("kr", KR[cdst][64:96, :], [KRb[cdst]])
                ksrc = [l] if seg == 0 else [l, L]
                nhalf = len(ksrc)
                for kb_ in range(2):
                    for hf in range(nhalf):
                        fw.op("act", lambda E, kb_=kb_, hf=hf: E.activation(out=K[kb_][64:96, hf * TS:(hf + 1) * TS],
                                                                            in_=KR[ksrc[hf]][64:96, :], func=AF.Copy),
                              reads=[KRb[ksrc[hf]]], writes=[Kb[kb_]])

                def lat(kc, k0, n):
                    hf = k0 // TS
                    return CKV[ksrc[hf]][:, kc, k0 - hf * TS:k0 - hf * TS + n], CKVb[ksrc[hf]]

                ptc = [0]
                ptdc = [0]
                pairc = [0]
                order = [(tt, h) for tt in range(2) for h in range(NH)]

                def produce(i):
                    tt, h = order[i]
                    kk = i % 2
                    gq = seg * 2 + tt
                    kend = (gq + 1) * TT
                    nkb = kend // 128
                    lrd = [CKVb[j] for j in ksrc] + [Wb]
                    for kb0 in range(0, nkb, 8):
                        nb8 = min(8, nkb - kb0)
                        bv = bank("c")
                        fw.wait("pe", fw.deps(lrd, [PSb[bv]]))
                        tok = None
                        for q8 in range(nb8):
                            kb = kb0 + q8
                            for kc in range(2):
                                la, lab = lat(kc, kb * 128, 128)
                                tok = fw.raw("pe", lambda E, la=la, kc=kc, q8=q8, bv=bv, h=h: E.matmul(
                                    out=PS[bv][:, q8 * 64:(q8 + 1) * 64], lhsT=la, rhs=wkv[:, kc, h * 128 + 64:(h + 1) * 128],
                                    start=(kc == 0), stop=(kc == 1)))
                        fw.mark(tok, lrd, [PSb[bv]])
                        fw.op("dve", lambda E, kb0=kb0, nb8=nb8, bv=bv, kk=kk: E.tensor_copy(
                            out=VE[:, kb0:kb0 + nb8, kk, 0:64],
                            in_=PS[bv][:, 0:nb8 * 64].rearrange("p (a e) -> p a e", a=nb8)), reads=[PSb[bv]], writes=[VEb[kk]])
                    for kt in range(kend // TT):
                        bkk = bank("c")
                        pairs = []
                        for kc in range(2):
                            la, lab = lat(kc, kt * TT, TT)
                            pairs.append((wkv[:, kc, h * 128:(h + 1) * 128], la))
                        mm(bkk, PS[bkk][:, :], pairs, lrd)
                        fw.op("dve", lambda E, kk=kk, kt=kt, bkk=bkk: E.tensor_copy(out=K[kk][0:64, kt * TT:(kt + 1) * TT],
                                                                               in_=PS[bkk][0:64, :]),
                              reads=[PSb[bkk]], writes=[Kb[kk]])
                    bq = bank("c")
                    mm(bq, PS[bq][:, :], [(wuq[:, kc, h * 128:(h + 1) * 128], CQN[:, kc, tsl(tt)]) for kc in range(3)],
                       [CQNb[kc][tt] for kc in range(3)] + [Wb])
                    fw.op("dve", lambda E, kk=kk, bq=bq: E.tensor_copy(out=Q[kk][0:64, :], in_=PS[bq][0:64, :]),
                          reads=[PSb[bq]], writes=[Qb[kk]])
                    fw.op("dve", lambda E, bq=bq, tt=tt, kk=kk: E.tensor_tensor(out=TQ1[kk][64:96, :], in0=PS[bq][64:96, :], in1=TAB[64:96, tsl(tt)], op=ALU.mult),
                          reads=[PSb[bq], TABb], writes=[TQ1b[kk]])
                    fw.op("dve", lambda E, bq=bq, tt=tt, kk=kk: E.tensor_tensor(out=TQ2[kk][64:96, :], in0=PS[bq][96:128, :], in1=TAB[96:128, tsl(tt)], op=ALU.mult),
                          reads=[PSb[bq], TABb], writes=[TQ2b[kk]])
                    fw.op("pool", lambda E, kk=kk: E.tensor_tensor(out=Q[kk][64:96, :], in0=TQ1[kk][64:96, :], in1=TQ2[kk][64:96, :], op=ALU.add),
                          reads=[TQ1b[kk], TQ2b[kk]], writes=[Qb[kk]])

                def attend(i):
                    tt, h = order[i]
                    kk = i % 2
                    gq = seg * 2 + tt
                    nkb = (gq + 1) * TT // 128
                    hp, hh = h // 2, h % 2
                    bo = bank("b")

                    def issue_pv(item, last):
                        kb, q_lo, pt, ptb, first = item
                        fw.wait("pe", fw.deps([ptb, VEb[kk]], [PSb[bo]] if first else []))
                        tok = fw.raw("pe", lambda E, kb=kb, q_lo=q_lo, pt=pt, first=first, last=last: E.matmul(
                            out=PS[bo][:, q_lo:TT], lhsT=VE[:, kb, kk, :], rhs=pt, start=first, stop=last))
                        fw.mark(tok, [ptb, VEb[kk]], [PSb[bo]])

                    nd = gq * 4
                    units = [("pair", j) for j in range(0, nd, 2)] + [("diag", kb) for kb in range(nd, nkb)]
                    prev = None
                    for u in units:
                        if u[0] == "pair":
                            kb = u[1]
                            p = pairc[0] % 2
                            pairc[0] += 1
                            for j in range(2):
                                mm(2 * p + j, PS[2 * p + j][:, :], [(K[kk][:, (kb + j) * 128:(kb + j + 1) * 128], Q[kk][:, :])], [Kb[kk], Qb[kk]])
                            pi = ptc[0] % 2
                            ptc[0] += 1
                            pt, ptb = PT[pi], PTb[pi]
                            fw.op("act", lambda E, pt=pt, p=p: E.activation(out=pt[:, :], in_=PSP[p][:, :], func=AF.Exp, scale=scale),
                                  reads=[PSb[2 * p], PSb[2 * p + 1]], writes=[ptb])
                            items = [(kb, 0, pt[:, 0:TT], ptb, kb == 0), (kb + 1, 0, pt[:, TT:2 * TT], ptb, False)]
                        else:
                            kb = u[1]
                            q_lo = kb * 128 - gq * TT
                            nq = TT - q_lo
                            bs = bank("a")
                            mm(bs, PS[bs][:, 0:nq], [(K[kk][:, kb * 128:(kb + 1) * 128], Q[kk][:, q_lo:TT])], [Kb[kk], Qb[kk]])
                            j = ptdc[0] % 2
                            ptdc[0] += 1
                            pt, ptb = PTD[j], PTDb[j]
                            fw.op("act", lambda E, pt=pt, bs=bs, nq=nq: E.activation(out=pt[0:64, 0:nq], in_=PS[bs][0:64, 0:nq], func=AF.Exp, scale=scale),
                                  reads=[PSb[bs]], writes=[ptb])
                            if nq > 64:
                                fw.op("act", lambda E, pt=pt, bs=bs, nq=nq: E.activation(out=pt[64:128, 64:nq], in_=PS[bs][64:128, 64:nq], func=AF.Exp, scale=scale),
                                      reads=[PSb[bs]], writes=[ptb])
                            items = [(kb, q_lo, pt[:, 0:nq], ptb, kb == 0)]
                        if prev is not None:
                            for it in prev:
                                issue_pv(it, False)
                        prev = items
                    for idx, it in enumerate(prev):
                        issue_pv(it, idx == len(prev) - 1)
                    fw.op("act", lambda E: E.activation(out=RC[64:128, :], in_=PS[bo][64:128, :], func=AF.Ln),
                          reads=[PSb[bo]], writes=[RCb])
                    fw.op("act", lambda E: E.activation(out=RC[64:128, :], in_=RC[64:128, :], func=AF.Exp, scale=-1.0),
                          reads=[RCb], writes=[RCb])
                    fw.op("dve", lambda E: E.tensor_copy(out=RC[0:64, :], in_=RC[64:128, :]), reads=[RCb], writes=[RCb])
                    fw.op("dve", lambda E: E.tensor_tensor(out=OT[hh * 64:(hh + 1) * 64, hp, :], in0=PS[bo][0:64, :],
                                                           in1=RC[0:64, :], op=ALU.mult),
                          reads=[PSb[bo], RCb], writes=[OTb[hp]])

                produce(0)
                for i in range(len(order)):
                    if i + 1 < len(order):
                        produce(i + 1)
                    attend(i)
                    tt, h = order[i]
                    if h == NH - 1:
                        assert mix_first[0]
                        gate_merge(l, 1, tt, wrb, OT, OTb, wgl, [Wob], None, None, [T1, T2], [T1b, T2b])
                mix_first[0] = False
            return dmas, comp
        f._name = "B1"
        return f

    def st_B2(l):
        def f(W, Wb, Wo, Wob):
            wrb, o = wview(W, 0, [8, D])
            wgl, o = wview(W, o, [8, D])
            dmas = [(wrb, kview(wrb_d[l])), (wgl, kview(win_d[l, :, C_GL + D:C_GL + 2 * D]))]
            return dmas, (lambda: None)
        f._name = "B2"
        return f

    def st_A1(l, sq, seg):
        def f(W, Wb, Wo, Wob):
            wxa, o = wview(W, 0, [8, D])
            wga, o = wview(W, o, [8, D])
            lwa, o = wview(W, o, [8, 128])
            lwx, o = wview(W, o, [8, 128])
            dmas = [(wxa, kview(win_d[l, :, C_XA:C_XA + D])), (wga, kview(win_d[l, :, C_GA:C_GA + D]))]
            for j in range(2):
                dmas.append((lwa[64 * j:64 * j + 64, :, 64 * j:64 * j + 64],
                             lwa_d[l].rearrange("(n two) c d -> two c n d", two=2)[j]))
                dmas.append((lwx[64 * j:64 * j + 64, :, 64 * j:64 * j + 64],
                             lwx_d[l].rearrange("(n two) c d -> two c n d", two=2)[j]))
            wra, o2 = wview(Wo, 0, [8, D])
            wgl, o2 = wview(Wo, o2, [8, D])
            pre = [("z", lwa[0:64, :, 64:128]), ("z", lwa[64:128, :, 0:64]), ("z", lwx[0:64, :, 64:128]), ("z", lwx[64:128, :, 0:64])]

            def comp():
                fw.barrier()
                cv = Carver(SCR, MIX_BASE)
                NS = 3
                XA = [cv.take([TT + 3], F32) for _ in range(NS)]; XAb = [Buf() for _ in range(NS)]
                XC = [cv.take([TT], F32) for _ in range(NS)]; XCb = [Buf() for _ in range(NS)]
                XCH = [cv.take([TT], BF16) for _ in range(NS)]; XCHb = [Buf() for _ in range(NS)]
                RI = [cv.take([2, TT], F32) for _ in range(NS)]; RIb = [Buf() for _ in range(NS)]
                T = [cv.take([TT], F32) for _ in range(NS)]; Tb = [Buf() for _ in range(NS)]
                HH, HHb = T, Tb
                GG8 = cv.take([8, TT], BF16); GG8b = [Buf() for _ in range(8)]
                AT = cv.take([8, TT], BF16); ATb = [Buf() for _ in range(8)]
                GT, GTb = XC[0:2], XCb[0:2]
                TM, TMb = [RI[0][:, 0, :], RI[0][:, 1, :]], [RIb[0], RIb[0]]
                slot = [0]
                assert cv.off <= SCRW * 4, cv.off
                if "B" not in phases:
                    norm_x("mix_norm", l)
                for tt in range(2):
                    nrd = [Nb[c][tt] for c in range(8)] + [Wb]
                    for c in range(8):
                        bg = bank("a")
                        mm(bg, PS[bg][:, :], [(wga[:, kc, c * 128:(c + 1) * 128], N[:, kc, tsl(tt)]) for kc in range(8)], nrd)
                        fw.op("act", lambda E, c=c, bg=bg: E.activation(out=GG8[:, c, :], in_=PS[bg][:, :], func=AF.Gelu_apprx_tanh),
                              reads=[PSb[bg]], writes=[GG8b[c]])
                    base = slot[0]
                    slot[0] += 8

                    def stage_a(c):
                        k = (base + c) % NS
                        bx = bank("a")
                        mm(bx, PS[bx][:, :], [(wxa[:, kc, c * 128:(c + 1) * 128], N[:, kc, tsl(tt)]) for kc in range(8)], nrd)
                        fw.op("pool", lambda E, k=k, c=c: E.tensor_copy(out=XA[k][:, 0:3], in_=HALO[:, l, c, :]),
                              reads=[HALOb[l][c]], writes=[XAb[k]])
                        fw.op("act", lambda E, k=k, bx=bx: E.activation(out=XA[k][:, 3:TT + 3], in_=PS[bx][:, :], func=AF.Copy),
                              reads=[PSb[bx]], writes=[XAb[k]])
                        fw.op("pool", lambda E, k=k, c=c: E.tensor_copy(out=HALO[:, l, c, :], in_=XA[k][:, TT:TT + 3]),
                              reads=[XAb[k]], writes=[HALOb[l][c]])
                        cw = lambda tap, c=c: prm_col(l, "conv_w", tap * 8 + c)
                        fw.op("dve", lambda E, k=k, c=c, cw=cw: E.tensor_scalar(out=XC[k], in0=XA[k][:, 0:TT], scalar1=cw(0),
                                                                             scalar2=prm_col(l, "conv_b", c), op0=ALU.mult, op1=ALU.add),
                              reads=[XAb[k], PRMb], writes=[XCb[k]])
                        for tap in range(1, 4):
                            fw.op("dve", lambda E, k=k, tap=tap, cw=cw: E.scalar_tensor_tensor(
                                out=XC[k], in0=XA[k][:, tap:tap + TT], scalar=cw(tap), in1=XC[k], op0=ALU.mult, op1=ALU.add),
                                reads=[XAb[k], XCb[k], PRMb], writes=[XCb[k]])
                        fw.op("dve", lambda E, k=k: E.tensor_copy(out=XCH[k], in_=XC[k]), reads=[XCb[k]], writes=[XCHb[k]])

                    def stage_b(c):
                        k = (base + c) % NS
                        R = RI[k][:, 0, :]
                        I = RI[k][:, 1, :]
                        RI2 = RI[k][:, :, :].rearrange("p a t -> p (a t)")
                        br_, bi_ = bank("c"), bank("c")
                        mm(br_, PS[br_][:, :], [(lwa[:, c, :], XCH[k])], [XCHb[k], Wb])
                        mm(bi_, PS[bi_][:, :], [(lwx[:, c, :], XCH[k])], [XCHb[k], Wb])
                        fw.op("act", lambda E, R=R, br_=br_, c=c: E.activation(out=R, in_=PS[br_][:, :], func=AF.Exp, scale=-1.0, bias=NEGB[:, l, 0, c:c + 1]),
                              reads=[PSb[br_], NEGBb], writes=[RIb[k]])
                        fw.op("act", lambda E, I=I, bi_=bi_, c=c: E.activation(out=I, in_=PS[bi_][:, :], func=AF.Exp, scale=-1.0, bias=NEGB[:, l, 1, c:c + 1]),
                              reads=[PSb[bi_], NEGBb], writes=[RIb[k]])
                        fw.op("act", lambda E, RI2=RI2: E.activation(out=RI2, in_=RI2, func=AF.Ln, bias=one_ap), reads=[RIb[k], CSTb], writes=[RIb[k]])
                        fw.op("act", lambda E, RI2=RI2: E.activation(out=RI2, in_=RI2, func=AF.Exp, scale=-1.0), reads=[RIb[k]], writes=[RIb[k]])
                        fw.op("act", lambda E, k=k, R=R, c=c: E.activation(out=T[k], in_=R, func=AF.Exp, scale=CLAM[:, l, 1, c:c + 1]),
                              reads=[RIb[k], CLAMb], writes=[Tb[k]])
                        fw.op("act", lambda E, R=R, c=c: E.activation(out=R, in_=R, func=AF.Exp, scale=CLAM[:, l, 0, c:c + 1]),
                              reads=[RIb[k], CLAMb], writes=[RIb[k]])
                        fw.op("act", lambda E, k=k: E.activation(out=T[k], in_=T[k], func=AF.Ln, scale=-1.0, bias=one_ap),
                              reads=[Tb[k], CSTb], writes=[Tb[k]])
                        fw.op("act", lambda E, k=k: E.activation(out=T[k], in_=T[k], func=AF.Exp, scale=0.5),
                              reads=[Tb[k]], writes=[Tb[k]])
                        if seg == 0 and tt == 0:
                            fw.op("dve", lambda E, k=k: E.memset(T[k][:, 0:1], 1.0), reads=[], writes=[Tb[k]])
                        fw.op("pool", lambda E, k=k, I=I: E.tensor_tensor(out=I, in0=I, in1=T[k], op=ALU.mult),
                              reads=[RIb[k], Tb[k]], writes=[RIb[k]])
                        fw.op("pool", lambda E, k=k, I=I: E.tensor_tensor(out=I, in0=I, in1=XC[k], op=ALU.mult),
                              reads=[RIb[k], XCb[k]], writes=[RIb[k]])
                        fw.op("dve", lambda E, k=k, c=c, R=R, I=I: E.tensor_tensor_scan(out=HH[k], data0=R, data1=I, initial=CAR[:, l, c:c + 1],
                                                                                      op0=ALU.mult, op1=ALU.add),
                              reads=[RIb[k], CARb[l][c], Tb[k]], writes=[HHb[k]])
                        fw.op("dve", lambda E, k=k, c=c: E.tensor_copy(out=CAR[:, l, c:c + 1], in_=HH[k][:, TT - 1:TT]),
                              reads=[HHb[k]], writes=[CARb[l][c]])
                        fw.op("dve", lambda E, k=k, c=c: E.tensor_tensor(out=AT[:, c, :], in0=HH[k], in1=GG8[:, c, :], op=ALU.mult),
                              reads=[HHb[k], GG8b[c]], writes=[ATb[c]])

                    for i in range(8 + 2):
                        if i < 8:
                            stage_a(i)
                        if i >= 2:
                            stage_b(i - 2)
                    gate_merge(l, 0, tt, wra, AT, ATb, wgl, [Wob], TM, TMb, GT, GTb)
                mix_first[0] = False
            return dmas, comp, pre
        f._name = "A1"
        return f

    def st_A2(l):
        def f(W, Wb, Wo, Wob):
            wra, o = wview(W, 0, [8, D])
            wgl, o = wview(W, o, [8, D])
            dmas = [(wra, kview(wra_d[l])), (wgl, kview(win_d[l, :, C_GL:C_GL + D]))]
            return dmas, (lambda: None)
        f._name = "A2"
        return f

    def st_C1(l, sq, seg):
        def f(W, Wb, Wo, Wob):
            wzu, o = wview(W, 0, [8, D])
            wzv, o = wview(W, o, [8, D])
            wmT, o = wview(W, o, [8, 128])
            dmas = [(wzu, kview(win_d[l, :, C_ZU:C_ZU + D])), (wzv, kview(win_d[l, :, C_ZV:C_ZV + D])), (wmT, swT_d[l])]
            wrc, o2 = wview(Wo, 0, [8, D])
            wgl, o2 = wview(Wo, o2, [8, D])
            post = [("z", wmT[64:128, :, 0:64])]

            def comp():
                fw.barrier()
                cv = Carver(SCR, MIX_BASE)
                UT = cv.take([8, TT], BF16); UTb = [Buf() for _ in range(8)]
                VT = [cv.take([D], F32) for _ in range(2)]; VTb = [Buf(), Buf()]
                VH = [cv.take([D], BF16) for _ in range(2)]; VHb = [Buf(), Buf()]
                CT = cv.take([8, TT], BF16); CTb = [Buf() for _ in range(8)]
                TC1 = cv.take([8, 128], F32); TC = [TC1, TC1]; TCb1 = Buf(); TCb = [TCb1, TCb1]
                B2 = cv.take([8, 128], F32); B2b = Buf()
                STA = [cv.take([16], F32) for _ in range(2)]; STAb = [Buf(), Buf()]
                GT = [cv.take([TT], F32) for _ in range(2)]; GTb = [Buf(), Buf()]
                TMall = cv.take([2, TT], F32); TM = [TMall[:, 0, :], TMall[:, 1, :]]; TMb = [Buf(), Buf()]
                BSB = TMall.rearrange("p a (b c) -> p (a b) c", c=128)
                assert cv.off <= SCRW * 4, cv.off
                if "B" not in phases and "A" not in phases:
                    norm_x("mix_norm", l)
                fw.dma("sp", "ld_bsb", BSB, sbs_d[l].rearrange("(g t) -> g t", g=8).partition_broadcast(128), writes=TMb)
                for hh in range(2):
                    bb = bank("c")
                    mm(bb, PS[bb][:, :], [(ONES[:], wmT[:, hh * 4:hh * 4 + 4, :])], [ONESb, Wb])
                    for g4 in range(4):
                        g = hh * 4 + g4
                        fw.op("dve", lambda E, g=g, g4=g4, bb=bb: E.scalar_tensor_tensor(
                            out=B2[:, g, :], in0=PS[bb][:, g4 * 128:(g4 + 1) * 128], scalar=prm_col(l, "sgu_norm_b", g),
                            in1=BSB[:, g, :], op0=ALU.mult, op1=ALU.add), reads=[PSb[bb], PRMb] + TMb, writes=[B2b])
                for tt in range(2):
                    nrd = [Nb[c][tt] for c in range(8)] + [Wb]
                    for g in range(8):
                        bu = bank("a")
                        mm(bu, PS[bu][:, :], [(wzu[:, kc, g * 128:(g + 1) * 128], N[:, kc, tsl(tt)]) for kc in range(8)], nrd)
                        fw.op("act", lambda E, g=g, bu=bu: E.activation(out=UT[:, g, :], in_=PS[bu][:, :], func=AF.Gelu_apprx_tanh),
                              reads=[PSb[bu]], writes=[UTb[g]])
                    def stage_v(blk):
                        k = blk % 2
                        t0 = tt * TT + blk * 128
                        for hv in range(2):
                            bv = bank("a")
                            mm(bv, PS[bv][:, :], [(N[:, kc, t0:t0 + 128], wzv[:, kc, hv * 512:(hv + 1) * 512]) for kc in range(8)], nrd)
                            fw.op("act", lambda E, k=k, hv=hv, bv=bv: E.activation(out=VT[k][:, hv * 512:(hv + 1) * 512], in_=PS[bv][:, :],
                                                                                  func=AF.Gelu_apprx_tanh), reads=[PSb[bv]], writes=[VTb[k]])
                        for hv in range(2):
                            fw.op("dve", lambda E, k=k, hv=hv: E.bn_stats(out=STA[k][:, hv * 6:(hv + 1) * 6], in_=VT[k][:, hv * 512:(hv + 1) * 512]),
                                  reads=[VTb[k]], writes=[STAb[k]])
                        fw.op("dve", lambda E, k=k: E.bn_aggr(out=STA[k][:, 12:14], in_=STA[k][:, 0:12]), reads=[STAb[k]], writes=[STAb[k]])
                        fw.op("act", lambda E, k=k: E.activation(out=STA[k][:, 14:15], in_=STA[k][:, 13:14], func=AF.Sqrt, bias=eps_ap),
                              reads=[STAb[k], CSTb], writes=[STAb[k]])
                        fw.op("dve", lambda E, k=k: E.reciprocal(out=STA[k][:, 15:16], in_=STA[k][:, 14:15]), reads=[STAb[k]], writes=[STAb[k]])
                        fw.op("dve", lambda E, k=k: E.tensor_scalar(out=VH[k], in0=VT[k], scalar1=STA[k][:, 12:13], scalar2=STA[k][:, 15:16],
                                                                   op0=ALU.subtract, op1=ALU.mult), reads=[VTb[k], STAb[k]], writes=[VHb[k]])

                    def stage_p(blk):
                        k = blk % 2
                        for hg in range(2):
                            bp = bank("c")
                            fw.wait("pe", fw.deps([VHb[k], Wb], [PSb[bp]]))
                            tok = None
                            for g4 in range(4):
                                g = hg * 4 + g4
                                tok = fw.raw("pe", lambda E, k=k, g=g, g4=g4, bp=bp: E.matmul(
                                    out=PS[bp][:, g4 * 128:(g4 + 1) * 128], lhsT=VH[k][:, g * 128:(g + 1) * 128], rhs=wmT[:, g, :],
                                    start=True, stop=True))
                            fw.mark(tok, [VHb[k], Wb], [PSb[bp]])
                            for g4 in range(4):
                                g = hg * 4 + g4
                                fw.op("dve", lambda E, k=k, g=g, g4=g4, bp=bp: E.scalar_tensor_tensor(
                                    out=TC[k][:, g, :], in0=PS[bp][:, g4 * 128:(g4 + 1) * 128], scalar=prm_col(l, "sgu_norm_g", g),
                                    in1=B2[:, g, :], op0=ALU.mult, op1=ALU.add), reads=[PSb[bp], B2b, PRMb], writes=[TCb[k]])
                        fw.op("pool", lambda E, k=k, blk=blk: E.tensor_tensor(out=CT[:, :, blk * 128:(blk + 1) * 128], in0=TC[k][:],
                                                                             in1=UT[:, :, blk * 128:(blk + 1) * 128], op=ALU.mult),
                              reads=[TCb[k]] + UTb, writes=CTb)

                    for i in range(4 + 1):
                        if i < 4:
                            stage_v(i)
                        if i >= 1:
                            stage_p(i - 1)
                    gate_merge(l, 2, tt, wrc, CT, CTb, wgl, [Wob], TM, TMb, GT, GTb)
                mix_first[0] = False
            return dmas, comp, [], post
        f._name = "C1"
        return f

    def st_C2(l):
        def f(W, Wb, Wo, Wob):
            wrc, o = wview(W, 0, [8, D])
            wgl, o = wview(W, o, [8, D])
            dmas = [(wrc, kview(wrc_d[l])), (wgl, kview(win_d[l, :, C_GL + 2 * D:C_GL + 3 * D]))]
            return dmas, (lambda: None)
        f._name = "C2"
        return f

    def st_wout(l):
        def f(W, Wb, Wo, Wob):
            wo, o = wview(W, 0, [8, D])
            dmas = [(wo, kview(wout_d[l]))]

            def comp():
                for tt in range(2):
                    rd = [MGb[c][tt] for c in range(8)] + [Wb]
                    for d in range(8):
                        bd = bank("b")
                        mm(bd, PS[bd][:, :], [(wo[:, c, d * 128:(d + 1) * 128], MG[:, c, tsl(tt)]) for c in range(8)], rd)
                        fw.op("dve", lambda E, d=d, tt=tt, bd=bd: E.tensor_tensor(out=X[:, d, tsl(tt)], in0=X[:, d, tsl(tt)], in1=PS[bd][:, :], op=ALU.add),
                              reads=[PSb[bd], Xb[d][tt]], writes=[Xb[d][tt]])
                mix_first[0] = True
            return dmas, comp
        f._name = "wout"
        return f

    def st_ple(l, sq, seg):
        def f(W, Wb, Wo, Wob):
            wpg, o = wview(W, 0, [8, D])
            wpp, o = wview(W, o, [2, D])
            dmas = [(wpg, kview(wpg_d[l])), (wpp, kview(wpp_d[l]))]

            def comp():
                fw.barrier()
                cv = Carver(SCR)
                PTK = [cv.take([PLE], F32) for _ in range(2)]; PTKb = [Buf(), Buf()]
                PTT = cv.take([2, TS], BF16); PTTb = [Buf(), Buf()]
                GT = [cv.take([TT], F32) for _ in range(2)]; GTb = [Buf(), Buf()]
                TM = [cv.take([TT], F32) for _ in range(2)]; TMb = [Buf(), Buf()]
                norm_x("ple_norm", l)
                for tb in range(8):
                    k = tb % 2
                    r0 = sq * S + seg * TS + tb * 128
                    fw.dma("sp", "pl%d" % k, PTK[k], p_d[l, r0:r0 + 128, :], writes=[PTKb[k]])
                    bk = bank("c")
                    fw.wait("pe", fw.deps([PTKb[k], IDENTb], [PSb[bk]]))
                    tok = None
                    for j in range(2):
                        tok = fw.raw("pe", lambda E, j=j, k=k, bk=bk: E.transpose(out=PS[bk][:, j * 128:(j + 1) * 128],
                                                                              in_=PTK[k][:, j * 128:(j + 1) * 128], identity=IDENT[:]))
                    fw.mark(tok, [PTKb[k], IDENTb], [PSb[bk]])
                    fw.op("act", lambda E, tb=tb, bk=bk: E.activation(out=PTT[:, :, tb * 128:(tb + 1) * 128],
                                                                  in_=PS[bk][:, 0:256].rearrange("p (a b) -> p a b", a=2), func=AF.Copy),
                          reads=[PSb[bk]], writes=[PTTb[tb // 4]])
                for tt in range(2):
                    nrd = [Nb[c][tt] for c in range(8)] + [Wb]
                    for d in range(8):
                        bg, bp = bank("a"), bank("a")
                        mm(bg, PS[bg][:, :], [(wpg[:, c, d * 128:(d + 1) * 128], N[:, c, tsl(tt)]) for c in range(8)], nrd)
                        mm(bp, PS[bp][:, :], [(wpp[:, j, d * 128:(d + 1) * 128], PTT[:, j, tsl(tt)]) for j in range(2)], [PTTb[tt], Wb])
                        k = d % 2
                        fw.op("act", lambda E, k=k, bg=bg: E.activation(out=GT[k], in_=PS[bg][:, :], func=AF.Sigmoid),
                              reads=[PSb[bg]], writes=[GTb[k]])
                        fw.op("dve", lambda E, k=k, bp=bp: E.tensor_tensor(out=TM[k], in0=GT[k], in1=PS[bp][:, :], op=ALU.mult),
                              reads=[GTb[k], PSb[bp]], writes=[TMb[k]])
                        fw.op("pool", lambda E, k=k, d=d, tt=tt: E.tensor_tensor(out=X[:, d, tsl(tt)], in0=X[:, d, tsl(tt)], in1=TM[k], op=ALU.add),
                              reads=[TMb[k], Xb[d][tt]], writes=[Xb[d][tt]])
            return dmas, comp
        f._name = "ple"
        return f

    for sq in range(n_seq):
        for seg in range(n_seg):
            stages.append(st_load_x(sq, seg))
            for l in range(n_layers):
                if "ffn1" in phases:
                    for gi in range(len(FFN_GROUPS)):
                        stages.append(st_ffn(l, 0, gi))
                if "B" in phases:
                    stages.append(st_B1(l, sq, seg))
                    stages.append(st_B2(l))
                if "A" in phases:
                    stages.append(st_A1(l, sq, seg))
                    stages.append(st_A2(l))
                if "C" in phases:
                    stages.append(st_C1(l, sq, seg))
                    stages.append(st_C2(l))
                if "wout" in phases:
                    stages.append(st_wout(l))
                if "ffn2" in phases:
                    for gi in range(len(FFN_GROUPS)):
                        stages.append(st_ffn(l, 1, gi))
                if "ple" in phases:
                    stages.append(st_ple(l, sq, seg))
            stages.append(st_store(sq, seg))

    half = 0
    pending = None
    marks = []
    nc._marks = marks

    def pe_count():
        return sum(1 for it in fw.q["pe"] if it[0] == "o")

    for sfn in stages:
        res = sfn(WA[half], WAb[half], WA[1 - half], WAb[1 - half])
        dmas, comp = res[0], res[1]
        pre = res[2] if len(res) > 2 else []
        post = res[3] if len(res) > 3 else []
        if dmas:
            hb = WAb[half]
            fw.wait("pool", fw.deps([], [hb]))
            tok = None
            for kind, ap in pre:
                tok = fw.raw("pool", lambda E, ap=ap: E.memset(ap, 0.0))
            if tok is not None:
                fw.wait("pool", {tok[0]: tok[1]})
            for dst, src in dmas:
                tok = fw.dma_raw("pool", "wl%d" % half, dst, src)
            hb.lw = tok
            hb.rd = {}
            for kind, ap in post:
                fw.op("pool", lambda E, ap=ap: E.memset(ap, 0.0), reads=[], writes=[hb])
            half = 1 - half
        if pending is not None:
            marks.append((pending_name, pe_count()))
            pending()
        pending = comp
        pending_name = getattr(sfn, "_name", "?")
    marks.append((pending_name, pe_count()))
    pending()
    marks.append(("end", pe_count()))
    fw.wait("sp", fw.deps([], [], extra=out_toks))
    fw.emit()
    st.close()
    return nc


_CACHE = {}


def _host_weights(inp):
    f = lambda k: np.ascontiguousarray(np.asarray(inp[k], dtype=np.float32))
    w_in = f("w_in")
    w = {}
    for k in ("ffn1_w_gu", "ffn2_w_gu", "ffn1_w_down", "ffn2_w_down", "mla_w_ukv", "w_read_a", "w_read_b", "w_read_c",
              "w_out", "ple_w_gate", "ple_w_proj", "lru_w_a", "lru_w_x"):
        w[k] = f(k)
    w["w_in"] = w_in
    kr = w_in[:, :, C_KR:C_KR + 32]
    kr_sw = np.concatenate([kr[:, :, 16:32], kr[:, :, 0:16]], axis=-1)
    w["w_attn"] = np.ascontiguousarray(np.concatenate(
        [w_in[:, :, C_CQ:C_CQ + 384], w_in[:, :, C_CKV:C_CKV + 256], w_in[:, :, C_CKV:C_CKV + 64], kr, kr_sw], axis=-1))
    uq = f("mla_w_uq").reshape(L, 384, NH, 96)
    uqx = np.concatenate([uq, uq[..., 80:96], uq[..., 64:80]], axis=-1)
    w["w_uqx"] = np.ascontiguousarray(uqx.reshape(L, 384, NH * 128))
    w["sgu_wT"] = np.ascontiguousarray(f("sgu_w_s").transpose(0, 3, 1, 2))
    w["sgu_b_s"] = np.ascontiguousarray(f("sgu_b_s").reshape(L, 1024))
    prm = np.zeros((PRM_ROWS, 128), np.float32)
    for l in range(L):
        for name, k in PRM_LAYOUT:
            r0 = l * PRM_PER_LAYER + PRM_OFF[name]
            prm[r0:r0 + k] = f(name)[l].reshape(k, 128)
    prm[PRM_FINAL:PRM_FINAL + 8] = f("final_norm").reshape(8, 128)
    w["prm"] = prm
    half = 16
    inv_freq = (10000.0 ** (-np.arange(half, dtype=np.float32) / half)).astype(np.float32)
    ang = np.arange(S, dtype=np.float32)[None, :] * inv_freq[:, None]
    cos, sin = np.cos(ang).astype(np.float32), np.sin(ang).astype(np.float32)
    w["rope"] = np.ascontiguousarray(np.concatenate([cos, cos, -sin, sin], axis=0))
    w["ident"] = np.eye(128, dtype=np.float32)
    return w


def kernel(**inputs):
    x = np.asarray(inputs["x"], dtype=np.float32)
    p = np.asarray(inputs["p"], dtype=np.float32)
    w = _host_weights(inputs)
    if "nc" not in _CACHE:
        _CACHE["nc"] = build()
    nc = _CACHE["nc"]
    in_maps = []
    for c in range(8):
        m = dict(w)
        m["x"] = np.ascontiguousarray(x[2 * c:2 * c + 2].reshape(2 * S, D))
        m["p"] = np.ascontiguousarray(p[:, 2 * c:2 * c + 2].reshape(L, 2 * S, PLE))
        in_maps.append(m)
    res = run_bass_kernel_spmd(nc, in_maps, core_ids=list(range(8)))
    out = np.stack([res.results[c]["out"].reshape(2, S, D) for c in range(8)], axis=0).reshape(16, S, D)
    return np.ascontiguousarray(out.astype(np.float32))
```

```python
from contextlib import ExitStack
import numpy as np
import concourse.bass as bass
import concourse.mybir as mybir
from concourse.bass_utils import run_bass_kernel_spmd

F32 = mybir.dt.float32
BF16 = mybir.dt.bfloat16
ALU = mybir.AluOpType
AF = mybir.ActivationFunctionType

D = 1024
S = 2048
L = 2
DFF = 2816
PLE = 256
TS = 1024
TT = 512
EPS = 1e-6
NH = 16
ENGS = ("pe", "act", "dve", "pool", "sp")
EPOCH = 16000
C_XA, C_GA, C_CQ, C_CKV, C_KR, C_ZU, C_ZV, C_GL = 0, 1024, 2048, 2432, 2688, 2720, 3744, 4768
PRM_LAYOUT = [("ffn1_norm", 8), ("mix_norm", 8), ("ffn2_norm", 8), ("ple_norm", 8), ("conv_w", 32),
              ("conv_b", 8), ("lru_b_a", 8), ("lru_b_x", 8), ("lru_lambda", 8), ("mla_q_norm", 3),
              ("mla_kv_norm", 2), ("sgu_norm_g", 8), ("sgu_norm_b", 8), ("gate_bias", 24)]
PRM_OFF = {}
_o = 0
for _n, _k in PRM_LAYOUT:
    PRM_OFF[_n] = _o
    _o += _k
PRM_PER_LAYER = _o
PRM_FINAL = L * PRM_PER_LAYER
PRM_ROWS = 384
FFN_GROUPS = [(0, 6), (6, 6), (12, 5), (17, 5)]


class Buf:
    __slots__ = ("lw", "rd")

    def __init__(self):
        self.lw = None
        self.rd = {}


class Fw:
    def __init__(self, nc, stack):
        self.nc = nc
        self.stack = stack
        self.q = {e: [] for e in ENGS}
        self.sems = {}
        self.cnt = {}
        self.seen = {e: {} for e in ENGS}
        self.cur = {}
        self.last = {}
        self.pending_sp = {}
        for e in ("pe", "act", "dve", "pool"):
            self._new_epoch(e)

    def new_sem(self, key):
        self.sems[key] = self.stack.enter_context(self.nc.semaphore(key.replace("#", "_")))
        self.cnt[key] = 0

    def _new_epoch(self, e):
        k = "%s#%d" % (e, sum(1 for x in self.sems if x.startswith(e + "#")))
        self.new_sem(k)
        self.cur[e] = k

    @staticmethod
    def _add(deps, t):
        if t is not None and deps.get(t[0], 0) < t[1]:
            deps[t[0]] = t[1]

    def deps(self, reads, writes, extra=()):
        deps = {}
        for b in reads:
            self._add(deps, b.lw)
        for b in writes:
            self._add(deps, b.lw)
            for k, v in b.rd.items():
                if deps.get(k, 0) < v:
                    deps[k] = v
        for t in extra:
            self._add(deps, t)
        return deps

    def wait(self, eng, deps):
        seen = self.seen[eng]
        for k, v in deps.items():
            if eng == "pe" and k.startswith("pe#"):
                continue
            if seen.get(k, 0) >= v:
                continue
            seen[k] = v
            self.q[eng].append(("w", k, v))

    def mark(self, tok, reads, writes):
        k, v = tok
        for b in reads:
            if b.rd.get(k, 0) < v:
                b.rd[k] = v
        for b in writes:
            b.lw = tok
            b.rd = {}

    def raw(self, eng, fn):
        if self.cnt[self.cur[eng]] >= EPOCH:
            self._new_epoch(eng)
        key = self.cur[eng]
        self.cnt[key] += 1
        self.q[eng].append(("o", fn, key, 1))
        self.last[eng] = (key, self.cnt[key])
        return (key, self.cnt[key])

    def op(self, eng, fn, reads=(), writes=(), extra=()):
        self.wait(eng, self.deps(reads, writes, extra))
        tok = self.raw(eng, fn)
        self.mark(tok, reads, writes)
        return tok

    def dma_raw(self, qeng, chan, out, in_):
        if chan not in self.sems:
            self.new_sem(chan)
        self.cnt[chan] += 16
        self.q[qeng].append(("o", lambda E: E.dma_start(out=out, in_=in_), chan, 16))
        return (chan, self.cnt[chan])

    def dma(self, qeng, chan, out, in_, reads=(), writes=(), extra=()):
        self.wait(qeng, self.deps(reads, writes, extra))
        tok = self.dma_raw(qeng, chan, out, in_)
        self.mark(tok, reads, writes)
        if qeng == "sp":
            self.pending_sp[tok[0]] = tok[1]
        return tok

    def barrier(self):
        toks = {}
        for e in ("pe", "act", "dve", "pool"):
            t = self.last.get(e)
            if t is not None:
                toks[t[0]] = t[1]
        toks.update(self.pending_sp)
        self.pending_sp = {}
        for e in ("pe", "act", "dve", "pool", "sp"):
            self.wait(e, toks)

    def emit(self):
        nc = self.nc
        with nc.Block() as block:
            def run(eng):
                def body(E):
                    sems = self.sems
                    for it in self.q[eng]:
                        if it[0] == "w":
                            E.wait_ge(sems[it[1]], it[2])
                        else:
                            it[1](E).then_inc(sems[it[2]], it[3])
                return body
            block.tensor(run("pe"))
            block.scalar(run("act"))
            block.vector(run("dve"))
            block.gpsimd(run("pool"))
            block.sync(run("sp"))


class Carver:
    def __init__(self, scr, base=0):
        self.scr = scr
        self.off = base

    def take(self, shape, dtype):
        n = 1
        for s in shape:
            n *= s
        nbytes = n * (4 if dtype == F32 else 2)
        nbytes = (nbytes + 31) // 32 * 32
        o = self.off
        self.off += nbytes
        v = self.scr[:, o // 4:(o + nbytes) // 4]
        if dtype != F32:
            v = v.bitcast(dtype)
        v = v[:, 0:n]
        if len(shape) == 2:
            v = v.rearrange("p (a b) -> p a b", a=shape[0])
        elif len(shape) == 3:
            v = v.rearrange("p (a b c) -> p a b c", a=shape[0], b=shape[1])
        return v


def build(n_seq=2, n_layers=2, phases=("ffn1", "B", "A", "C", "wout", "ffn2", "ple"), n_seg=2, dbg=()):
    nc = bass.Bass("TRN2", target_bir_lowering=False, dynamic_dma_scratch_size=8192)
    dbg_done = set()
    NT = n_seq * S
    dr = lambda n, s: nc.dram_tensor(n, s, F32, kind="ExternalInput").ap()
    x_d = dr("x", [NT, D])
    p_d = dr("p", [L, NT, PLE])
    out_d = nc.dram_tensor("out", [NT, D], F32, kind="ExternalOutput").ap()
    wgu_d = [dr("ffn1_w_gu", [L, D, 2 * DFF]), dr("ffn2_w_gu", [L, D, 2 * DFF])]
    wdn_d = [dr("ffn1_w_down", [L, DFF, D]), dr("ffn2_w_down", [L, DFF, D])]
    win_d = dr("w_in", [L, D, 7840])
    wattn_d = dr("w_attn", [L, D, 768])
    wuqx_d = dr("w_uqx", [L, 384, 2048])
    wukv_d = dr("mla_w_ukv", [L, 256, 2048])
    wra_d = dr("w_read_a", [L, D, D])
    wrb_d = dr("w_read_b", [L, D, D])
    wrc_d = dr("w_read_c", [L, D, D])
    wout_d = dr("w_out", [L, D, D])
    wpg_d = dr("ple_w_gate", [L, D, D])
    wpp_d = dr("ple_w_proj", [L, PLE, D])
    lwa_d = dr("lru_w_a", [L, 16, 64, 64])
    lwx_d = dr("lru_w_x", [L, 16, 64, 64])
    swT_d = dr("sgu_wT", [L, 128, 8, 128])
    sbs_d = dr("sgu_b_s", [L, 1024])
    prm_d = dr("prm", [PRM_ROWS, 128])
    rope_d = dr("rope", [64, S])
    ident_d = dr("ident", [128, 128])

    st = ExitStack()
    fw = Fw(nc, st)
    sb = lambda n, s, d: st.enter_context(nc.sbuf_tensor(n, s, d))
    WAW = 18432
    WA = [sb("wa%d" % i, [128, WAW], BF16) for i in range(2)]
    WAb = [Buf(), Buf()]
    X = sb("X", [128, 8, TS], F32)
    Xb = [[Buf() for _ in range(2)] for _ in range(8)]
    N = sb("N", [128, 8, TS], BF16)
    Nb = [[Buf() for _ in range(2)] for _ in range(8)]
    CKV = [sb("ckv%d" % l, [128, 2, TS], BF16) for l in range(L)] + [sb("ckvc", [128, 2, TS], BF16)]
    CKVb = [Buf() for _ in range(L + 1)]
    KR = [sb("kr%d" % l, [128, TS], BF16) for l in range(L)] + [sb("krc", [128, TS], BF16)]
    KRb = [Buf() for _ in range(L + 1)]
    PRM = sb("prm_sb", [128, PRM_ROWS], F32); PRMb = Buf()
    IDENT = sb("ident_sb", [128, 128], F32); IDENTb = Buf()
    ONES = sb("ones", [128, 128], BF16); ONESb = Buf()
    CST = sb("cst", [128, 4], F32); CSTb = Buf()
    CLAM = sb("clam", [128, L, 2, 8], F32); CLAMb = Buf()
    HALO = sb("halo", [128, L, 8, 3], F32); HALOb = [[Buf() for _ in range(8)] for _ in range(L)]
    CAR = sb("car", [128, L, 8], F32); CARb = [[Buf() for _ in range(8)] for _ in range(L)]
    SQ = [sb("sq%d" % i, [128, TT], BF16) for i in range(2)]; SQb = [Buf(), Buf()]
    STD = sb("std", [128, TT], F32); STDb = Buf()
    RSTD = [sb("rstd%d" % i, [128, TT], F32) for i in range(2)]; RSTDb = [Buf(), Buf()]
    SCRW = 17152
    SCR = sb("scr", [128, SCRW], F32)
    PSP = [st.enter_context(nc.psum_tensor("psp%d" % i, [128, 2 * TT], F32)) for i in range(2)]
    PS = [PSP[i // 2][:, (i % 2) * TT:(i % 2 + 1) * TT] for i in range(4)]
    PS += [st.enter_context(nc.psum_tensor("ps%d" % i, [128, TT], F32))[:, :] for i in range(4, 8)]
    PSb = [Buf() for _ in range(8)]
    rot = {"a": [0, 1, 2, 3], "b": [4, 5], "c": [6, 7]}
    rotp = {"a": 0, "b": 0, "c": 0}

    def bank(role):
        lst = rot[role]
        i = lst[rotp[role] % len(lst)]
        rotp[role] += 1
        return i

    def mm(bk, out_ap, pairs, reads):
        fw.wait("pe", fw.deps(reads, [PSb[bk]]))
        n = len(pairs)
        tok = None
        for i, (lh, rh) in enumerate(pairs):
            tok = fw.raw("pe", lambda E, lh=lh, rh=rh, i=i: E.matmul(out=out_ap, lhsT=lh, rhs=rh,
                                                                    start=(i == 0), stop=(i == n - 1)))
        fw.mark(tok, reads, [PSb[bk]])
        return tok

    def prm_col(l, name, c):
        j = l * PRM_PER_LAYER + PRM_OFF[name] + c
        return PRM[:, j:j + 1]

    def tsl(tt):
        return slice(tt * TT, (tt + 1) * TT)

    eps_ap = CST[:, 0:1]
    one_ap = CST[:, 1:2]

    fw.dma("sp", "ld_id", IDENT[:], ident_d, writes=[IDENTb])
    fw.op("dve", lambda E: E.memset(ONES[:], 1.0), writes=[ONESb])
    fw.op("dve", lambda E: E.memset(CST[:, 0:1], EPS), writes=[CSTb])
    fw.op("dve", lambda E: E.memset(CST[:, 1:2], 1.0), writes=[CSTb])
    PTMP = SCR[:, 0:PRM_ROWS].rearrange("p (a b) -> p a b", a=3)
    PTMPb = Buf()
    fw.dma("sp", "ld_prm", PTMP, prm_d.rearrange("(a p) f -> p a f", p=128), writes=[PTMPb])
    bk = bank("a")
    for a in range(3):
        fw.op("pe", lambda E, a=a: E.transpose(out=PS[bk][:, a * 128:(a + 1) * 128], in_=PTMP[:, a, :],
                                               identity=IDENT[:]), reads=[PTMPb, IDENTb], writes=[PSb[bk]])
    fw.op("dve", lambda E: E.tensor_copy(out=PRM[:], in_=PS[bk][:, 0:PRM_ROWS]), reads=[PSb[bk]], writes=[PRMb])
    for l in range(n_layers):
        j = l * PRM_PER_LAYER + PRM_OFF["lru_lambda"]
        fw.op("act", lambda E, l=l, j=j: E.activation(out=CLAM[:, l, 0, :], in_=PRM[:, j:j + 8], func=AF.Exp, scale=-1.0),
              reads=[PRMb], writes=[CLAMb])
        fw.op("act", lambda E, l=l: E.activation(out=CLAM[:, l, 0, :], in_=CLAM[:, l, 0, :], func=AF.Ln, bias=one_ap),
              reads=[CLAMb, CSTb], writes=[CLAMb])
        fw.op("dve", lambda E, l=l: E.tensor_scalar(out=CLAM[:, l, 1, :], in0=CLAM[:, l, 0, :], scalar1=-16.0, scalar2=None,
                                                    op0=ALU.mult), reads=[CLAMb], writes=[CLAMb])
        fw.op("dve", lambda E, l=l: E.tensor_scalar(out=CLAM[:, l, 0, :], in0=CLAM[:, l, 0, :], scalar1=-8.0, scalar2=None,
                                                    op0=ALU.mult), reads=[CLAMb], writes=[CLAMb])

    NEGB = sb("negb", [128, L, 2, 8], F32); NEGBb = Buf()
    for l in range(n_layers):
        for jj, nm in enumerate(("lru_b_a", "lru_b_x")):
            j = l * PRM_PER_LAYER + PRM_OFF[nm]
            fw.op("dve", lambda E, l=l, jj=jj, j=j: E.tensor_scalar(out=NEGB[:, l, jj, :], in0=PRM[:, j:j + 8], scalar1=-1.0, scalar2=None,
                                                                    op0=ALU.mult), reads=[PRMb], writes=[NEGBb])

    rk = [0]

    def rmsnorm(src_aps, src_bufs, gain_aps, out_aps, out_bufs, dn, n=TT):
        nch = len(src_aps)
        bk = bank("c")
        fw.wait("pe", fw.deps([], [PSb[bk]]))
        tok = None
        for c in range(nch):
            k = c % 2
            fw.op("act", lambda E, c=c, k=k: E.activation(out=SQ[k][:, 0:n], in_=src_aps[c], func=AF.Square),
                  reads=[src_bufs[c]], writes=[SQb[k]])
            fw.wait("pe", fw.deps([SQb[k], ONESb], []))
            tok = fw.raw("pe", lambda E, c=c, k=k: E.matmul(out=PS[bk][:, 0:n], lhsT=ONES[:], rhs=SQ[k][:, 0:n],
                                                           start=(c == 0), stop=(c == nch - 1)))
            fw.mark(tok, [SQb[k], ONESb], [])
        fw.mark(tok, [], [PSb[bk]])
        fw.op("act", lambda E: E.activation(out=STD[:, 0:n], in_=PS[bk][:, 0:n], func=AF.Sqrt, scale=1.0 / dn, bias=eps_ap),
              reads=[PSb[bk], CSTb], writes=[STDb])
        r = rk[0] % 2
        rk[0] += 1
        fw.op("dve", lambda E: E.reciprocal(out=RSTD[r][:, 0:n], in_=STD[:, 0:n]), reads=[STDb], writes=[RSTDb[r]])
        for c in range(nch):
            fw.op("dve", lambda E, c=c: E.scalar_tensor_tensor(out=out_aps[c], in0=src_aps[c], scalar=gain_aps[c],
                                                               in1=RSTD[r][:, 0:n], op0=ALU.mult, op1=ALU.mult),
                  reads=[src_bufs[c], RSTDb[r], PRMb], writes=[out_bufs[c]])

    def norm_x(gname, l):
        for tt in range(2):
            rmsnorm([X[:, c, tsl(tt)] for c in range(8)], [Xb[c][tt] for c in range(8)],
                    [prm_col(l, gname, c) for c in range(8)],
                    [N[:, c, tsl(tt)] for c in range(8)], [Nb[c][tt] for c in range(8)], D)

    def wview(W, off, shape):
        n = 1
        for s_ in shape:
            n *= s_
        v = W[:, off:off + n]
        if len(shape) == 2:
            v = v.rearrange("p (a b) -> p a b", a=shape[0])
        elif len(shape) == 3:
            v = v.rearrange("p (a b c) -> p a b c", a=shape[0], b=shape[1])
        return v, off + n

    def kview(src2d):
        return src2d.rearrange("(c p) f -> p c f", p=128)

    stages = []

    def st_load_x(sq, seg):
        def f(W, Wb, Wo, Wob):
            def comp():
                fw.barrier()
                cv = Carver(SCR)
                XT = [cv.take([D], F32) for _ in range(2)]
                XTb = [Buf(), Buf()]
                for l in range(n_layers):
                    if seg == 0:
                        fw.op("dve", lambda E, l=l: E.memset(HALO[:, l, :, :], 0.0), writes=HALOb[l])
                        fw.op("dve", lambda E, l=l: E.memset(CAR[:, l, :], 0.0), writes=CARb[l])
                for tb in range(8):
                    k = tb % 2
                    r0 = sq * S + seg * TS + tb * 128
                    fw.dma("sp", "xl%d" % k, XT[k], x_d[r0:r0 + 128, :], writes=[XTb[k]])
                    for hh in range(2):
                        bk = bank("a")
                        fw.wait("pe", fw.deps([XTb[k], IDENTb], [PSb[bk]]))
                        tok = None
                        for c4 in range(4):
                            c = hh * 4 + c4
                            tok = fw.raw("pe", lambda E, c=c, c4=c4, k=k, bk=bk: E.transpose(
                                out=PS[bk][:, c4 * 128:(c4 + 1) * 128], in_=XT[k][:, c * 128:(c + 1) * 128], identity=IDENT[:]))
                        fw.mark(tok, [XTb[k], IDENTb], [PSb[bk]])
                        tt = tb // 4
                        dst = X[:, hh * 4:hh * 4 + 4, tb * 128:(tb + 1) * 128]
                        src = PS[bk][:, :].rearrange("p (a b) -> p a b", a=4)
                        wr = [Xb[c][tt] for c in range(hh * 4, hh * 4 + 4)]
                        if hh == 0:
                            fw.op("act", lambda E, dst=dst, src=src: E.activation(out=dst, in_=src, func=AF.Copy),
                                  reads=[PSb[bk]], writes=wr)
                        else:
                            fw.op("dve", lambda E, dst=dst, src=src: E.tensor_copy(out=dst, in_=src),
                                  reads=[PSb[bk]], writes=wr)
            return [], comp
        f._name = "load_x"
        return f

    out_toks = []

    def tap(name, ap, bufs):
        if name not in dbg or name in dbg_done:
            return
        dbg_done.add(name)
        dd = nc.dram_tensor("dbg_" + name, list(ap.shape), ap.dtype, kind="ExternalOutput").ap()
        out_toks.append(fw.dma("sp", "dbg", dd, ap, reads=bufs))

    def st_store(sq, seg):
        def f(W, Wb, Wo, Wob):
            def comp():
                fw.barrier()
                cv = Carver(SCR)
                Y = cv.take([8, TT], F32); Yb = [Buf() for _ in range(8)]
                OTK = [cv.take([D], F32) for _ in range(2)]; OTKb = [Buf(), Buf()]
                for tt in range(2):
                    rmsnorm([X[:, c, tsl(tt)] for c in range(8)], [Xb[c][tt] for c in range(8)],
                            [PRM[:, PRM_FINAL + c:PRM_FINAL + c + 1] for c in range(8)],
                            [Y[:, c, :] for c in range(8)], Yb, D)
                    for blk in range(4):
                        k = blk % 2
                        for hh in range(2):
                            bk = bank("a")
                            rd = [Yb[c] for c in range(hh * 4, hh * 4 + 4)] + [IDENTb]
                            fw.wait("pe", fw.deps(rd, [PSb[bk]]))
                            tok = None
                            for c4 in range(4):
                                c = hh * 4 + c4
                                tok = fw.raw("pe", lambda E, c=c, c4=c4, bk=bk, blk=blk: E.transpose(
                                    out=PS[bk][:, c4 * 128:(c4 + 1) * 128], in_=Y[:, c, blk * 128:(blk + 1) * 128], identity=IDENT[:]))
                            fw.mark(tok, rd, [PSb[bk]])
                            dst = OTK[k][:, hh * 512:(hh + 1) * 512]
                            if hh == 0:
                                fw.op("act", lambda E, dst=dst, bk=bk: E.activation(out=dst, in_=PS[bk][:, :], func=AF.Copy),
                                      reads=[PSb[bk]], writes=[OTKb[k]])
                            else:
                                fw.op("dve", lambda E, dst=dst, bk=bk: E.tensor_copy(out=dst, in_=PS[bk][:, :]),
                                      reads=[PSb[bk]], writes=[OTKb[k]])
                        r0 = sq * S + seg * TS + tt * TT + blk * 128
                        out_toks.append(fw.dma("sp", "st%d" % k, out_d[r0:r0 + 128, :], OTK[k], reads=[OTKb[k]]))
            return [], comp
        f._name = "store"
        return f

    def st_ffn(l, which, gi):
        f0, nf = FFN_GROUPS[gi]
        gname = "ffn1_norm" if which == 0 else "ffn2_norm"

        def f(W, Wb, Wo, Wob):
            wg, o = wview(W, 0, [8, nf * 128])
            wu, o = wview(W, o, [8, nf * 128])
            wd, o = wview(W, o, [nf, D])
            dmas = [(wg, kview(wgu_d[which][l, :, f0 * 128:(f0 + nf) * 128])),
                    (wu, kview(wgu_d[which][l, :, DFF + f0 * 128:DFF + (f0 + nf) * 128])),
                    (wd, wdn_d[which][l, f0 * 128:(f0 + nf) * 128, :].rearrange("(f p) d -> p f d", p=128))]

            def comp():
                cv = Carver(SCR)
                H = cv.take([2, 6, TT], BF16)
                SG = [cv.take([TT], F32) for _ in range(2)]
                if gi == 0:
                    norm_x(gname, l)
                    fw.barrier()
                    st_ffn.Hb = [[Buf() for _ in range(6)] for _ in range(2)]
                    st_ffn.SGb = [Buf(), Buf()]
                Hb, SGb = st_ffn.Hb, st_ffn.SGb
                k = 0
                for tt in range(2):
                    nrd = [Nb[c][tt] for c in range(8)] + [Wb]
                    for fi in range(nf):
                        bg, bu = bank("a"), bank("a")
                        mm(bg, PS[bg][:, :], [(wg[:, c, fi * 128:(fi + 1) * 128], N[:, c, tsl(tt)]) for c in range(8)], nrd)
                        mm(bu, PS[bu][:, :], [(wu[:, c, fi * 128:(fi + 1) * 128], N[:, c, tsl(tt)]) for c in range(8)], nrd)
                        kk = k % 2
                        k += 1
                        fw.op("act", lambda E, kk=kk, bg=bg: E.activation(out=SG[kk], in_=PS[bg][:, :], func=AF.Silu),
                              reads=[PSb[bg]], writes=[SGb[kk]])
                        fw.op("dve", lambda E, kk=kk, bu=bu, tt=tt, fi=fi: E.tensor_tensor(
                            out=H[:, tt, fi, :], in0=SG[kk], in1=PS[bu][:, :], op=ALU.mult),
                            reads=[SGb[kk], PSb[bu]], writes=[Hb[tt][fi]])
                for tt in range(2):
                    hrd = [Hb[tt][fi] for fi in range(nf)] + [Wb]
                    for d in range(8):
                        bd = bank("b")
                        mm(bd, PS[bd][:, :], [(wd[:, fi, d * 128:(d + 1) * 128], H[:, tt, fi, :]) for fi in range(nf)], hrd)
                        fw.op("dve", lambda E, d=d, tt=tt, bd=bd: E.scalar_tensor_tensor(
                            out=X[:, d, tsl(tt)], in0=PS[bd][:, :], scalar=0.5, in1=X[:, d, tsl(tt)],
                            op0=ALU.mult, op1=ALU.add), reads=[PSb[bd], Xb[d][tt]], writes=[Xb[d][tt]])
            return dmas, comp
        f._name = "ffn"
        return f

    MGcv = Carver(SCR)
    MG = MGcv.take([8, TS], BF16)
    MGb = [[Buf() for _ in range(2)] for _ in range(8)]
    MIX_BASE = MGcv.off
    mix_first = [True]

    def gate_merge(l, br, tt, yw, Y, Yb, glw, Wbs, TMPG, TMPGb, GT, GTb):
        first = mix_first[0]
        for d in range(8):
            by, bgl = bank("a"), bank("a")
            mm(by, PS[by][:, :], [(yw[:, c, d * 128:(d + 1) * 128], Y[:, c, :]) for c in range(8)], Yb + Wbs)
            mm(bgl, PS[bgl][:, :], [(glw[:, c, d * 128:(d + 1) * 128], N[:, c, tsl(tt)]) for c in range(8)],
               [Nb[c][tt] for c in range(8)] + Wbs)
            k = d % 2
            fw.op("act", lambda E, k=k, bgl=bgl, d=d: E.activation(out=GT[k], in_=PS[bgl][:, :], func=AF.Sigmoid,
                                                                   bias=prm_col(l, "gate_bias", br * 8 + d)),
                  reads=[PSb[bgl], PRMb], writes=[GTb[k]])
            if first:
                fw.op("dve", lambda E, k=k, by=by, d=d: E.tensor_tensor(out=MG[:, d, tsl(tt)], in0=GT[k], in1=PS[by][:, :], op=ALU.mult),
                      reads=[GTb[k], PSb[by]], writes=[MGb[d][tt]])
            else:
                fw.op("dve", lambda E, k=k, by=by: E.tensor_tensor(out=TMPG[k], in0=GT[k], in1=PS[by][:, :], op=ALU.mult),
                      reads=[GTb[k], PSb[by]], writes=[TMPGb[k]])
                fw.op("pool", lambda E, k=k, d=d: E.tensor_tensor(out=MG[:, d, tsl(tt)], in0=MG[:, d, tsl(tt)], in1=TMPG[k], op=ALU.add),
                      reads=[TMPGb[k], MGb[d][tt]], writes=[MGb[d][tt]])

    def st_B1(l, sq, seg):
        def f(W, Wb, Wo, Wob):
            wat, o = wview(W, 0, [8, 768])
            wuq, o = wview(W, o, [3, 2048])
            wkv, o = wview(W, o, [2, 2048])
            dmas = [(wat, kview(wattn_d[l])), (wuq, kview(wuqx_d[l])), (wkv, kview(wukv_d[l]))]
            wrb, o2 = wview(Wo, 0, [8, D])
            wgl, o2 = wview(Wo, o2, [8, D])

            def comp():
                norm_x("mix_norm", l)
                fw.barrier()
                cv = Carver(SCR, MIX_BASE)
                CQN = cv.take([3, TS], BF16); CQNb = [[Buf(), Buf()] for _ in range(3)]
                Q = [cv.take([TT], BF16) for _ in range(2)]; Qb = [Buf(), Buf()]
                K = [cv.take([S], BF16) for _ in range(2)]; Kb = [Buf(), Buf()]
                VE = cv.take([16, 2, 128], BF16); VEb = [Buf(), Buf()]
                PT = [cv.take([2 * TT], BF16) for _ in range(2)]; PTb = [Buf() for _ in range(2)]
                PTD = [cv.take([TT], BF16) for _ in range(2)]; PTDb = [Buf(), Buf()]
                OT = cv.take([8, TT], BF16); OTb = [Buf() for _ in range(8)]
                TAB = cv.take([TS], F32); TABb = Buf()
                T1 = cv.take([TT], F32); T1b = Buf()
                T2 = cv.take([TT], F32); T2b = Buf()
                TQ1 = [T1, T1]; TQ1b = [T1b, T1b]
                TQ2 = [T2, T2]; TQ2b = [T2b, T2b]
                RC = cv.take([TT], F32); RCb = Buf()
                assert cv.off <= SCRW * 4, cv.off
                cdst = l if seg == 0 else L
                scale = float((64 + 32) ** -0.5)
                fw.dma("sp", "ld_tab", TAB[64:128, :], rope_d[:, seg * TS:(seg + 1) * TS], writes=[TABb])
                fw.op("pool", lambda E: E.memset(VE[:, :, :, 64:128], 1.0), writes=VEb)
                for k in range(2):
                    fw.op("pool", lambda E, k=k: E.memset(Q[k][64:128, :], 0.0), writes=[Qb[k]])
                    fw.op("pool", lambda E, k=k: E.memset(K[k][64:128, :], 0.0), writes=[Kb[k]])
                for k in range(2):
                    fw.op("pool", lambda E, k=k: E.memset(PTD[k][64:128, 0:64], 0.0), writes=[PTDb[k]])
                for tt in range(2):
                    nrd = [Nb[c][tt] for c in range(8)] + [Wb]
                    bks = [bank("a") for _ in range(3)]
                    for j in range(3):
                        mm(bks[j], PS[bks[j]][:, :], [(wat[:, c, j * 128:(j + 1) * 128], N[:, c, tsl(tt)]) for c in range(8)], nrd)
                    rmsnorm([PS[bks[j]][:, :] for j in range(3)], [PSb[bks[j]] for j in range(3)],
                            [prm_col(l, "mla_q_norm", j) for j in range(3)],
                            [CQN[:, j, tsl(tt)] for j in range(3)], [CQNb[j][tt] for j in range(3)], 384)
                    bks = [bank("a") for _ in range(2)]
                    for j in range(2):
                        mm(bks[j], PS[bks[j]][:, :], [(wat[:, c, 384 + j * 128:384 + (j + 1) * 128], N[:, c, tsl(tt)]) for c in range(8)], nrd)
                    rmsnorm([PS[bks[j]][:, :] for j in range(2)], [PSb[bks[j]] for j in range(2)],
                            [prm_col(l, "mla_kv_norm", j) for j in range(2)],
                            [CKV[cdst][:, j, tsl(tt)] for j in range(2)], [CKVb[cdst], CKVb[cdst]], 256)
                    bkr = bank("a")
                    mm(bkr, PS[bkr][:, :], [(wat[:, c, 640:768], N[:, c, tsl(tt)]) for c in range(8)], nrd)
                    fw.op("dve", lambda E, bkr=bkr, tt=tt: E.tensor_tensor(out=T1[64:96, :], in0=PS[bkr][64:96, :], in1=TAB[64:96, tsl(tt)], op=ALU.mult),
                          reads=[PSb[bkr], TABb], writes=[T1b])
                    fw.op("dve", lambda E, bkr=bkr, tt=tt: E.tensor_tensor(out=T2[64:96, :], in0=PS[bkr][96:128, :], in1=TAB[96:128, tsl(tt)], op=ALU.mult),
                          reads=[PSb[bkr], TABb], writes=[T2b])
                    fw.op("pool", lambda E, tt=tt: E.tensor_tensor(out=KR[cdst][64:96, tsl(tt)], in0=T1[64:96, :], in1=T2[64:96, :], op=ALU.add),
                          reads=[T1b, T2b], writes=[KRb[cdst]])
                tap("cqn", CQN, [b for bb in CQNb for b in bb])
                tap("ckv", CKV[cdst][:], [CKVb[cdst]])
                tap("kr", KR[cdst][64:96, :], [KRb[cdst]])
                ksrc = [l] if seg == 0 else [l, L]
                nhalf = len(ksrc)
                for kb_ in range(2):
                    for hf in range(nhalf):
                        fw.op("act", lambda E, kb_=kb_, hf=hf: E.activation(out=K[kb_][64:96, hf * TS:(hf + 1) * TS],
                                                                            in_=KR[ksrc[hf]][64:96, :], func=AF.Copy),
                              reads=[KRb[ksrc[hf]]], writes=[Kb[kb_]])

                def lat(kc, k0, n):
                    hf = k0 // TS
                    return CKV[ksrc[hf]][:, kc, k0 - hf * TS:k0 - hf * TS + n], CKVb[ksrc[hf]]

                ptc = [0]
                ptdc = [0]
                pairc = [0]
                order = [(tt, h) for tt in range(2) for h in range(NH)]

                def produce(i):
                    pieces = []
                    tt, h = order[i]
                    kk = i % 2
                    gq = seg * 2 + tt
                    kend = (gq + 1) * TT
                    nkb = kend // 128
                    lrd = [CKVb[j] for j in ksrc] + [Wb]
                    def piece_v(kb0):
                        nb8 = min(8, nkb - kb0)
                        bv = bank("c")
                        fw.wait("pe", fw.deps(lrd, [PSb[bv]]))
                        tok = None
                        for q8 in range(nb8):
                            kb = kb0 + q8
                            for kc in range(2):
                                la, lab = lat(kc, kb * 128, 128)
                                tok = fw.raw("pe", lambda E, la=la, kc=kc, q8=q8, bv=bv, h=h: E.matmul(
                                    out=PS[bv][:, q8 * 64:(q8 + 1) * 64], lhsT=la, rhs=wkv[:, kc, h * 128 + 64:(h + 1) * 128],
                                    start=(kc == 0), stop=(kc == 1)))
                        fw.mark(tok, lrd, [PSb[bv]])
                        fw.op("dve", lambda E, kb0=kb0, nb8=nb8, bv=bv, kk=kk: E.tensor_copy(
                            out=VE[:, kb0:kb0 + nb8, kk, 0:64],
                            in_=PS[bv][:, 0:nb8 * 64].rearrange("p (a e) -> p a e", a=nb8)), reads=[PSb[bv]], writes=[VEb[kk]])
                    for kb0 in range(0, nkb, 8):
                        pieces.append(lambda kb0=kb0: piece_v(kb0))

                    def piece_k(kt):
                        bkk = bank("c")
                        pairs = []
                        for kc in range(2):
                            la, lab = lat(kc, kt * TT, TT)
                            pairs.append((wkv[:, kc, h * 128:(h + 1) * 128], la))
                        mm(bkk, PS[bkk][:, :], pairs, lrd)
                        fw.op("dve", lambda E, kk=kk, kt=kt, bkk=bkk: E.tensor_copy(out=K[kk][0:64, kt * TT:(kt + 1) * TT],
                                                                               in_=PS[bkk][0:64, :]),
                              reads=[PSb[bkk]], writes=[Kb[kk]])
                    for kt in range(kend // TT):
                        pieces.append(lambda kt=kt: piece_k(kt))

                    def piece_q():
                        bq = bank("c")
                        mm(bq, PS[bq][:, :], [(wuq[:, kc, h * 128:(h + 1) * 128], CQN[:, kc, tsl(tt)]) for kc in range(3)],
                           [CQNb[kc][tt] for kc in range(3)] + [Wb])
                        piece_q2(bq)

                    def piece_q2(bq):
                        fw.op("dve", lambda E, kk=kk, bq=bq: E.tensor_copy(out=Q[kk][0:64, :], in_=PS[bq][0:64, :]),
                              reads=[PSb[bq]], writes=[Qb[kk]])
                        fw.op("dve", lambda E, bq=bq, tt=tt, kk=kk: E.tensor_tensor(out=TQ1[kk][64:96, :], in0=PS[bq][64:96, :], in1=TAB[64:96, tsl(tt)], op=ALU.mult),
                              reads=[PSb[bq], TABb], writes=[TQ1b[kk]])
                        fw.op("dve", lambda E, bq=bq, tt=tt, kk=kk: E.tensor_tensor(out=TQ2[kk][64:96, :], in0=PS[bq][96:128, :], in1=TAB[96:128, tsl(tt)], op=ALU.mult),
                              reads=[PSb[bq], TABb], writes=[TQ2b[kk]])
                        fw.op("pool", lambda E, kk=kk: E.tensor_tensor(out=Q[kk][64:96, :], in0=TQ1[kk][64:96, :], in1=TQ2[kk][64:96, :], op=ALU.add),
                              reads=[TQ1b[kk], TQ2b[kk]], writes=[Qb[kk]])
                    pieces.append(piece_q)
                    return pieces

                def attend(i, pieces, prev_norm):
                    tt, h = order[i]
                    kk = i % 2
                    gq = seg * 2 + tt
                    nkb = (gq + 1) * TT // 128
                    hp, hh = h // 2, h % 2
                    bo = bank("b")

                    def issue_pv(item, last):
                        kb, q_lo, pt, ptb, first = item
                        fw.wait("pe", fw.deps([ptb, VEb[kk]], [PSb[bo]] if first else []))
                        tok = fw.raw("pe", lambda E, kb=kb, q_lo=q_lo, pt=pt, first=first, last=last: E.matmul(
                            out=PS[bo][:, q_lo:TT], lhsT=VE[:, kb, kk, :], rhs=pt, start=first, stop=last))
                        fw.mark(tok, [ptb, VEb[kk]], [PSb[bo]])

                    nd = gq * 4
                    units = [("pair", j) for j in range(0, nd, 2)] + [("diag", kb) for kb in range(nd, nkb)]
                    prev = None
                    for u in units:
                        if u[0] == "pair":
                            kb = u[1]
                            p = pairc[0] % 2
                            pairc[0] += 1
                            for j in range(2):
                                mm(2 * p + j, PS[2 * p + j][:, :], [(K[kk][:, (kb + j) * 128:(kb + j + 1) * 128], Q[kk][:, :])], [Kb[kk], Qb[kk]])
                            pi = ptc[0] % 2
                            ptc[0] += 1
                            pt, ptb = PT[pi], PTb[pi]
                            fw.op("act", lambda E, pt=pt, p=p: E.activation(out=pt[:, :], in_=PSP[p][:, :], func=AF.Exp, scale=scale),
                                  reads=[PSb[2 * p], PSb[2 * p + 1]], writes=[ptb])
                            items = [(kb, 0, pt[:, 0:TT], ptb, kb == 0), (kb + 1, 0, pt[:, TT:2 * TT], ptb, False)]
                        else:
                            kb = u[1]
                            q_lo = kb * 128 - gq * TT
                            nq = TT - q_lo
                            bs = bank("a")
                            mm(bs, PS[bs][:, 0:nq], [(K[kk][:, kb * 128:(kb + 1) * 128], Q[kk][:, q_lo:TT])], [Kb[kk], Qb[kk]])
                            j = ptdc[0] % 2
                            ptdc[0] += 1
                            pt, ptb = PTD[j], PTDb[j]
                            fw.op("act", lambda E, pt=pt, bs=bs, nq=nq: E.activation(out=pt[0:64, 0:nq], in_=PS[bs][0:64, 0:nq], func=AF.Exp, scale=scale),
                                  reads=[PSb[bs]], writes=[ptb])
                            if nq > 64:
                                fw.op("act", lambda E, pt=pt, bs=bs, nq=nq: E.activation(out=pt[64:128, 64:nq], in_=PS[bs][64:128, 64:nq], func=AF.Exp, scale=scale),
                                      reads=[PSb[bs]], writes=[ptb])
                            items = [(kb, q_lo, pt[:, 0:nq], ptb, kb == 0)]
                        if pieces:
                            pieces.pop(0)()
                        if prev is not None:
                            for it in prev:
                                issue_pv(it, False)
                            if prev_norm is not None:
                                prev_norm()
                                prev_norm = None
                        prev = items
                    while pieces:
                        pieces.pop(0)()
                    if prev_norm is not None:
                        prev_norm()
                    for idx, it in enumerate(prev):
                        issue_pv(it, idx == len(prev) - 1)
                    def norm_fn():
                        fw.op("act", lambda E: E.activation(out=RC[64:128, :], in_=PS[bo][64:128, :], func=AF.Ln),
                              reads=[PSb[bo]], writes=[RCb])
                        fw.op("act", lambda E: E.activation(out=RC[64:128, :], in_=RC[64:128, :], func=AF.Exp, scale=-1.0),
                              reads=[RCb], writes=[RCb])
                        fw.op("dve", lambda E: E.tensor_copy(out=RC[0:64, :], in_=RC[64:128, :]), reads=[RCb], writes=[RCb])
                        fw.op("dve", lambda E: E.tensor_tensor(out=OT[hh * 64:(hh + 1) * 64, hp, :], in0=PS[bo][0:64, :],
                                                               in1=RC[0:64, :], op=ALU.mult),
                              reads=[PSb[bo], RCb], writes=[OTb[hp]])
                    return norm_fn

                for pc in produce(0):
                    pc()
                pnorm = None
                for i in range(len(order)):
                    pnorm = attend(i, produce(i + 1) if i + 1 < len(order) else [], pnorm)
                    tt, h = order[i]
                    if h == NH - 1:
                        pnorm()
                        pnorm = None
                        assert mix_first[0]
                        gate_merge(l, 1, tt, wrb, OT, OTb, wgl, [Wob], None, None, [T1, T2], [T1b, T2b])
                mix_first[0] = False
            return dmas, comp
        f._name = "B1"
        return f

    def st_B2(l):
        def f(W, Wb, Wo, Wob):
            wrb, o = wview(W, 0, [8, D])
            wgl, o = wview(W, o, [8, D])
            dmas = [(wrb, kview(wrb_d[l])), (wgl, kview(win_d[l, :, C_GL + D:C_GL + 2 * D]))]
            return dmas, (lambda: None)
        f._name = "B2"
        return f

    def st_A1(l, sq, seg):
        def f(W, Wb, Wo, Wob):
            wxa, o = wview(W, 0, [8, D])
            wga, o = wview(W, o, [8, D])
            lwa, o = wview(W, o, [8, 128])
            lwx, o = wview(W, o, [8, 128])
            dmas = [(wxa, kview(win_d[l, :, C_XA:C_XA + D])), (wga, kview(win_d[l, :, C_GA:C_GA + D]))]
            for j in range(2):
                dmas.append((lwa[64 * j:64 * j + 64, :, 64 * j:64 * j + 64],
                             lwa_d[l].rearrange("(n two) c d -> two c n d", two=2)[j]))
                dmas.append((lwx[64 * j:64 * j + 64, :, 64 * j:64 * j + 64],
                             lwx_d[l].rearrange("(n two) c d -> two c n d", two=2)[j]))
            wra, o2 = wview(Wo, 0, [8, D])
            wgl, o2 = wview(Wo, o2, [8, D])
            pre = [("z", lwa[0:64, :, 64:128]), ("z", lwa[64:128, :, 0:64]), ("z", lwx[0:64, :, 64:128]), ("z", lwx[64:128, :, 0:64])]

            def comp():
                fw.barrier()
                cv = Carver(SCR, MIX_BASE)
                NS = 3
                XA = [cv.take([TT + 3], F32) for _ in range(NS)]; XAb = [Buf() for _ in range(NS)]
                XC = [cv.take([TT], F32) for _ in range(NS)]; XCb = [Buf() for _ in range(NS)]
                XCH = [cv.take([TT], BF16) for _ in range(NS)]; XCHb = [Buf() for _ in range(NS)]
                RI = [cv.take([2, TT], F32) for _ in range(NS)]; RIb = [Buf() for _ in range(NS)]
                T = [cv.take([TT], F32) for _ in range(NS)]; Tb = [Buf() for _ in range(NS)]
                HH, HHb = T, Tb
                GG8 = cv.take([8, TT], BF16); GG8b = [Buf() for _ in range(8)]
                AT = cv.take([8, TT], BF16); ATb = [Buf() for _ in range(8)]
                GT, GTb = XC[0:2], XCb[0:2]
                TM, TMb = [RI[0][:, 0, :], RI[0][:, 1, :]], [RIb[0], RIb[0]]
                slot = [0]
                assert cv.off <= SCRW * 4, cv.off
                if "B" not in phases:
                    norm_x("mix_norm", l)
                for tt in range(2):
                    nrd = [Nb[c][tt] for c in range(8)] + [Wb]
                    for c in range(8):
                        bg = bank("a")
                        mm(bg, PS[bg][:, :], [(wga[:, kc, c * 128:(c + 1) * 128], N[:, kc, tsl(tt)]) for kc in range(8)], nrd)
                        fw.op("act", lambda E, c=c, bg=bg: E.activation(out=GG8[:, c, :], in_=PS[bg][:, :], func=AF.Gelu_apprx_tanh),
                              reads=[PSb[bg]], writes=[GG8b[c]])
                    base = slot[0]
                    slot[0] += 8

                    def stage_a(c):
                        k = (base + c) % NS
                        bx = bank("a")
                        mm(bx, PS[bx][:, :], [(wxa[:, kc, c * 128:(c + 1) * 128], N[:, kc, tsl(tt)]) for kc in range(8)], nrd)
                        fw.op("pool", lambda E, k=k, c=c: E.tensor_copy(out=XA[k][:, 0:3], in_=HALO[:, l, c, :]),
                              reads=[HALOb[l][c]], writes=[XAb[k]])
                        fw.op("dve", lambda E, k=k, bx=bx: E.tensor_copy(out=XA[k][:, 3:TT + 3], in_=PS[bx][:, :]),
                              reads=[PSb[bx]], writes=[XAb[k]])
                        fw.op("pool", lambda E, k=k, c=c: E.tensor_copy(out=HALO[:, l, c, :], in_=XA[k][:, TT:TT + 3]),
                              reads=[XAb[k]], writes=[HALOb[l][c]])
                        cw = lambda tap, c=c: prm_col(l, "conv_w", tap * 8 + c)
                        fw.op("dve", lambda E, k=k, c=c, cw=cw: E.tensor_scalar(out=XC[k], in0=XA[k][:, 0:TT], scalar1=cw(0),
                                                                             scalar2=prm_col(l, "conv_b", c), op0=ALU.mult, op1=ALU.add),
                              reads=[XAb[k], PRMb], writes=[XCb[k]])
                        for tap in range(1, 4):
                            fw.op("dve", lambda E, k=k, tap=tap, cw=cw: E.scalar_tensor_tensor(
                                out=XC[k], in0=XA[k][:, tap:tap + TT], scalar=cw(tap), in1=XC[k], op0=ALU.mult, op1=ALU.add),
                                reads=[XAb[k], XCb[k], PRMb], writes=[XCb[k]])
                        fw.op("dve", lambda E, k=k: E.tensor_copy(out=XCH[k], in_=XC[k]), reads=[XCb[k]], writes=[XCHb[k]])

                    def stage_b(c):
                        k = (base + c) % NS
                        R = RI[k][:, 0, :]
                        I = RI[k][:, 1, :]
                        RI2 = RI[k][:, :, :].rearrange("p a t -> p (a t)")
                        br_, bi_ = bank("c"), bank("c")
                        mm(br_, PS[br_][:, :], [(lwa[:, c, :], XCH[k])], [XCHb[k], Wb])
                        mm(bi_, PS[bi_][:, :], [(lwx[:, c, :], XCH[k])], [XCHb[k], Wb])
                        fw.op("act", lambda E, R=R, br_=br_, c=c: E.activation(out=R, in_=PS[br_][:, :], func=AF.Exp, scale=-1.0, bias=NEGB[:, l, 0, c:c + 1]),
                              reads=[PSb[br_], NEGBb], writes=[RIb[k]])
                        fw.op("act", lambda E, I=I, bi_=bi_, c=c: E.activation(out=I, in_=PS[bi_][:, :], func=AF.Exp, scale=-1.0, bias=NEGB[:, l, 1, c:c + 1]),
                              reads=[PSb[bi_], NEGBb], writes=[RIb[k]])
                        fw.op("act", lambda E, RI2=RI2: E.activation(out=RI2, in_=RI2, func=AF.Ln, bias=one_ap), reads=[RIb[k], CSTb], writes=[RIb[k]])
                        fw.op("act", lambda E, RI2=RI2: E.activation(out=RI2, in_=RI2, func=AF.Exp, scale=-1.0), reads=[RIb[k]], writes=[RIb[k]])
                        fw.op("act", lambda E, R=R, c=c: E.activation(out=R, in_=R, func=AF.Exp, scale=CLAM[:, l, 0, c:c + 1]),
                              reads=[RIb[k], CLAMb], writes=[RIb[k]])
                        fw.op("pool", lambda E, k=k, R=R: E.tensor_tensor(out=T[k], in0=R, in1=R, op=ALU.mult),
                              reads=[RIb[k]], writes=[Tb[k]])
                        fw.op("act", lambda E, k=k: E.activation(out=T[k], in_=T[k], func=AF.Ln, scale=-1.0, bias=one_ap),
                              reads=[Tb[k], CSTb], writes=[Tb[k]])
                        fw.op("act", lambda E, k=k: E.activation(out=T[k], in_=T[k], func=AF.Exp, scale=0.5),
                              reads=[Tb[k]], writes=[Tb[k]])
                        if seg == 0 and tt == 0:
                            fw.op("dve", lambda E, k=k: E.memset(T[k][:, 0:1], 1.0), reads=[], writes=[Tb[k]])
                        fw.op("pool", lambda E, k=k, I=I: E.tensor_tensor(out=I, in0=I, in1=T[k], op=ALU.mult),
                              reads=[RIb[k], Tb[k]], writes=[RIb[k]])
                        fw.op("pool", lambda E, k=k, I=I: E.tensor_tensor(out=I, in0=I, in1=XC[k], op=ALU.mult),
                              reads=[RIb[k], XCb[k]], writes=[RIb[k]])
                        fw.op("dve", lambda E, k=k, c=c, R=R, I=I: E.tensor_tensor_scan(out=HH[k], data0=R, data1=I, initial=CAR[:, l, c:c + 1],
                                                                                      op0=ALU.mult, op1=ALU.add),
                              reads=[RIb[k], CARb[l][c], Tb[k]], writes=[HHb[k]])
                        fw.op("dve", lambda E, k=k, c=c: E.tensor_copy(out=CAR[:, l, c:c + 1], in_=HH[k][:, TT - 1:TT]),
                              reads=[HHb[k]], writes=[CARb[l][c]])
                        fw.op("dve", lambda E, k=k, c=c: E.tensor_tensor(out=AT[:, c, :], in0=HH[k], in1=GG8[:, c, :], op=ALU.mult),
                              reads=[HHb[k], GG8b[c]], writes=[ATb[c]])

                    for i in range(8 + 2):
                        if i < 8:
                            stage_a(i)
                        if i >= 2:
                            stage_b(i - 2)
                    gate_merge(l, 0, tt, wra, AT, ATb, wgl, [Wob], TM, TMb, GT, GTb)
                mix_first[0] = False
            return dmas, comp, pre
        f._name = "A1"
        return f

    def st_A2(l):
        def f(W, Wb, Wo, Wob):
            wra, o = wview(W, 0, [8, D])
            wgl, o = wview(W, o, [8, D])
            dmas = [(wra, kview(wra_d[l])), (wgl, kview(win_d[l, :, C_GL:C_GL + D]))]
            return dmas, (lambda: None)
        f._name = "A2"
        return f

    def st_C1(l, sq, seg):
        def f(W, Wb, Wo, Wob):
            wzu, o = wview(W, 0, [8, D])
            wzv, o = wview(W, o, [8, D])
            wmT, o = wview(W, o, [8, 128])
            dmas = [(wzu, kview(win_d[l, :, C_ZU:C_ZU + D])), (wzv, kview(win_d[l, :, C_ZV:C_ZV + D])), (wmT, swT_d[l])]
            wrc, o2 = wview(Wo, 0, [8, D])
            wgl, o2 = wview(Wo, o2, [8, D])
            post = [("z", wmT[64:128, :, 0:64])]

            def comp():
                fw.barrier()
                cv = Carver(SCR, MIX_BASE)
                UT = cv.take([8, TT], BF16); UTb = [Buf() for _ in range(8)]
                VT = [cv.take([D], F32) for _ in range(2)]; VTb = [Buf(), Buf()]
                VH = [cv.take([D], BF16) for _ in range(2)]; VHb = [Buf(), Buf()]
                CT = cv.take([8, TT], BF16); CTb = [Buf() for _ in range(8)]
                TC1 = cv.take([8, 128], F32); TC = [TC1, TC1]; TCb1 = Buf(); TCb = [TCb1, TCb1]
                B2 = cv.take([8, 128], F32); B2b = Buf()
                STA = [cv.take([16], F32) for _ in range(2)]; STAb = [Buf(), Buf()]
                GT = [cv.take([TT], F32) for _ in range(2)]; GTb = [Buf(), Buf()]
                TMall = cv.take([2, TT], F32); TM = [TMall[:, 0, :], TMall[:, 1, :]]; TMb = [Buf(), Buf()]
                BSB = TMall.rearrange("p a (b c) -> p (a b) c", c=128)
                assert cv.off <= SCRW * 4, cv.off
                if "B" not in phases and "A" not in phases:
                    norm_x("mix_norm", l)
                fw.dma("sp", "ld_bsb", BSB, sbs_d[l].rearrange("(g t) -> g t", g=8).partition_broadcast(128), writes=TMb)
                for hh in range(2):
                    bb = bank("c")
                    mm(bb, PS[bb][:, :], [(ONES[:], wmT[:, hh * 4:hh * 4 + 4, :])], [ONESb, Wb])
                    for g4 in range(4):
                        g = hh * 4 + g4
                        fw.op("dve", lambda E, g=g, g4=g4, bb=bb: E.scalar_tensor_tensor(
                            out=B2[:, g, :], in0=PS[bb][:, g4 * 128:(g4 + 1) * 128], scalar=prm_col(l, "sgu_norm_b", g),
                            in1=BSB[:, g, :], op0=ALU.mult, op1=ALU.add), reads=[PSb[bb], PRMb] + TMb, writes=[B2b])
                for tt in range(2):
                    nrd = [Nb[c][tt] for c in range(8)] + [Wb]
                    for g in range(8):
                        bu = bank("a")
                        mm(bu, PS[bu][:, :], [(wzu[:, kc, g * 128:(g + 1) * 128], N[:, kc, tsl(tt)]) for kc in range(8)], nrd)
                        fw.op("act", lambda E, g=g, bu=bu: E.activation(out=UT[:, g, :], in_=PS[bu][:, :], func=AF.Gelu_apprx_tanh),
                              reads=[PSb[bu]], writes=[UTb[g]])
                    def stage_v(blk):
                        k = blk % 2
                        t0 = tt * TT + blk * 128
                        for hv in range(2):
                            bv = bank("a")
                            mm(bv, PS[bv][:, :], [(N[:, kc, t0:t0 + 128], wzv[:, kc, hv * 512:(hv + 1) * 512]) for kc in range(8)], nrd)
                            fw.op("act", lambda E, k=k, hv=hv, bv=bv: E.activation(out=VT[k][:, hv * 512:(hv + 1) * 512], in_=PS[bv][:, :],
                                                                                  func=AF.Gelu_apprx_tanh), reads=[PSb[bv]], writes=[VTb[k]])
                        for hv in range(2):
                            fw.op("dve", lambda E, k=k, hv=hv: E.bn_stats(out=STA[k][:, hv * 6:(hv + 1) * 6], in_=VT[k][:, hv * 512:(hv + 1) * 512]),
                                  reads=[VTb[k]], writes=[STAb[k]])
                        fw.op("dve", lambda E, k=k: E.bn_aggr(out=STA[k][:, 12:14], in_=STA[k][:, 0:12]), reads=[STAb[k]], writes=[STAb[k]])
                        fw.op("act", lambda E, k=k: E.activation(out=STA[k][:, 14:15], in_=STA[k][:, 13:14], func=AF.Sqrt, bias=eps_ap),
                              reads=[STAb[k], CSTb], writes=[STAb[k]])
                        fw.op("dve", lambda E, k=k: E.reciprocal(out=STA[k][:, 15:16], in_=STA[k][:, 14:15]), reads=[STAb[k]], writes=[STAb[k]])
                        fw.op("dve", lambda E, k=k: E.tensor_scalar(out=VH[k], in0=VT[k], scalar1=STA[k][:, 12:13], scalar2=STA[k][:, 15:16],
                                                                   op0=ALU.subtract, op1=ALU.mult), reads=[VTb[k], STAb[k]], writes=[VHb[k]])

                    def stage_p(blk):
                        k = blk % 2
                        for hg in range(2):
                            bp = bank("c")
                            fw.wait("pe", fw.deps([VHb[k], Wb], [PSb[bp]]))
                            tok = None
                            for g4 in range(4):
                                g = hg * 4 + g4
                                tok = fw.raw("pe", lambda E, k=k, g=g, g4=g4, bp=bp: E.matmul(
                                    out=PS[bp][:, g4 * 128:(g4 + 1) * 128], lhsT=VH[k][:, g * 128:(g + 1) * 128], rhs=wmT[:, g, :],
                                    start=True, stop=True))
                            fw.mark(tok, [VHb[k], Wb], [PSb[bp]])
                            for g4 in range(4):
                                g = hg * 4 + g4
                                fw.op("dve", lambda E, k=k, g=g, g4=g4, bp=bp: E.scalar_tensor_tensor(
                                    out=TC[k][:, g, :], in0=PS[bp][:, g4 * 128:(g4 + 1) * 128], scalar=prm_col(l, "sgu_norm_g", g),
                                    in1=B2[:, g, :], op0=ALU.mult, op1=ALU.add), reads=[PSb[bp], B2b, PRMb], writes=[TCb[k]])
                        fw.op("pool", lambda E, k=k, blk=blk: E.tensor_tensor(out=CT[:, :, blk * 128:(blk + 1) * 128], in0=TC[k][:],
                                                                             in1=UT[:, :, blk * 128:(blk + 1) * 128], op=ALU.mult),
                              reads=[TCb[k]] + UTb, writes=CTb)

                    for i in range(4 + 1):
                        if i < 4:
                            stage_v(i)
                        if i >= 1:
                            stage_p(i - 1)
                    gate_merge(l, 2, tt, wrc, CT, CTb, wgl, [Wob], TM, TMb, GT, GTb)
                mix_first[0] = False
            return dmas, comp, [], post
        f._name = "C1"
        return f

    def st_C2(l):
        def f(W, Wb, Wo, Wob):
            wrc, o = wview(W, 0, [8, D])
            wgl, o = wview(W, o, [8, D])
            dmas = [(wrc, kview(wrc_d[l])), (wgl, kview(win_d[l, :, C_GL + 2 * D:C_GL + 3 * D]))]
            return dmas, (lambda: None)
        f._name = "C2"
        return f

    def st_wout(l):
        def f(W, Wb, Wo, Wob):
            wo, o = wview(W, 0, [8, D])
            dmas = [(wo, kview(wout_d[l]))]

            def comp():
                for tt in range(2):
                    rd = [MGb[c][tt] for c in range(8)] + [Wb]
                    for d in range(8):
                        bd = bank("b")
                        mm(bd, PS[bd][:, :], [(wo[:, c, d * 128:(d + 1) * 128], MG[:, c, tsl(tt)]) for c in range(8)], rd)
                        fw.op("dve", lambda E, d=d, tt=tt, bd=bd: E.tensor_tensor(out=X[:, d, tsl(tt)], in0=X[:, d, tsl(tt)], in1=PS[bd][:, :], op=ALU.add),
                              reads=[PSb[bd], Xb[d][tt]], writes=[Xb[d][tt]])
                mix_first[0] = True
            return dmas, comp
        f._name = "wout"
        return f

    def st_ple(l, sq, seg):
        def f(W, Wb, Wo, Wob):
            wpg, o = wview(W, 0, [8, D])
            wpp, o = wview(W, o, [2, D])
            dmas = [(wpg, kview(wpg_d[l])), (wpp, kview(wpp_d[l]))]

            def comp():
                norm_x("ple_norm", l)
                fw.barrier()
                cv = Carver(SCR)
                PTK = [cv.take([PLE], F32) for _ in range(2)]; PTKb = [Buf(), Buf()]
                PTT = cv.take([2, TS], BF16); PTTb = [Buf(), Buf()]
                GT = [cv.take([TT], F32) for _ in range(2)]; GTb = [Buf(), Buf()]
                TM = [cv.take([TT], F32) for _ in range(2)]; TMb = [Buf(), Buf()]
                for tb in range(8):
                    k = tb % 2
                    r0 = sq * S + seg * TS + tb * 128
                    fw.dma("sp", "pl%d" % k, PTK[k], p_d[l, r0:r0 + 128, :], writes=[PTKb[k]])
                    bk = bank("c")
                    fw.wait("pe", fw.deps([PTKb[k], IDENTb], [PSb[bk]]))
                    tok = None
                    for j in range(2):
                        tok = fw.raw("pe", lambda E, j=j, k=k, bk=bk: E.transpose(out=PS[bk][:, j * 128:(j + 1) * 128],
                                                                              in_=PTK[k][:, j * 128:(j + 1) * 128], identity=IDENT[:]))
                    fw.mark(tok, [PTKb[k], IDENTb], [PSb[bk]])
                    fw.op("act", lambda E, tb=tb, bk=bk: E.activation(out=PTT[:, :, tb * 128:(tb + 1) * 128],
                                                                  in_=PS[bk][:, 0:256].rearrange("p (a b) -> p a b", a=2), func=AF.Copy),
                          reads=[PSb[bk]], writes=[PTTb[tb // 4]])
                for tt in range(2):
                    nrd = [Nb[c][tt] for c in range(8)] + [Wb]
                    for d in range(8):
                        bg, bp = bank("a"), bank("a")
                        mm(bg, PS[bg][:, :], [(wpg[:, c, d * 128:(d + 1) * 128], N[:, c, tsl(tt)]) for c in range(8)], nrd)
                        mm(bp, PS[bp][:, :], [(wpp[:, j, d * 128:(d + 1) * 128], PTT[:, j, tsl(tt)]) for j in range(2)], [PTTb[tt], Wb])
                        k = d % 2
                        fw.op("act", lambda E, k=k, bg=bg: E.activation(out=GT[k], in_=PS[bg][:, :], func=AF.Sigmoid),
                              reads=[PSb[bg]], writes=[GTb[k]])
                        fw.op("dve", lambda E, k=k, bp=bp: E.tensor_tensor(out=TM[k], in0=GT[k], in1=PS[bp][:, :], op=ALU.mult),
                              reads=[GTb[k], PSb[bp]], writes=[TMb[k]])
                        fw.op("pool", lambda E, k=k, d=d, tt=tt: E.tensor_tensor(out=X[:, d, tsl(tt)], in0=X[:, d, tsl(tt)], in1=TM[k], op=ALU.add),
                              reads=[TMb[k], Xb[d][tt]], writes=[Xb[d][tt]])
            return dmas, comp
        f._name = "ple"
        return f

    for sq in range(n_seq):
        for seg in range(n_seg):
            stages.append(st_load_x(sq, seg))
            for l in range(n_layers):
                if "ffn1" in phases:
                    for gi in range(len(FFN_GROUPS)):
                        stages.append(st_ffn(l, 0, gi))
                if "B" in phases:
                    stages.append(st_B1(l, sq, seg))
                    stages.append(st_B2(l))
                if "A" in phases:
                    stages.append(st_A1(l, sq, seg))
                    stages.append(st_A2(l))
                if "C" in phases:
                    stages.append(st_C1(l, sq, seg))
                    stages.append(st_C2(l))
                if "wout" in phases:
                    stages.append(st_wout(l))
                if "ffn2" in phases:
                    for gi in range(len(FFN_GROUPS)):
                        stages.append(st_ffn(l, 1, gi))
                if "ple" in phases:
                    stages.append(st_ple(l, sq, seg))
            stages.append(st_store(sq, seg))

    half = 0
    pending = None
    marks = []
    nc._marks = marks

    def pe_count():
        return sum(1 for it in fw.q["pe"] if it[0] == "o")

    for sfn in stages:
        res = sfn(WA[half], WAb[half], WA[1 - half], WAb[1 - half])
        dmas, comp = res[0], res[1]
        pre = res[2] if len(res) > 2 else []
        post = res[3] if len(res) > 3 else []
        if dmas:
            hb = WAb[half]
            fw.wait("pool", fw.deps([], [hb]))
            tok = None
            for kind, ap in pre:
                tok = fw.raw("pool", lambda E, ap=ap: E.memset(ap, 0.0))
            if tok is not None:
                fw.wait("pool", {tok[0]: tok[1]})
            for dst, src in dmas:
                tok = fw.dma_raw("pool", "wl%d" % half, dst, src)
            hb.lw = tok
            hb.rd = {}
            for kind, ap in post:
                fw.op("pool", lambda E, ap=ap: E.memset(ap, 0.0), reads=[], writes=[hb])
            half = 1 - half
        if pending is not None:
            marks.append((pending_name, pe_count()))
            pending()
        pending = comp
        pending_name = getattr(sfn, "_name", "?")
    marks.append((pending_name, pe_count()))
    pending()
    marks.append(("end", pe_count()))
    fw.wait("sp", fw.deps([], [], extra=out_toks))
    fw.emit()
    st.close()
    return nc


_CACHE = {}


def _host_weights(inp):
    f = lambda k: np.ascontiguousarray(np.asarray(inp[k], dtype=np.float32))
    w_in = f("w_in")
    w = {}
    for k in ("ffn1_w_gu", "ffn2_w_gu", "ffn1_w_down", "ffn2_w_down", "mla_w_ukv", "w_read_a", "w_read_b", "w_read_c",
              "w_out", "ple_w_gate", "ple_w_proj", "lru_w_a", "lru_w_x"):
        w[k] = f(k)
    w["w_in"] = w_in
    kr = w_in[:, :, C_KR:C_KR + 32]
    kr_sw = np.concatenate([kr[:, :, 16:32], kr[:, :, 0:16]], axis=-1)
    w["w_attn"] = np.ascontiguousarray(np.concatenate(
        [w_in[:, :, C_CQ:C_CQ + 384], w_in[:, :, C_CKV:C_CKV + 256], w_in[:, :, C_CKV:C_CKV + 64], kr, kr_sw], axis=-1))
    uq = f("mla_w_uq").reshape(L, 384, NH, 96)
    uqx = np.concatenate([uq, uq[..., 80:96], uq[..., 64:80]], axis=-1)
    w["w_uqx"] = np.ascontiguousarray(uqx.reshape(L, 384, NH * 128))
    w["sgu_wT"] = np.ascontiguousarray(f("sgu_w_s").transpose(0, 3, 1, 2))
    w["sgu_b_s"] = np.ascontiguousarray(f("sgu_b_s").reshape(L, 1024))
    prm = np.zeros((PRM_ROWS, 128), np.float32)
    for l in range(L):
        for name, k in PRM_LAYOUT:
            r0 = l * PRM_PER_LAYER + PRM_OFF[name]
            prm[r0:r0 + k] = f(name)[l].reshape(k, 128)
    prm[PRM_FINAL:PRM_FINAL + 8] = f("final_norm").reshape(8, 128)
    w["prm"] = prm
    half = 16
    inv_freq = (10000.0 ** (-np.arange(half, dtype=np.float32) / half)).astype(np.float32)
    ang = np.arange(S, dtype=np.float32)[None, :] * inv_freq[:, None]
    cos, sin = np.cos(ang).astype(np.float32), np.sin(ang).astype(np.float32)
    w["rope"] = np.ascontiguousarray(np.concatenate([cos, cos, -sin, sin], axis=0))
    w["ident"] = np.eye(128, dtype=np.float32)
    return w


def kernel(**inputs):
    x = np.asarray(inputs["x"], dtype=np.float32)
    p = np.asarray(inputs["p"], dtype=np.float32)
    w = _host_weights(inputs)
    if "nc" not in _CACHE:
        _CACHE["nc"] = build()
    nc = _CACHE["nc"]
    in_maps = []
    for c in range(8):
        m = dict(w)
        m["x"] = np.ascontiguousarray(x[2 * c:2 * c + 2].reshape(2 * S, D))
        m["p"] = np.ascontiguousarray(p[:, 2 * c:2 * c + 2].reshape(L, 2 * S, PLE))
        in_maps.append(m)
    res = run_bass_kernel_spmd(nc, in_maps, core_ids=list(range(8)))
    out = np.stack([res.results[c]["out"].reshape(2, S, D) for c in range(8)], axis=0).reshape(16, S, D)
    return np.ascontiguousarray(out.astype(np.float32))
```

```python
from contextlib import ExitStack
import numpy as np
import concourse.bass as bass
import concourse.mybir as mybir
from concourse.bass_utils import run_bass_kernel_spmd

F32 = mybir.dt.float32
BF16 = mybir.dt.bfloat16
ALU = mybir.AluOpType
AF = mybir.ActivationFunctionType

D = 1024
S = 2048
L = 2
DFF = 2816
PLE = 256
TS = 1024
TT = 512
EPS = 1e-6
NH = 16
ENGS = ("pe", "act", "dve", "pool", "sp")
EPOCH = 16000
C_XA, C_GA, C_CQ, C_CKV, C_KR, C_ZU, C_ZV, C_GL = 0, 1024, 2048, 2432, 2688, 2720, 3744, 4768
PRM_LAYOUT = [("ffn1_norm", 8), ("mix_norm", 8), ("ffn2_norm", 8), ("ple_norm", 8), ("conv_w", 32),
              ("conv_b", 8), ("lru_b_a", 8), ("lru_b_x", 8), ("lru_lambda", 8), ("mla_q_norm", 3),
              ("mla_kv_norm", 2), ("sgu_norm_g", 8), ("sgu_norm_b", 8), ("gate_bias", 24)]
PRM_OFF = {}
_o = 0
for _n, _k in PRM_LAYOUT:
    PRM_OFF[_n] = _o
    _o += _k
PRM_PER_LAYER = _o
PRM_FINAL = L * PRM_PER_LAYER
PRM_ROWS = 384
FFN_GROUPS = [(0, 6), (6, 6), (12, 5), (17, 5)]


class Buf:
    __slots__ = ("lw", "rd")

    def __init__(self):
        self.lw = None
        self.rd = {}


class Fw:
    def __init__(self, nc, stack):
        self.nc = nc
        self.stack = stack
        self.q = {e: [] for e in ENGS}
        self.sems = {}
        self.cnt = {}
        self.seen = {e: {} for e in ENGS}
        self.cur = {}
        self.last = {}
        self.pending_sp = {}
        for e in ("pe", "act", "dve", "pool"):
            self._new_epoch(e)

    def new_sem(self, key):
        self.sems[key] = self.stack.enter_context(self.nc.semaphore(key.replace("#", "_")))
        self.cnt[key] = 0

    def _new_epoch(self, e):
        k = "%s#%d" % (e, sum(1 for x in self.sems if x.startswith(e + "#")))
        self.new_sem(k)
        self.cur[e] = k

    @staticmethod
    def _add(deps, t):
        if t is not None and deps.get(t[0], 0) < t[1]:
            deps[t[0]] = t[1]

    def deps(self, reads, writes, extra=()):
        deps = {}
        for b in reads:
            self._add(deps, b.lw)
        for b in writes:
            self._add(deps, b.lw)
            for k, v in b.rd.items():
                if deps.get(k, 0) < v:
                    deps[k] = v
        for t in extra:
            self._add(deps, t)
        return deps

    def wait(self, eng, deps):
        seen = self.seen[eng]
        for k, v in deps.items():
            if eng == "pe" and k.startswith("pe#"):
                continue
            if seen.get(k, 0) >= v:
                continue
            seen[k] = v
            self.q[eng].append(("w", k, v))

    def mark(self, tok, reads, writes):
        k, v = tok
        for b in reads:
            if b.rd.get(k, 0) < v:
                b.rd[k] = v
        for b in writes:
            b.lw = tok
            b.rd = {}

    def raw(self, eng, fn):
        if self.cnt[self.cur[eng]] >= EPOCH:
            self._new_epoch(eng)
        key = self.cur[eng]
        self.cnt[key] += 1
        self.q[eng].append(("o", fn, key, 1))
        self.last[eng] = (key, self.cnt[key])
        return (key, self.cnt[key])

    def op(self, eng, fn, reads=(), writes=(), extra=()):
        self.wait(eng, self.deps(reads, writes, extra))
        tok = self.raw(eng, fn)
        self.mark(tok, reads, writes)
        return tok

    def dma_raw(self, qeng, chan, out, in_):
        if chan not in self.sems:
            self.new_sem(chan)
        self.cnt[chan] += 16
        self.q[qeng].append(("o", lambda E: E.dma_start(out=out, in_=in_), chan, 16))
        return (chan, self.cnt[chan])

    def dma(self, qeng, chan, out, in_, reads=(), writes=(), extra=()):
        self.wait(qeng, self.deps(reads, writes, extra))
        tok = self.dma_raw(qeng, chan, out, in_)
        self.mark(tok, reads, writes)
        if qeng == "sp":
            self.pending_sp[tok[0]] = tok[1]
        return tok

    def barrier(self):
        toks = {}
        for e in ("pe", "act", "dve", "pool"):
            t = self.last.get(e)
            if t is not None:
                toks[t[0]] = t[1]
        toks.update(self.pending_sp)
        self.pending_sp = {}
        for e in ("pe", "act", "dve", "pool", "sp"):
            self.wait(e, toks)

    def emit(self):
        nc = self.nc
        with nc.Block() as block:
            def run(eng):
                def body(E):
                    sems = self.sems
                    for it in self.q[eng]:
                        if it[0] == "w":
                            E.wait_ge(sems[it[1]], it[2])
                        else:
                            it[1](E).then_inc(sems[it[2]], it[3])
                return body
            block.tensor(run("pe"))
            block.scalar(run("act"))
            block.vector(run("dve"))
            block.gpsimd(run("pool"))
            block.sync(run("sp"))


class Carver:
    def __init__(self, scr, base=0):
        self.scr = scr
        self.off = base

    def take(self, shape, dtype):
        n = 1
        for s in shape:
            n *= s
        nbytes = n * (4 if dtype == F32 else 2)
        nbytes = (nbytes + 31) // 32 * 32
        o = self.off
        self.off += nbytes
        v = self.scr[:, o // 4:(o + nbytes) // 4]
        if dtype != F32:
            v = v.bitcast(dtype)
        v = v[:, 0:n]
        if len(shape) == 2:
            v = v.rearrange("p (a b) -> p a b", a=shape[0])
        elif len(shape) == 3:
            v = v.rearrange("p (a b c) -> p a b c", a=shape[0], b=shape[1])
        return v


def build(n_seq=2, n_layers=2, phases=("ffn1", "B", "A", "C", "wout", "ffn2", "ple"), n_seg=2, dbg=()):
    nc = bass.Bass("TRN2", target_bir_lowering=False, dynamic_dma_scratch_size=8192)
    dbg_done = set()
    NT = n_seq * S
    dr = lambda n, s: nc.dram_tensor(n, s, F32, kind="ExternalInput").ap()
    x_d = dr("x", [NT, D])
    p_d = dr("p", [L, NT, PLE])
    out_d = nc.dram_tensor("out", [NT, D], F32, kind="ExternalOutput").ap()
    wgu_d = [dr("ffn1_w_gu", [L, D, 2 * DFF]), dr("ffn2_w_gu", [L, D, 2 * DFF])]
    wdn_d = [dr("ffn1_w_down", [L, DFF, D]), dr("ffn2_w_down", [L, DFF, D])]
    win_d = dr("w_in", [L, D, 7840])
    wattn_d = dr("w_attn", [L, D, 768])
    wuqx_d = dr("w_uqx", [L, 384, 2048])
    wukv_d = dr("mla_w_ukv", [L, 256, 2048])
    wra_d = dr("w_read_a", [L, D, D])
    wrb_d = dr("w_read_b", [L, D, D])
    wrc_d = dr("w_read_c", [L, D, D])
    wout_d = dr("w_out", [L, D, D])
    wpg_d = dr("ple_w_gate", [L, D, D])
    wpp_d = dr("ple_w_proj", [L, PLE, D])
    lwa_d = dr("lru_w_a", [L, 16, 64, 64])
    lwx_d = dr("lru_w_x", [L, 16, 64, 64])
    swT_d = dr("sgu_wT", [L, 128, 8, 128])
    sbs_d = dr("sgu_b_s", [L, 1024])
    prm_d = dr("prm", [PRM_ROWS, 128])
    rope_d = dr("rope", [64, S])
    ident_d = dr("ident", [128, 128])

    st = ExitStack()
    fw = Fw(nc, st)
    sb = lambda n, s, d: st.enter_context(nc.sbuf_tensor(n, s, d))
    WAW = 18432
    WA = [sb("wa%d" % i, [128, WAW], BF16) for i in range(2)]
    WAb = [Buf(), Buf()]
    X = sb("X", [128, 8, TS], F32)
    Xb = [[Buf() for _ in range(2)] for _ in range(8)]
    N = sb("N", [128, 8, TS], BF16)
    Nb = [[Buf() for _ in range(2)] for _ in range(8)]
    CKV = [sb("ckv%d" % l, [128, 2, TS], BF16) for l in range(L)] + [sb("ckvc", [128, 2, TS], BF16)]
    CKVb = [Buf() for _ in range(L + 1)]
    KR = [sb("kr%d" % l, [128, TS], BF16) for l in range(L)] + [sb("krc", [128, TS], BF16)]
    KRb = [Buf() for _ in range(L + 1)]
    PRM = sb("prm_sb", [128, PRM_ROWS], F32); PRMb = Buf()
    IDENT = sb("ident_sb", [128, 128], F32); IDENTb = Buf()
    ONES = sb("ones", [128, 128], BF16); ONESb = Buf()
    CST = sb("cst", [128, 4], F32); CSTb = Buf()
    CLAM = sb("clam", [128, L, 2, 8], F32); CLAMb = Buf()
    HALO = sb("halo", [128, L, 8, 3], F32); HALOb = [[Buf() for _ in range(8)] for _ in range(L)]
    CAR = sb("car", [128, L, 8], F32); CARb = [[Buf() for _ in range(8)] for _ in range(L)]
    SQ = [sb("sq%d" % i, [128, TT], BF16) for i in range(2)]; SQb = [Buf(), Buf()]
    STD = sb("std", [128, TT], F32); STDb = Buf()
    RSTD = [sb("rstd%d" % i, [128, TT], F32) for i in range(2)]; RSTDb = [Buf(), Buf()]
    SCRW = 17152
    SCR = sb("scr", [128, SCRW], F32)
    PSP = [st.enter_context(nc.psum_tensor("psp%d" % i, [128, 2 * TT], F32)) for i in range(2)]
    PS = [PSP[i // 2][:, (i % 2) * TT:(i % 2 + 1) * TT] for i in range(4)]
    PS += [st.enter_context(nc.psum_tensor("ps%d" % i, [128, TT], F32))[:, :] for i in range(4, 8)]
    PSb = [Buf() for _ in range(8)]
    rot = {"a": [0, 1, 2, 3], "b": [4, 5], "c": [6, 7]}
    rotp = {"a": 0, "b": 0, "c": 0}

    def bank(role):
        lst = rot[role]
        i = lst[rotp[role] % len(lst)]
        rotp[role] += 1
        return i

    def mm(bk, out_ap, pairs, reads):
        fw.wait("pe", fw.deps(reads, [PSb[bk]]))
        n = len(pairs)
        tok = None
        for i, (lh, rh) in enumerate(pairs):
            tok = fw.raw("pe", lambda E, lh=lh, rh=rh, i=i: E.matmul(out=out_ap, lhsT=lh, rhs=rh,
                                                                    start=(i == 0), stop=(i == n - 1)))
        fw.mark(tok, reads, [PSb[bk]])
        return tok

    def prm_col(l, name, c):
        j = l * PRM_PER_LAYER + PRM_OFF[name] + c
        return PRM[:, j:j + 1]

    def tsl(tt):
        return slice(tt * TT, (tt + 1) * TT)

    eps_ap = CST[:, 0:1]
    one_ap = CST[:, 1:2]

    fw.dma("sp", "ld_id", IDENT[:], ident_d, writes=[IDENTb])
    fw.op("dve", lambda E: E.memset(ONES[:], 1.0), writes=[ONESb])
    fw.op("dve", lambda E: E.memset(CST[:, 0:1], EPS), writes=[CSTb])
    fw.op("dve", lambda E: E.memset(CST[:, 1:2], 1.0), writes=[CSTb])
    PTMP = SCR[:, 0:PRM_ROWS].rearrange("p (a b) -> p a b", a=3)
    PTMPb = Buf()
    fw.dma("sp", "ld_prm", PTMP, prm_d.rearrange("(a p) f -> p a f", p=128), writes=[PTMPb])
    bk = bank("a")
    for a in range(3):
        fw.op("pe", lambda E, a=a: E.transpose(out=PS[bk][:, a * 128:(a + 1) * 128], in_=PTMP[:, a, :],
                                               identity=IDENT[:]), reads=[PTMPb, IDENTb], writes=[PSb[bk]])
    fw.op("dve", lambda E: E.tensor_copy(out=PRM[:], in_=PS[bk][:, 0:PRM_ROWS]), reads=[PSb[bk]], writes=[PRMb])
    for l in range(n_layers):
        j = l * PRM_PER_LAYER + PRM_OFF["lru_lambda"]
        fw.op("act", lambda E, l=l, j=j: E.activation(out=CLAM[:, l, 0, :], in_=PRM[:, j:j + 8], func=AF.Exp, scale=-1.0),
              reads=[PRMb], writes=[CLAMb])
        fw.op("act", lambda E, l=l: E.activation(out=CLAM[:, l, 0, :], in_=CLAM[:, l, 0, :], func=AF.Ln, bias=one_ap),
              reads=[CLAMb, CSTb], writes=[CLAMb])
        fw.op("dve", lambda E, l=l: E.tensor_scalar(out=CLAM[:, l, 1, :], in0=CLAM[:, l, 0, :], scalar1=-16.0, scalar2=None,
                                                    op0=ALU.mult), reads=[CLAMb], writes=[CLAMb])
        fw.op("dve", lambda E, l=l: E.tensor_scalar(out=CLAM[:, l, 0, :], in0=CLAM[:, l, 0, :], scalar1=-8.0, scalar2=None,
                                                    op0=ALU.mult), reads=[CLAMb], writes=[CLAMb])

    NEGB = sb("negb", [128, L, 2, 8], F32); NEGBb = Buf()
    for l in range(n_layers):
        for jj, nm in enumerate(("lru_b_a", "lru_b_x")):
            j = l * PRM_PER_LAYER + PRM_OFF[nm]
            fw.op("dve", lambda E, l=l, jj=jj, j=j: E.tensor_scalar(out=NEGB[:, l, jj, :], in0=PRM[:, j:j + 8], scalar1=-1.0, scalar2=None,
                                                                    op0=ALU.mult), reads=[PRMb], writes=[NEGBb])

    rk = [0]

    def rmsnorm(src_aps, src_bufs, gain_aps, out_aps, out_bufs, dn, n=TT):
        nch = len(src_aps)
        bk = bank("c")
        fw.wait("pe", fw.deps([], [PSb[bk]]))
        tok = None
        for c in range(nch):
            k = c % 2
            fw.op("act", lambda E, c=c, k=k: E.activation(out=SQ[k][:, 0:n], in_=src_aps[c], func=AF.Square),
                  reads=[src_bufs[c]], writes=[SQb[k]])
            fw.wait("pe", fw.deps([SQb[k], ONESb], []))
            tok = fw.raw("pe", lambda E, c=c, k=k: E.matmul(out=PS[bk][:, 0:n], lhsT=ONES[:], rhs=SQ[k][:, 0:n],
                                                           start=(c == 0), stop=(c == nch - 1)))
            fw.mark(tok, [SQb[k], ONESb], [])
        fw.mark(tok, [], [PSb[bk]])
        fw.op("act", lambda E: E.activation(out=STD[:, 0:n], in_=PS[bk][:, 0:n], func=AF.Sqrt, scale=1.0 / dn, bias=eps_ap),
              reads=[PSb[bk], CSTb], writes=[STDb])
        r = rk[0] % 2
        rk[0] += 1
        fw.op("dve", lambda E: E.reciprocal(out=RSTD[r][:, 0:n], in_=STD[:, 0:n]), reads=[STDb], writes=[RSTDb[r]])
        for c in range(nch):
            fw.op("dve", lambda E, c=c: E.scalar_tensor_tensor(out=out_aps[c], in0=src_aps[c], scalar=gain_aps[c],
                                                               in1=RSTD[r][:, 0:n], op0=ALU.mult, op1=ALU.mult),
                  reads=[src_bufs[c], RSTDb[r], PRMb], writes=[out_bufs[c]])

    def norm_x(gname, l):
        for tt in range(2):
            rmsnorm([X[:, c, tsl(tt)] for c in range(8)], [Xb[c][tt] for c in range(8)],
                    [prm_col(l, gname, c) for c in range(8)],
                    [N[:, c, tsl(tt)] for c in range(8)], [Nb[c][tt] for c in range(8)], D)

    def wview(W, off, shape):
        n = 1
        for s_ in shape:
            n *= s_
        v = W[:, off:off + n]
        if len(shape) == 2:
            v = v.rearrange("p (a b) -> p a b", a=shape[0])
        elif len(shape) == 3:
            v = v.rearrange("p (a b c) -> p a b c", a=shape[0], b=shape[1])
        return v, off + n

    def kview(src2d):
        return src2d.rearrange("(c p) f -> p c f", p=128)

    stages = []

    def st_load_x(sq, seg):
        def f(W, Wb, Wo, Wob):
            def comp():
                fw.barrier()
                cv = Carver(SCR)
                XT = [cv.take([D], F32) for _ in range(2)]
                XTb = [Buf(), Buf()]
                for l in range(n_layers):
                    if seg == 0:
                        fw.op("dve", lambda E, l=l: E.memset(HALO[:, l, :, :], 0.0), writes=HALOb[l])
                        fw.op("dve", lambda E, l=l: E.memset(CAR[:, l, :], 0.0), writes=CARb[l])
                for tb in range(8):
                    k = tb % 2
                    r0 = sq * S + seg * TS + tb * 128
                    fw.dma("sp", "xl%d" % k, XT[k], x_d[r0:r0 + 128, :], writes=[XTb[k]])
                    for hh in range(2):
                        bk = bank("a")
                        fw.wait("pe", fw.deps([XTb[k], IDENTb], [PSb[bk]]))
                        tok = None
                        for c4 in range(4):
                            c = hh * 4 + c4
                            tok = fw.raw("pe", lambda E, c=c, c4=c4, k=k, bk=bk: E.transpose(
                                out=PS[bk][:, c4 * 128:(c4 + 1) * 128], in_=XT[k][:, c * 128:(c + 1) * 128], identity=IDENT[:]))
                        fw.mark(tok, [XTb[k], IDENTb], [PSb[bk]])
                        tt = tb // 4
                        dst = X[:, hh * 4:hh * 4 + 4, tb * 128:(tb + 1) * 128]
                        src = PS[bk][:, :].rearrange("p (a b) -> p a b", a=4)
                        wr = [Xb[c][tt] for c in range(hh * 4, hh * 4 + 4)]
                        if hh == 0:
                            fw.op("act", lambda E, dst=dst, src=src: E.activation(out=dst, in_=src, func=AF.Copy),
                                  reads=[PSb[bk]], writes=wr)
                        else:
                            fw.op("dve", lambda E, dst=dst, src=src: E.tensor_copy(out=dst, in_=src),
                                  reads=[PSb[bk]], writes=wr)
            return [], comp
        f._name = "load_x"
        return f

    out_toks = []

    def tap(name, ap, bufs):
        if name not in dbg or name in dbg_done:
            return
        dbg_done.add(name)
        dd = nc.dram_tensor("dbg_" + name, list(ap.shape), ap.dtype, kind="ExternalOutput").ap()
        out_toks.append(fw.dma("sp", "dbg", dd, ap, reads=bufs))

    def st_store(sq, seg):
        def f(W, Wb, Wo, Wob):
            def comp():
                fw.barrier()
                cv = Carver(SCR)
                Y = cv.take([8, TT], F32); Yb = [Buf() for _ in range(8)]
                OTK = [cv.take([D], F32) for _ in range(2)]; OTKb = [Buf(), Buf()]
                for tt in range(2):
                    rmsnorm([X[:, c, tsl(tt)] for c in range(8)], [Xb[c][tt] for c in range(8)],
                            [PRM[:, PRM_FINAL + c:PRM_FINAL + c + 1] for c in range(8)],
                            [Y[:, c, :] for c in range(8)], Yb, D)
                    for blk in range(4):
                        k = blk % 2
                        for hh in range(2):
                            bk = bank("a")
                            rd = [Yb[c] for c in range(hh * 4, hh * 4 + 4)] + [IDENTb]
                            fw.wait("pe", fw.deps(rd, [PSb[bk]]))
                            tok = None
                            for c4 in range(4):
                                c = hh * 4 + c4
                                tok = fw.raw("pe", lambda E, c=c, c4=c4, bk=bk, blk=blk: E.transpose(
                                    out=PS[bk][:, c4 * 128:(c4 + 1) * 128], in_=Y[:, c, blk * 128:(blk + 1) * 128], identity=IDENT[:]))
                            fw.mark(tok, rd, [PSb[bk]])
                            dst = OTK[k][:, hh * 512:(hh + 1) * 512]
                            if hh == 0:
                                fw.op("act", lambda E, dst=dst, bk=bk: E.activation(out=dst, in_=PS[bk][:, :], func=AF.Copy),
                                      reads=[PSb[bk]], writes=[OTKb[k]])
                            else:
                                fw.op("dve", lambda E, dst=dst, bk=bk: E.tensor_copy(out=dst, in_=PS[bk][:, :]),
                                      reads=[PSb[bk]], writes=[OTKb[k]])
                        r0 = sq * S + seg * TS + tt * TT + blk * 128
                        out_toks.append(fw.dma("sp", "st%d" % k, out_d[r0:r0 + 128, :], OTK[k], reads=[OTKb[k]]))
            return [], comp
        f._name = "store"
        return f

    def st_ffn(l, which, gi):
        f0, nf = FFN_GROUPS[gi]
        gname = "ffn1_norm" if which == 0 else "ffn2_norm"

        def f(W, Wb, Wo, Wob):
            wg, o = wview(W, 0, [8, nf * 128])
            wu, o = wview(W, o, [8, nf * 128])
            wd, o = wview(W, o, [nf, D])
            dmas = [(wg, kview(wgu_d[which][l, :, f0 * 128:(f0 + nf) * 128])),
                    (wu, kview(wgu_d[which][l, :, DFF + f0 * 128:DFF + (f0 + nf) * 128])),
                    (wd, wdn_d[which][l, f0 * 128:(f0 + nf) * 128, :].rearrange("(f p) d -> p f d", p=128))]

            def comp():
                cv = Carver(SCR)
                H = cv.take([2, 6, TT], BF16)
                SG = [cv.take([TT], F32) for _ in range(2)]
                if gi == 0:
                    norm_x(gname, l)
                    fw.barrier()
                    st_ffn.Hb = [[Buf() for _ in range(6)] for _ in range(2)]
                    st_ffn.SGb = [Buf(), Buf()]
                Hb, SGb = st_ffn.Hb, st_ffn.SGb
                k = 0
                for tt in range(2):
                    nrd = [Nb[c][tt] for c in range(8)] + [Wb]
                    for fi in range(nf):
                        bg, bu = bank("a"), bank("a")
                        mm(bg, PS[bg][:, :], [(wg[:, c, fi * 128:(fi + 1) * 128], N[:, c, tsl(tt)]) for c in range(8)], nrd)
                        mm(bu, PS[bu][:, :], [(wu[:, c, fi * 128:(fi + 1) * 128], N[:, c, tsl(tt)]) for c in range(8)], nrd)
                        kk = k % 2
                        k += 1
                        fw.op("act", lambda E, kk=kk, bg=bg: E.activation(out=SG[kk], in_=PS[bg][:, :], func=AF.Silu),
                              reads=[PSb[bg]], writes=[SGb[kk]])
                        fw.op("dve", lambda E, kk=kk, bu=bu, tt=tt, fi=fi: E.tensor_tensor(
                            out=H[:, tt, fi, :], in0=SG[kk], in1=PS[bu][:, :], op=ALU.mult),
                            reads=[SGb[kk], PSb[bu]], writes=[Hb[tt][fi]])
                for tt in range(2):
                    hrd = [Hb[tt][fi] for fi in range(nf)] + [Wb]
                    for d in range(8):
                        bd = bank("b")
                        mm(bd, PS[bd][:, :], [(wd[:, fi, d * 128:(d + 1) * 128], H[:, tt, fi, :]) for fi in range(nf)], hrd)
                        fw.op("dve", lambda E, d=d, tt=tt, bd=bd: E.scalar_tensor_tensor(
                            out=X[:, d, tsl(tt)], in0=PS[bd][:, :], scalar=0.5, in1=X[:, d, tsl(tt)],
                            op0=ALU.mult, op1=ALU.add), reads=[PSb[bd], Xb[d][tt]], writes=[Xb[d][tt]])
            return dmas, comp
        f._name = "ffn"
        return f

    MGcv = Carver(SCR)
    MG = MGcv.take([8, TS], BF16)
    MGb = [[Buf() for _ in range(2)] for _ in range(8)]
    MIX_BASE = MGcv.off
    mix_first = [True]

    def gate_merge(l, br, tt, yw, Y, Yb, glw, Wbs, TMPG, TMPGb, GT, GTb):
        first = mix_first[0]
        for d in range(8):
            by, bgl = bank("a"), bank("a")
            mm(by, PS[by][:, :], [(yw[:, c, d * 128:(d + 1) * 128], Y[:, c, :]) for c in range(8)], Yb + Wbs)
            mm(bgl, PS[bgl][:, :], [(glw[:, c, d * 128:(d + 1) * 128], N[:, c, tsl(tt)]) for c in range(8)],
               [Nb[c][tt] for c in range(8)] + Wbs)
            k = d % 2
            fw.op("act", lambda E, k=k, bgl=bgl, d=d: E.activation(out=GT[k], in_=PS[bgl][:, :], func=AF.Sigmoid,
                                                                   bias=prm_col(l, "gate_bias", br * 8 + d)),
                  reads=[PSb[bgl], PRMb], writes=[GTb[k]])
            if first:
                fw.op("dve", lambda E, k=k, by=by, d=d: E.tensor_tensor(out=MG[:, d, tsl(tt)], in0=GT[k], in1=PS[by][:, :], op=ALU.mult),
                      reads=[GTb[k], PSb[by]], writes=[MGb[d][tt]])
            else:
                fw.op("dve", lambda E, k=k, by=by: E.tensor_tensor(out=TMPG[k], in0=GT[k], in1=PS[by][:, :], op=ALU.mult),
                      reads=[GTb[k], PSb[by]], writes=[TMPGb[k]])
                fw.op("pool", lambda E, k=k, d=d: E.tensor_tensor(out=MG[:, d, tsl(tt)], in0=MG[:, d, tsl(tt)], in1=TMPG[k], op=ALU.add),
                      reads=[TMPGb[k], MGb[d][tt]], writes=[MGb[d][tt]])

    def st_B1(l, sq, seg):
        def f(W, Wb, Wo, Wob):
            wat, o = wview(W, 0, [8, 768])
            wuq, o = wview(W, o, [3, 2048])
            wkv, o = wview(W, o, [2, 2048])
            dmas = [(wat, kview(wattn_d[l])), (wuq, kview(wuqx_d[l])), (wkv, kview(wukv_d[l]))]
            wrb, o2 = wview(Wo, 0, [8, D])
            wgl, o2 = wview(Wo, o2, [8, D])

            def comp():
                norm_x("mix_norm", l)
                fw.barrier()
                cv = Carver(SCR, MIX_BASE)
                CQN = cv.take([3, TS], BF16); CQNb = [[Buf(), Buf()] for _ in range(3)]
                Q = [cv.take([TT], BF16) for _ in range(2)]; Qb = [Buf(), Buf()]
                K = [cv.take([S], BF16) for _ in range(2)]; Kb = [Buf(), Buf()]
                VE = cv.take([16, 2, 128], BF16); VEb = [Buf(), Buf()]
                PT = [cv.take([2 * TT], BF16) for _ in range(2)]; PTb = [Buf() for _ in range(2)]
                PTD = [cv.take([TT], BF16) for _ in range(2)]; PTDb = [Buf(), Buf()]
                OT = cv.take([8, TT], BF16); OTb = [Buf() for _ in range(8)]
                TAB = cv.take([TS], F32); TABb = Buf()
                T1 = cv.take([TT], F32); T1b = Buf()
                T2 = cv.take([TT], F32); T2b = Buf()
                TQ1 = [T1, T1]; TQ1b = [T1b, T1b]
                TQ2 = [T2, T2]; TQ2b = [T2b, T2b]
                RC = cv.take([TT], F32); RCb = Buf()
                assert cv.off <= SCRW * 4, cv.off
                cdst = l if seg == 0 else L
                scale = float((64 + 32) ** -0.5)
                fw.dma("sp", "ld_tab", TAB[64:128, :], rope_d[:, seg * TS:(seg + 1) * TS], writes=[TABb])
                fw.op("pool", lambda E: E.memset(VE[:, :, :, 64:128], 1.0), writes=VEb)
                for k in range(2):
                    fw.op("pool", lambda E, k=k: E.memset(Q[k][64:128, :], 0.0), writes=[Qb[k]])
                    fw.op("pool", lambda E, k=k: E.memset(K[k][64:128, :], 0.0), writes=[Kb[k]])
                for k in range(2):
                    fw.op("pool", lambda E, k=k: E.memset(PTD[k][64:128, 0:64], 0.0), writes=[PTDb[k]])
                for tt in range(2):
                    nrd = [Nb[c][tt] for c in range(8)] + [Wb]
                    bks = [bank("a") for _ in range(3)]
                    for j in range(3):
                        mm(bks[j], PS[bks[j]][:, :], [(wat[:, c, j * 128:(j + 1) * 128], N[:, c, tsl(tt)]) for c in range(8)], nrd)
                    rmsnorm([PS[bks[j]][:, :] for j in range(3)], [PSb[bks[j]] for j in range(3)],
                            [prm_col(l, "mla_q_norm", j) for j in range(3)],
                            [CQN[:, j, tsl(tt)] for j in range(3)], [CQNb[j][tt] for j in range(3)], 384)
                    bks = [bank("a") for _ in range(2)]
                    for j in range(2):
                        mm(bks[j], PS[bks[j]][:, :], [(wat[:, c, 384 + j * 128:384 + (j + 1) * 128], N[:, c, tsl(tt)]) for c in range(8)], nrd)
                    rmsnorm([PS[bks[j]][:, :] for j in range(2)], [PSb[bks[j]] for j in range(2)],
                            [prm_col(l, "mla_kv_norm", j) for j in range(2)],
                            [CKV[cdst][:, j, tsl(tt)] for j in range(2)], [CKVb[cdst], CKVb[cdst]], 256)
                    bkr = bank("a")
                    mm(bkr, PS[bkr][:, :], [(wat[:, c, 640:768], N[:, c, tsl(tt)]) for c in range(8)], nrd)
                    fw.op("dve", lambda E, bkr=bkr, tt=tt: E.tensor_tensor(out=T1[64:96, :], in0=PS[bkr][64:96, :], in1=TAB[64:96, tsl(tt)], op=ALU.mult),
                          reads=[PSb[bkr], TABb], writes=[T1b])
                    fw.op("dve", lambda E, bkr=bkr, tt=tt: E.tensor_tensor(out=T2[64:96, :], in0=PS[bkr][96:128, :], in1=TAB[96:128, tsl(tt)], op=ALU.mult),
                          reads=[PSb[bkr], TABb], writes=[T2b])
                    fw.op("pool", lambda E, tt=tt: E.tensor_tensor(out=KR[cdst][64:96, tsl(tt)], in0=T1[64:96, :], in1=T2[64:96, :], op=ALU.add),
                          reads=[T1b, T2b], writes=[KRb[cdst]])
                tap("cqn", CQN, [b for bb in CQNb for b in bb])
                tap("ckv", CKV[cdst][:], [CKVb[cdst]])
                tap("kr", KR[cdst][64:96, :], [KRb[cdst]])
                ksrc = [l] if seg == 0 else [l, L]
                nhalf = len(ksrc)
                for kb_ in range(2):
                    for hf in range(nhalf):
                        fw.op("act", lambda E, kb_=kb_, hf=hf: E.activation(out=K[kb_][64:96, hf * TS:(hf + 1) * TS],
                                                                            in_=KR[ksrc[hf]][64:96, :], func=AF.Copy),
                              reads=[KRb[ksrc[hf]]], writes=[Kb[kb_]])

                def lat(kc, k0, n):
                    hf = k0 // TS
                    return CKV[ksrc[hf]][:, kc, k0 - hf * TS:k0 - hf * TS + n], CKVb[ksrc[hf]]

                ptc = [0]
                ptdc = [0]
                pairc = [0]
                order = [(tt, h) for tt in range(2) for h in range(NH)]

                def produce(i):
                    pieces = []
                    tt, h = order[i]
                    kk = i % 2
                    gq = seg * 2 + tt
                    kend = (gq + 1) * TT
                    nkb = kend // 128
                    lrd = [CKVb[j] for j in ksrc] + [Wb]
                    def piece_v(kb0):
                        nb8 = min(8, nkb - kb0)
                        bv = bank("c")
                        fw.wait("pe", fw.deps(lrd, [PSb[bv]]))
                        tok = None
                        for q8 in range(nb8):
                            kb = kb0 + q8
                            for kc in range(2):
                                la, lab = lat(kc, kb * 128, 128)
                                tok = fw.raw("pe", lambda E, la=la, kc=kc, q8=q8, bv=bv, h=h: E.matmul(
                                    out=PS[bv][:, q8 * 64:(q8 + 1) * 64], lhsT=la, rhs=wkv[:, kc, h * 128 + 64:(h + 1) * 128],
                                    start=(kc == 0), stop=(kc == 1)))
                        fw.mark(tok, lrd, [PSb[bv]])
                        fw.op("dve", lambda E, kb0=kb0, nb8=nb8, bv=bv, kk=kk: E.tensor_copy(
                            out=VE[:, kb0:kb0 + nb8, kk, 0:64],
                            in_=PS[bv][:, 0:nb8 * 64].rearrange("p (a e) -> p a e", a=nb8)), reads=[PSb[bv]], writes=[VEb[kk]])
                    for kb0 in range(0, nkb, 8):
                        pieces.append(lambda kb0=kb0: piece_v(kb0))

                    def piece_k(kt):
                        bkk = bank("c")
                        pairs = []
                        for kc in range(2):
                            la, lab = lat(kc, kt * TT, TT)
                            pairs.append((wkv[:, kc, h * 128:(h + 1) * 128], la))
                        mm(bkk, PS[bkk][:, :], pairs, lrd)
                        fw.op("dve", lambda E, kk=kk, kt=kt, bkk=bkk: E.tensor_copy(out=K[kk][0:64, kt * TT:(kt + 1) * TT],
                                                                               in_=PS[bkk][0:64, :]),
                              reads=[PSb[bkk]], writes=[Kb[kk]])
                    for kt in range(kend // TT):
                        pieces.append(lambda kt=kt: piece_k(kt))

                    def piece_q():
                        bq = bank("c")
                        mm(bq, PS[bq][:, :], [(wuq[:, kc, h * 128:(h + 1) * 128], CQN[:, kc, tsl(tt)]) for kc in range(3)],
                           [CQNb[kc][tt] for kc in range(3)] + [Wb])
                        piece_q2(bq)

                    def piece_q2(bq):
                        fw.op("dve", lambda E, kk=kk, bq=bq: E.tensor_copy(out=Q[kk][0:64, :], in_=PS[bq][0:64, :]),
                              reads=[PSb[bq]], writes=[Qb[kk]])
                        fw.op("dve", lambda E, bq=bq, tt=tt, kk=kk: E.tensor_tensor(out=TQ1[kk][64:96, :], in0=PS[bq][64:96, :], in1=TAB[64:96, tsl(tt)], op=ALU.mult),
                              reads=[PSb[bq], TABb], writes=[TQ1b[kk]])
                        fw.op("dve", lambda E, bq=bq, tt=tt, kk=kk: E.tensor_tensor(out=TQ2[kk][64:96, :], in0=PS[bq][96:128, :], in1=TAB[96:128, tsl(tt)], op=ALU.mult),
                              reads=[PSb[bq], TABb], writes=[TQ2b[kk]])
                        fw.op("pool", lambda E, kk=kk: E.tensor_tensor(out=Q[kk][64:96, :], in0=TQ1[kk][64:96, :], in1=TQ2[kk][64:96, :], op=ALU.add),
                              reads=[TQ1b[kk], TQ2b[kk]], writes=[Qb[kk]])
                    pieces.append(piece_q)
                    return pieces

                def attend(i, pieces, prev_norm):
                    tt, h = order[i]
                    kk = i % 2
                    gq = seg * 2 + tt
                    nkb = (gq + 1) * TT // 128
                    hp, hh = h // 2, h % 2
                    bo = bank("b")

                    def issue_pv(item, last):
                        kb, q_lo, pt, ptb, first = item
                        fw.wait("pe", fw.deps([ptb, VEb[kk]], [PSb[bo]] if first else []))
                        tok = fw.raw("pe", lambda E, kb=kb, q_lo=q_lo, pt=pt, first=first, last=last: E.matmul(
                            out=PS[bo][:, q_lo:TT], lhsT=VE[:, kb, kk, :], rhs=pt, start=first, stop=last))
                        fw.mark(tok, [ptb, VEb[kk]], [PSb[bo]])

                    nd = gq * 4
                    units = [("pair", j) for j in range(0, nd, 2)] + [("diag", kb) for kb in range(nd, nkb)]
                    prev = None
                    for u in units:
                        if u[0] == "pair":
                            kb = u[1]
                            p = pairc[0] % 2
                            pairc[0] += 1
                            for j in range(2):
                                mm(2 * p + j, PS[2 * p + j][:, :], [(K[kk][:, (kb + j) * 128:(kb + j + 1) * 128], Q[kk][:, :])], [Kb[kk], Qb[kk]])
                            pi = ptc[0] % 2
                            ptc[0] += 1
                            pt, ptb = PT[pi], PTb[pi]
                            fw.op("act", lambda E, pt=pt, p=p: E.activation(out=pt[:, :], in_=PSP[p][:, :], func=AF.Exp, scale=scale),
                                  reads=[PSb[2 * p], PSb[2 * p + 1]], writes=[ptb])
                            items = [(kb, 0, pt[:, 0:TT], ptb, kb == 0), (kb + 1, 0, pt[:, TT:2 * TT], ptb, False)]
                        else:
                            kb = u[1]
                            q_lo = kb * 128 - gq * TT
                            nq = TT - q_lo
                            bs = bank("a")
                            mm(bs, PS[bs][:, 0:nq], [(K[kk][:, kb * 128:(kb + 1) * 128], Q[kk][:, q_lo:TT])], [Kb[kk], Qb[kk]])
                            j = ptdc[0] % 2
                            ptdc[0] += 1
                            pt, ptb = PTD[j], PTDb[j]
                            fw.op("act", lambda E, pt=pt, bs=bs, nq=nq: E.activation(out=pt[0:64, 0:nq], in_=PS[bs][0:64, 0:nq], func=AF.Exp, scale=scale),
                                  reads=[PSb[bs]], writes=[ptb])
                            if nq > 64:
                                fw.op("act", lambda E, pt=pt, bs=bs, nq=nq: E.activation(out=pt[64:128, 64:nq], in_=PS[bs][64:128, 64:nq], func=AF.Exp, scale=scale),
                                      reads=[PSb[bs]], writes=[ptb])
                            items = [(kb, q_lo, pt[:, 0:nq], ptb, kb == 0)]
                        if pieces:
                            pieces.pop(0)()
                        if prev is not None:
                            for it in prev:
                                issue_pv(it, False)
                            if prev_norm is not None:
                                prev_norm()
                                prev_norm = None
                        prev = items
                    while pieces:
                        pieces.pop(0)()
                    if prev_norm is not None:
                        prev_norm()
                    for idx, it in enumerate(prev):
                        issue_pv(it, idx == len(prev) - 1)
                    def norm_fn():
                        fw.op("act", lambda E: E.activation(out=RC[64:128, :], in_=PS[bo][64:128, :], func=AF.Ln),
                              reads=[PSb[bo]], writes=[RCb])
                        fw.op("act", lambda E: E.activation(out=RC[64:128, :], in_=RC[64:128, :], func=AF.Exp, scale=-1.0),
                              reads=[RCb], writes=[RCb])
                        fw.op("dve", lambda E: E.tensor_copy(out=RC[0:64, :], in_=RC[64:128, :]), reads=[RCb], writes=[RCb])
                        fw.op("dve", lambda E: E.tensor_tensor(out=OT[hh * 64:(hh + 1) * 64, hp, :], in0=PS[bo][0:64, :],
                                                               in1=RC[0:64, :], op=ALU.mult),
                              reads=[PSb[bo], RCb], writes=[OTb[hp]])
                    return norm_fn

                for pc in produce(0):
                    pc()
                pnorm = None
                for i in range(len(order)):
                    pnorm = attend(i, produce(i + 1) if i + 1 < len(order) else [], pnorm)
                    tt, h = order[i]
                    if h == NH - 1:
                        pnorm()
                        pnorm = None
                        assert mix_first[0]
                        gate_merge(l, 1, tt, wrb, OT, OTb, wgl, [Wob], None, None, [T1, T2], [T1b, T2b])
                mix_first[0] = False
            return dmas, comp
        f._name = "B1"
        return f

    def st_B2(l):
        def f(W, Wb, Wo, Wob):
            wrb, o = wview(W, 0, [8, D])
            wgl, o = wview(W, o, [8, D])
            dmas = [(wrb, kview(wrb_d[l])), (wgl, kview(win_d[l, :, C_GL + D:C_GL + 2 * D]))]
            return dmas, (lambda: None)
        f._name = "B2"
        return f

    def st_A1(l, sq, seg):
        def f(W, Wb, Wo, Wob):
            wxa, o = wview(W, 0, [8, D])
            wga, o = wview(W, o, [8, D])
            lwa, o = wview(W, o, [8, 128])
            lwx, o = wview(W, o, [8, 128])
            dmas = [(wxa, kview(win_d[l, :, C_XA:C_XA + D])), (wga, kview(win_d[l, :, C_GA:C_GA + D]))]
            for j in range(2):
                dmas.append((lwa[64 * j:64 * j + 64, :, 64 * j:64 * j + 64],
                             lwa_d[l].rearrange("(n two) c d -> two c n d", two=2)[j]))
                dmas.append((lwx[64 * j:64 * j + 64, :, 64 * j:64 * j + 64],
                             lwx_d[l].rearrange("(n two) c d -> two c n d", two=2)[j]))
            wra, o2 = wview(Wo, 0, [8, D])
            wgl, o2 = wview(Wo, o2, [8, D])
            pre = [("z", lwa[0:64, :, 64:128]), ("z", lwa[64:128, :, 0:64]), ("z", lwx[0:64, :, 64:128]), ("z", lwx[64:128, :, 0:64])]

            def comp():
                fw.barrier()
                cv = Carver(SCR, MIX_BASE)
                NS = 3
                XA = [cv.take([TT + 3], F32) for _ in range(NS)]; XAb = [Buf() for _ in range(NS)]
                XC = [cv.take([TT], F32) for _ in range(NS)]; XCb = [Buf() for _ in range(NS)]
                XCH = [cv.take([TT], BF16) for _ in range(NS)]; XCHb = [Buf() for _ in range(NS)]
                RI = [cv.take([2, TT], F32) for _ in range(NS)]; RIb = [Buf() for _ in range(NS)]
                T = [cv.take([TT], F32) for _ in range(NS)]; Tb = [Buf() for _ in range(NS)]
                HH, HHb = T, Tb
                GG8 = cv.take([8, TT], BF16); GG8b = [Buf() for _ in range(8)]
                AT = cv.take([8, TT], BF16); ATb = [Buf() for _ in range(8)]
                GT, GTb = XC[0:2], XCb[0:2]
                TM, TMb = [RI[0][:, 0, :], RI[0][:, 1, :]], [RIb[0], RIb[0]]
                slot = [0]
                assert cv.off <= SCRW * 4, cv.off
                if "B" not in phases:
                    norm_x("mix_norm", l)
                for tt in range(2):
                    nrd = [Nb[c][tt] for c in range(8)] + [Wb]
                    for c in range(8):
                        bg = bank("a")
                        mm(bg, PS[bg][:, :], [(wga[:, kc, c * 128:(c + 1) * 128], N[:, kc, tsl(tt)]) for kc in range(8)], nrd)
                        fw.op("act", lambda E, c=c, bg=bg: E.activation(out=GG8[:, c, :], in_=PS[bg][:, :], func=AF.Gelu_apprx_tanh),
                              reads=[PSb[bg]], writes=[GG8b[c]])
                    base = slot[0]
                    slot[0] += 8

                    def stage_a(c):
                        k = (base + c) % NS
                        bx = bank("a")
                        mm(bx, PS[bx][:, :], [(wxa[:, kc, c * 128:(c + 1) * 128], N[:, kc, tsl(tt)]) for kc in range(8)], nrd)
                        fw.op("pool", lambda E, k=k, c=c: E.tensor_copy(out=XA[k][:, 0:3], in_=HALO[:, l, c, :]),
                              reads=[HALOb[l][c]], writes=[XAb[k]])
                        fw.op("dve", lambda E, k=k, bx=bx: E.tensor_copy(out=XA[k][:, 3:TT + 3], in_=PS[bx][:, :]),
                              reads=[PSb[bx]], writes=[XAb[k]])
                        fw.op("pool", lambda E, k=k, c=c: E.tensor_copy(out=HALO[:, l, c, :], in_=XA[k][:, TT:TT + 3]),
                              reads=[XAb[k]], writes=[HALOb[l][c]])
                        cw = lambda tap, c=c: prm_col(l, "conv_w", tap * 8 + c)
                        fw.op("dve", lambda E, k=k, c=c, cw=cw: E.tensor_scalar(out=XC[k], in0=XA[k][:, 0:TT], scalar1=cw(0),
                                                                             scalar2=prm_col(l, "conv_b", c), op0=ALU.mult, op1=ALU.add),
                              reads=[XAb[k], PRMb], writes=[XCb[k]])
                        for tap in range(1, 4):
                            fw.op("dve", lambda E, k=k, tap=tap, cw=cw: E.scalar_tensor_tensor(
                                out=XC[k], in0=XA[k][:, tap:tap + TT], scalar=cw(tap), in1=XC[k], op0=ALU.mult, op1=ALU.add),
                                reads=[XAb[k], XCb[k], PRMb], writes=[XCb[k]])
                        fw.op("dve", lambda E, k=k: E.tensor_copy(out=XCH[k], in_=XC[k]), reads=[XCb[k]], writes=[XCHb[k]])

                    def stage_b(c):
                        k = (base + c) % NS
                        R = RI[k][:, 0, :]
                        I = RI[k][:, 1, :]
                        RI2 = RI[k][:, :, :].rearrange("p a t -> p (a t)")
                        br_, bi_ = bank("c"), bank("c")
                        mm(br_, PS[br_][:, :], [(lwa[:, c, :], XCH[k])], [XCHb[k], Wb])
                        mm(bi_, PS[bi_][:, :], [(lwx[:, c, :], XCH[k])], [XCHb[k], Wb])
                        fw.op("act", lambda E, R=R, br_=br_, c=c: E.activation(out=R, in_=PS[br_][:, :], func=AF.Exp, scale=-1.0, bias=NEGB[:, l, 0, c:c + 1]),
                              reads=[PSb[br_], NEGBb], writes=[RIb[k]])
                        fw.op("act", lambda E, I=I, bi_=bi_, c=c: E.activation(out=I, in_=PS[bi_][:, :], func=AF.Exp, scale=-1.0, bias=NEGB[:, l, 1, c:c + 1]),
                              reads=[PSb[bi_], NEGBb], writes=[RIb[k]])
                        fw.op("act", lambda E, RI2=RI2: E.activation(out=RI2, in_=RI2, func=AF.Ln, bias=one_ap), reads=[RIb[k], CSTb], writes=[RIb[k]])
                        fw.op("act", lambda E, RI2=RI2: E.activation(out=RI2, in_=RI2, func=AF.Exp, scale=-1.0), reads=[RIb[k]], writes=[RIb[k]])
                        fw.op("act", lambda E, R=R, c=c: E.activation(out=R, in_=R, func=AF.Exp, scale=CLAM[:, l, 0, c:c + 1]),
                              reads=[RIb[k], CLAMb], writes=[RIb[k]])
                        fw.op("act", lambda E, k=k, R=R: E.activation(out=T[k], in_=R, func=AF.Square),
                              reads=[RIb[k]], writes=[Tb[k]])
                        fw.op("act", lambda E, k=k: E.activation(out=T[k], in_=T[k], func=AF.Ln, scale=-1.0, bias=one_ap),
                              reads=[Tb[k], CSTb], writes=[Tb[k]])
                        fw.op("act", lambda E, k=k: E.activation(out=T[k], in_=T[k], func=AF.Exp, scale=0.5),
                              reads=[Tb[k]], writes=[Tb[k]])

                    def stage_b2(c):
                        k = (base + c) % NS
                        R = RI[k][:, 0, :]
                        I = RI[k][:, 1, :]
                        if seg == 0 and tt == 0:
                            fw.op("dve", lambda E, k=k: E.memset(T[k][:, 0:1], 1.0), reads=[], writes=[Tb[k]])
                        fw.op("pool", lambda E, k=k, I=I: E.tensor_tensor(out=I, in0=I, in1=T[k], op=ALU.mult),
                              reads=[RIb[k], Tb[k]], writes=[RIb[k]])
                        fw.op("pool", lambda E, k=k, I=I: E.tensor_tensor(out=I, in0=I, in1=XC[k], op=ALU.mult),
                              reads=[RIb[k], XCb[k]], writes=[RIb[k]])
                        fw.op("dve", lambda E, k=k, c=c, R=R, I=I: E.tensor_tensor_scan(out=HH[k], data0=R, data1=I, initial=CAR[:, l, c:c + 1],
                                                                                      op0=ALU.mult, op1=ALU.add),
                              reads=[RIb[k], CARb[l][c], Tb[k]], writes=[HHb[k]])
                        fw.op("dve", lambda E, k=k, c=c: E.tensor_copy(out=CAR[:, l, c:c + 1], in_=HH[k][:, TT - 1:TT]),
                              reads=[HHb[k]], writes=[CARb[l][c]])
                        fw.op("pool", lambda E, k=k, c=c: E.tensor_tensor(out=AT[:, c, :], in0=HH[k], in1=GG8[:, c, :], op=ALU.mult),
                              reads=[HHb[k], GG8b[c]], writes=[ATb[c]])

                    for i in range(8 + 2):
                        if i < 8:
                            stage_a(i)
                        if 1 <= i < 9:
                            stage_b(i - 1)
                        if i >= 2:
                            stage_b2(i - 2)
                    gate_merge(l, 0, tt, wra, AT, ATb, wgl, [Wob], TM, TMb, GT, GTb)
                mix_first[0] = False
            return dmas, comp, pre
        f._name = "A1"
        return f

    def st_A2(l):
        def f(W, Wb, Wo, Wob):
            wra, o = wview(W, 0, [8, D])
            wgl, o = wview(W, o, [8, D])
            dmas = [(wra, kview(wra_d[l])), (wgl, kview(win_d[l, :, C_GL:C_GL + D]))]
            return dmas, (lambda: None)
        f._name = "A2"
        return f

    def st_C1(l, sq, seg):
        def f(W, Wb, Wo, Wob):
            wzu, o = wview(W, 0, [8, D])
            wzv, o = wview(W, o, [8, D])
            wmT, o = wview(W, o, [8, 128])
            dmas = [(wzu, kview(win_d[l, :, C_ZU:C_ZU + D])), (wzv, kview(win_d[l, :, C_ZV:C_ZV + D])), (wmT, swT_d[l])]
            wrc, o2 = wview(Wo, 0, [8, D])
            wgl, o2 = wview(Wo, o2, [8, D])
            post = [("z", wmT[64:128, :, 0:64])]

            def comp():
                fw.barrier()
                cv = Carver(SCR, MIX_BASE)
                UT = cv.take([8, TT], BF16); UTb = [Buf() for _ in range(8)]
                VT = [cv.take([D], F32) for _ in range(2)]; VTb = [Buf(), Buf()]
                VH = [cv.take([D], BF16) for _ in range(2)]; VHb = [Buf(), Buf()]
                CT = cv.take([8, TT], BF16); CTb = [Buf() for _ in range(8)]
                TC1 = cv.take([8, 128], F32); TC = [TC1, TC1]; TCb1 = Buf(); TCb = [TCb1, TCb1]
                B2 = cv.take([8, 128], F32); B2b = Buf()
                STA = [cv.take([16], F32) for _ in range(2)]; STAb = [Buf(), Buf()]
                GT = [cv.take([TT], F32) for _ in range(2)]; GTb = [Buf(), Buf()]
                TMall = cv.take([2, TT], F32); TM = [TMall[:, 0, :], TMall[:, 1, :]]; TMb = [Buf(), Buf()]
                BSB = TMall.rearrange("p a (b c) -> p (a b) c", c=128)
                assert cv.off <= SCRW * 4, cv.off
                if "B" not in phases and "A" not in phases:
                    norm_x("mix_norm", l)
                fw.dma("sp", "ld_bsb", BSB, sbs_d[l].rearrange("(g t) -> g t", g=8).partition_broadcast(128), writes=TMb)
                for hh in range(2):
                    bb = bank("c")
                    mm(bb, PS[bb][:, :], [(ONES[:], wmT[:, hh * 4:hh * 4 + 4, :])], [ONESb, Wb])
                    for g4 in range(4):
                        g = hh * 4 + g4
                        fw.op("dve", lambda E, g=g, g4=g4, bb=bb: E.scalar_tensor_tensor(
                            out=B2[:, g, :], in0=PS[bb][:, g4 * 128:(g4 + 1) * 128], scalar=prm_col(l, "sgu_norm_b", g),
                            in1=BSB[:, g, :], op0=ALU.mult, op1=ALU.add), reads=[PSb[bb], PRMb] + TMb, writes=[B2b])
                for tt in range(2):
                    nrd = [Nb[c][tt] for c in range(8)] + [Wb]
                    for g in range(8):
                        bu = bank("a")
                        mm(bu, PS[bu][:, :], [(wzu[:, kc, g * 128:(g + 1) * 128], N[:, kc, tsl(tt)]) for kc in range(8)], nrd)
                        fw.op("act", lambda E, g=g, bu=bu: E.activation(out=UT[:, g, :], in_=PS[bu][:, :], func=AF.Gelu_apprx_tanh),
                              reads=[PSb[bu]], writes=[UTb[g]])
                    def stage_v(blk):
                        k = blk % 2
                        t0 = tt * TT + blk * 128
                        for hv in range(2):
                            bv = bank("a")
                            mm(bv, PS[bv][:, :], [(N[:, kc, t0:t0 + 128], wzv[:, kc, hv * 512:(hv + 1) * 512]) for kc in range(8)], nrd)
                            fw.op("act", lambda E, k=k, hv=hv, bv=bv: E.activation(out=VT[k][:, hv * 512:(hv + 1) * 512], in_=PS[bv][:, :],
                                                                                  func=AF.Gelu_apprx_tanh), reads=[PSb[bv]], writes=[VTb[k]])
                        for hv in range(2):
                            fw.op("dve", lambda E, k=k, hv=hv: E.bn_stats(out=STA[k][:, hv * 6:(hv + 1) * 6], in_=VT[k][:, hv * 512:(hv + 1) * 512]),
                                  reads=[VTb[k]], writes=[STAb[k]])
                        fw.op("dve", lambda E, k=k: E.bn_aggr(out=STA[k][:, 12:14], in_=STA[k][:, 0:12]), reads=[STAb[k]], writes=[STAb[k]])
                        fw.op("act", lambda E, k=k: E.activation(out=STA[k][:, 14:15], in_=STA[k][:, 13:14], func=AF.Sqrt, bias=eps_ap),
                              reads=[STAb[k], CSTb], writes=[STAb[k]])
                        fw.op("dve", lambda E, k=k: E.reciprocal(out=STA[k][:, 15:16], in_=STA[k][:, 14:15]), reads=[STAb[k]], writes=[STAb[k]])
                        fw.op("dve", lambda E, k=k: E.tensor_scalar(out=VH[k], in0=VT[k], scalar1=STA[k][:, 12:13], scalar2=STA[k][:, 15:16],
                                                                   op0=ALU.subtract, op1=ALU.mult), reads=[VTb[k], STAb[k]], writes=[VHb[k]])

                    def stage_p(blk):
                        k = blk % 2
                        for hg in range(2):
                            bp = bank("c")
                            fw.wait("pe", fw.deps([VHb[k], Wb], [PSb[bp]]))
                            tok = None
                            for g4 in range(4):
                                g = hg * 4 + g4
                                tok = fw.raw("pe", lambda E, k=k, g=g, g4=g4, bp=bp: E.matmul(
                                    out=PS[bp][:, g4 * 128:(g4 + 1) * 128], lhsT=VH[k][:, g * 128:(g + 1) * 128], rhs=wmT[:, g, :],
                                    start=True, stop=True))
                            fw.mark(tok, [VHb[k], Wb], [PSb[bp]])
                            for g4 in range(4):
                                g = hg * 4 + g4
                                fw.op("dve", lambda E, k=k, g=g, g4=g4, bp=bp: E.scalar_tensor_tensor(
                                    out=TC[k][:, g, :], in0=PS[bp][:, g4 * 128:(g4 + 1) * 128], scalar=prm_col(l, "sgu_norm_g", g),
                                    in1=B2[:, g, :], op0=ALU.mult, op1=ALU.add), reads=[PSb[bp], B2b, PRMb], writes=[TCb[k]])
                        fw.op("pool", lambda E, k=k, blk=blk: E.tensor_tensor(out=CT[:, :, blk * 128:(blk + 1) * 128], in0=TC[k][:],
                                                                             in1=UT[:, :, blk * 128:(blk + 1) * 128], op=ALU.mult),
                              reads=[TCb[k]] + UTb, writes=CTb)

                    for i in range(4 + 1):
                        if i < 4:
                            stage_v(i)
                        if i >= 1:
                            stage_p(i - 1)
                    gate_merge(l, 2, tt, wrc, CT, CTb, wgl, [Wob], TM, TMb, GT, GTb)
                mix_first[0] = False
            return dmas, comp, [], post
        f._name = "C1"
        return f

    def st_C2(l):
        def f(W, Wb, Wo, Wob):
            wrc, o = wview(W, 0, [8, D])
            wgl, o = wview(W, o, [8, D])
            dmas = [(wrc, kview(wrc_d[l])), (wgl, kview(win_d[l, :, C_GL + 2 * D:C_GL + 3 * D]))]
            return dmas, (lambda: None)
        f._name = "C2"
        return f

    def st_wout(l):
        def f(W, Wb, Wo, Wob):
            wo, o = wview(W, 0, [8, D])
            dmas = [(wo, kview(wout_d[l]))]

            def comp():
                for tt in range(2):
                    rd = [MGb[c][tt] for c in range(8)] + [Wb]
                    for d in range(8):
                        bd = bank("b")
                        mm(bd, PS[bd][:, :], [(wo[:, c, d * 128:(d + 1) * 128], MG[:, c, tsl(tt)]) for c in range(8)], rd)
                        fw.op("dve", lambda E, d=d, tt=tt, bd=bd: E.tensor_tensor(out=X[:, d, tsl(tt)], in0=X[:, d, tsl(tt)], in1=PS[bd][:, :], op=ALU.add),
                              reads=[PSb[bd], Xb[d][tt]], writes=[Xb[d][tt]])
                mix_first[0] = True
            return dmas, comp
        f._name = "wout"
        return f

    def st_ple(l, sq, seg):
        def f(W, Wb, Wo, Wob):
            wpg, o = wview(W, 0, [8, D])
            wpp, o = wview(W, o, [2, D])
            dmas = [(wpg, kview(wpg_d[l])), (wpp, kview(wpp_d[l]))]

            def comp():
                norm_x("ple_norm", l)
                fw.barrier()
                cv = Carver(SCR)
                PTK = [cv.take([PLE], F32) for _ in range(2)]; PTKb = [Buf(), Buf()]
                PTT = cv.take([2, TS], BF16); PTTb = [Buf(), Buf()]
                GT = [cv.take([TT], F32) for _ in range(2)]; GTb = [Buf(), Buf()]
                TM = [cv.take([TT], F32) for _ in range(2)]; TMb = [Buf(), Buf()]
                for tb in range(8):
                    k = tb % 2
                    r0 = sq * S + seg * TS + tb * 128
                    fw.dma("sp", "pl%d" % k, PTK[k], p_d[l, r0:r0 + 128, :], writes=[PTKb[k]])
                    bk = bank("c")
                    fw.wait("pe", fw.deps([PTKb[k], IDENTb], [PSb[bk]]))
                    tok = None
                    for j in range(2):
                        tok = fw.raw("pe", lambda E, j=j, k=k, bk=bk: E.transpose(out=PS[bk][:, j * 128:(j + 1) * 128],
                                                                              in_=PTK[k][:, j * 128:(j + 1) * 128], identity=IDENT[:]))
                    fw.mark(tok, [PTKb[k], IDENTb], [PSb[bk]])
                    fw.op("act", lambda E, tb=tb, bk=bk: E.activation(out=PTT[:, :, tb * 128:(tb + 1) * 128],
                                                                  in_=PS[bk][:, 0:256].rearrange("p (a b) -> p a b", a=2), func=AF.Copy),
                          reads=[PSb[bk]], writes=[PTTb[tb // 4]])
                for tt in range(2):
                    nrd = [Nb[c][tt] for c in range(8)] + [Wb]
                    for d in range(8):
                        bg, bp = bank("a"), bank("a")
                        mm(bg, PS[bg][:, :], [(wpg[:, c, d * 128:(d + 1) * 128], N[:, c, tsl(tt)]) for c in range(8)], nrd)
                        mm(bp, PS[bp][:, :], [(wpp[:, j, d * 128:(d + 1) * 128], PTT[:, j, tsl(tt)]) for j in range(2)], [PTTb[tt], Wb])
                        k = d % 2
                        fw.op("act", lambda E, k=k, bg=bg: E.activation(out=GT[k], in_=PS[bg][:, :], func=AF.Sigmoid),
                              reads=[PSb[bg]], writes=[GTb[k]])
                        fw.op("dve", lambda E, k=k, bp=bp: E.tensor_tensor(out=TM[k], in0=GT[k], in1=PS[bp][:, :], op=ALU.mult),
                              reads=[GTb[k], PSb[bp]], writes=[TMb[k]])
                        fw.op("pool", lambda E, k=k, d=d, tt=tt: E.tensor_tensor(out=X[:, d, tsl(tt)], in0=X[:, d, tsl(tt)], in1=TM[k], op=ALU.add),
                              reads=[TMb[k], Xb[d][tt]], writes=[Xb[d][tt]])
            return dmas, comp
        f._name = "ple"
        return f

    for sq in range(n_seq):
        for seg in range(n_seg):
            stages.append(st_load_x(sq, seg))
            for l in range(n_layers):
                if "ffn1" in phases:
                    for gi in range(len(FFN_GROUPS)):
                        stages.append(st_ffn(l, 0, gi))
                if "B" in phases:
                    stages.append(st_B1(l, sq, seg))
                    stages.append(st_B2(l))
                if "A" in phases:
                    stages.append(st_A1(l, sq, seg))
                    stages.append(st_A2(l))
                if "C" in phases:
                    stages.append(st_C1(l, sq, seg))
                    stages.append(st_C2(l))
                if "wout" in phases:
                    stages.append(st_wout(l))
                if "ffn2" in phases:
                    for gi in range(len(FFN_GROUPS)):
                        stages.append(st_ffn(l, 1, gi))
                if "ple" in phases:
                    stages.append(st_ple(l, sq, seg))
            stages.append(st_store(sq, seg))

    half = 0
    pending = None
    marks = []
    nc._marks = marks

    def pe_count():
        return sum(1 for it in fw.q["pe"] if it[0] == "o")

    for sfn in stages:
        res = sfn(WA[half], WAb[half], WA[1 - half], WAb[1 - half])
        dmas, comp = res[0], res[1]
        pre = res[2] if len(res) > 2 else []
        post = res[3] if len(res) > 3 else []
        if dmas:
            hb = WAb[half]
            fw.wait("pool", fw.deps([], [hb]))
            tok = None
            for kind, ap in pre:
                tok = fw.raw("pool", lambda E, ap=ap: E.memset(ap, 0.0))
            if tok is not None:
                fw.wait("pool", {tok[0]: tok[1]})
            for dst, src in dmas:
                tok = fw.dma_raw("pool", "wl%d" % half, dst, src)
            hb.lw = tok
            hb.rd = {}
            for kind, ap in post:
                fw.op("pool", lambda E, ap=ap: E.memset(ap, 0.0), reads=[], writes=[hb])
            half = 1 - half
        if pending is not None:
            marks.append((pending_name, pe_count()))
            pending()
        pending = comp
        pending_name = getattr(sfn, "_name", "?")
    marks.append((pending_name, pe_count()))
    pending()
    marks.append(("end", pe_count()))
    fw.wait("sp", fw.deps([], [], extra=out_toks))
    fw.emit()
    st.close()
    return nc


_CACHE = {}


def _host_weights(inp):
    f = lambda k: np.ascontiguousarray(np.asarray(inp[k], dtype=np.float32))
    w_in = f("w_in")
    w = {}
    for k in ("ffn1_w_gu", "ffn2_w_gu", "ffn1_w_down", "ffn2_w_down", "mla_w_ukv", "w_read_a", "w_read_b", "w_read_c",
              "w_out", "ple_w_gate", "ple_w_proj", "lru_w_a", "lru_w_x"):
        w[k] = f(k)
    w["w_in"] = w_in
    kr = w_in[:, :, C_KR:C_KR + 32]
    kr_sw = np.concatenate([kr[:, :, 16:32], kr[:, :, 0:16]], axis=-1)
    w["w_attn"] = np.ascontiguousarray(np.concatenate(
        [w_in[:, :, C_CQ:C_CQ + 384], w_in[:, :, C_CKV:C_CKV + 256], w_in[:, :, C_CKV:C_CKV + 64], kr, kr_sw], axis=-1))
    uq = f("mla_w_uq").reshape(L, 384, NH, 96)
    uqx = np.concatenate([uq, uq[..., 80:96], uq[..., 64:80]], axis=-1)
    w["w_uqx"] = np.ascontiguousarray(uqx.reshape(L, 384, NH * 128))
    w["sgu_wT"] = np.ascontiguousarray(f("sgu_w_s").transpose(0, 3, 1, 2))
    w["sgu_b_s"] = np.ascontiguousarray(f("sgu_b_s").reshape(L, 1024))
    prm = np.zeros((PRM_ROWS, 128), np.float32)
    for l in range(L):
        for name, k in PRM_LAYOUT:
            r0 = l * PRM_PER_LAYER + PRM_OFF[name]
            prm[r0:r0 + k] = f(name)[l].reshape(k, 128)
    prm[PRM_FINAL:PRM_FINAL + 8] = f("final_norm").reshape(8, 128)
    w["prm"] = prm
    half = 16
    inv_freq = (10000.0 ** (-np.arange(half, dtype=np.float32) / half)).astype(np.float32)
    ang = np.arange(S, dtype=np.float32)[None, :] * inv_freq[:, None]
    cos, sin = np.cos(ang).astype(np.float32), np.sin(ang).astype(np.float32)
    w["rope"] = np.ascontiguousarray(np.concatenate([cos, cos, -sin, sin], axis=0))
    w["ident"] = np.eye(128, dtype=np.float32)
    return w


def kernel(**inputs):
    x = np.asarray(inputs["x"], dtype=np.float32)
    p = np.asarray(inputs["p"], dtype=np.float32)
    w = _host_weights(inputs)
    if "nc" not in _CACHE:
        _CACHE["nc"] = build()
    nc = _CACHE["nc"]
    in_maps = []
    for c in range(8):
        m = dict(w)
        m["x"] = np.ascontiguousarray(x[2 * c:2 * c + 2].reshape(2 * S, D))
        m["p"] = np.ascontiguousarray(p[:, 2 * c:2 * c + 2].reshape(L, 2 * S, PLE))
        in_maps.append(m)
    res = run_bass_kernel_spmd(nc, in_maps, core_ids=list(range(8)))
    out = np.stack([res.results[c]["out"].reshape(2, S, D) for c in range(8)], axis=0).reshape(16, S, D)
    return np.ascontiguousarray(out.astype(np.float32))
```

```python
from contextlib import ExitStack
import numpy as np
import concourse.bass as bass
import concourse.mybir as mybir
from concourse.bass_utils import run_bass_kernel_spmd

F32 = mybir.dt.float32
BF16 = mybir.dt.bfloat16
ALU = mybir.AluOpType
AF = mybir.ActivationFunctionType

D = 1024
S = 2048
L = 2
DFF = 2816
PLE = 256
TS = 1024
TT = 512
EPS = 1e-6
NH = 16
ENGS = ("pe", "act", "dve", "pool", "sp")
EPOCH = 16000
C_XA, C_GA, C_CQ, C_CKV, C_KR, C_ZU, C_ZV, C_GL = 0, 1024, 2048, 2432, 2688, 2720, 3744, 4768
PRM_LAYOUT = [("ffn1_norm", 8), ("mix_norm", 8), ("ffn2_norm", 8), ("ple_norm", 8), ("conv_w", 32),
              ("conv_b", 8), ("lru_b_a", 8), ("lru_b_x", 8), ("lru_lambda", 8), ("mla_q_norm", 3),
              ("mla_kv_norm", 2), ("sgu_norm_g", 8), ("sgu_norm_b", 8), ("gate_bias", 24)]
PRM_OFF = {}
_o = 0
for _n, _k in PRM_LAYOUT:
    PRM_OFF[_n] = _o
    _o += _k
PRM_PER_LAYER = _o
PRM_FINAL = L * PRM_PER_LAYER
PRM_ROWS = 384
FFN_GROUPS = [(0, 6), (6, 6), (12, 5), (17, 5)]


class Buf:
    __slots__ = ("lw", "rd")

    def __init__(self):
        self.lw = None
        self.rd = {}


class Fw:
    def __init__(self, nc, stack):
        self.nc = nc
        self.stack = stack
        self.q = {e: [] for e in ENGS}
        self.sems = {}
        self.cnt = {}
        self.seen = {e: {} for e in ENGS}
        self.cur = {}
        self.last = {}
        self.pending_sp = {}
        for e in ("pe", "act", "dve", "pool"):
            self._new_epoch(e)

    def new_sem(self, key):
        self.sems[key] = self.stack.enter_context(self.nc.semaphore(key.replace("#", "_")))
        self.cnt[key] = 0

    def _new_epoch(self, e):
        k = "%s#%d" % (e, sum(1 for x in self.sems if x.startswith(e + "#")))
        self.new_sem(k)
        self.cur[e] = k

    @staticmethod
    def _add(deps, t):
        if t is not None and deps.get(t[0], 0) < t[1]:
            deps[t[0]] = t[1]

    def deps(self, reads, writes, extra=()):
        deps = {}
        for b in reads:
            self._add(deps, b.lw)
        for b in writes:
            self._add(deps, b.lw)
            for k, v in b.rd.items():
                if deps.get(k, 0) < v:
                    deps[k] = v
        for t in extra:
            self._add(deps, t)
        return deps

    def wait(self, eng, deps):
        seen = self.seen[eng]
        for k, v in deps.items():
            if eng == "pe" and k.startswith("pe#"):
                continue
            if seen.get(k, 0) >= v:
                continue
            seen[k] = v
            self.q[eng].append(("w", k, v))

    def mark(self, tok, reads, writes):
        k, v = tok
        for b in reads:
            if b.rd.get(k, 0) < v:
                b.rd[k] = v
        for b in writes:
            b.lw = tok
            b.rd = {}

    def raw(self, eng, fn):
        if self.cnt[self.cur[eng]] >= EPOCH:
            self._new_epoch(eng)
        key = self.cur[eng]
        self.cnt[key] += 1
        self.q[eng].append(("o", fn, key, 1))
        self.last[eng] = (key, self.cnt[key])
        return (key, self.cnt[key])

    def op(self, eng, fn, reads=(), writes=(), extra=()):
        self.wait(eng, self.deps(reads, writes, extra))
        tok = self.raw(eng, fn)
        self.mark(tok, reads, writes)
        return tok

    def dma_raw(self, qeng, chan, out, in_):
        if chan not in self.sems:
            self.new_sem(chan)
        self.cnt[chan] += 16
        self.q[qeng].append(("o", lambda E: E.dma_start(out=out, in_=in_), chan, 16))
        return (chan, self.cnt[chan])

    def dma(self, qeng, chan, out, in_, reads=(), writes=(), extra=()):
        self.wait(qeng, self.deps(reads, writes, extra))
        tok = self.dma_raw(qeng, chan, out, in_)
        self.mark(tok, reads, writes)
        if qeng == "sp":
            self.pending_sp[tok[0]] = tok[1]
        return tok

    def barrier(self):
        toks = {}
        for e in ("pe", "act", "dve", "pool"):
            t = self.last.get(e)
            if t is not None:
                toks[t[0]] = t[1]
        toks.update(self.pending_sp)
        self.pending_sp = {}
        for e in ("pe", "act", "dve", "pool", "sp"):
            self.wait(e, toks)

    def emit(self):
        nc = self.nc
        with nc.Block() as block:
            def run(eng):
                def body(E):
                    sems = self.sems
                    for it in self.q[eng]:
                        if it[0] == "w":
                            E.wait_ge(sems[it[1]], it[2])
                        else:
                            it[1](E).then_inc(sems[it[2]], it[3])
                return body
            block.tensor(run("pe"))
            block.scalar(run("act"))
            block.vector(run("dve"))
            block.gpsimd(run("pool"))
            block.sync(run("sp"))


class Carver:
    def __init__(self, scr, base=0):
        self.scr = scr
        self.off = base

    def take(self, shape, dtype):
        n = 1
        for s in shape:
            n *= s
        nbytes = n * (4 if dtype == F32 else 2)
        nbytes = (nbytes + 31) // 32 * 32
        o = self.off
        self.off += nbytes
        v = self.scr[:, o // 4:(o + nbytes) // 4]
        if dtype != F32:
            v = v.bitcast(dtype)
        v = v[:, 0:n]
        if len(shape) == 2:
            v = v.rearrange("p (a b) -> p a b", a=shape[0])
        elif len(shape) == 3:
            v = v.rearrange("p (a b c) -> p a b c", a=shape[0], b=shape[1])
        return v


def build(n_seq=2, n_layers=2, phases=("ffn1", "B", "A", "C", "wout", "ffn2", "ple"), n_seg=2, dbg=()):
    nc = bass.Bass("TRN2", target_bir_lowering=False, dynamic_dma_scratch_size=8192)
    dbg_done = set()
    NT = n_seq * S
    dr = lambda n, s: nc.dram_tensor(n, s, F32, kind="ExternalInput").ap()
    x_d = dr("x", [NT, D])
    p_d = dr("p", [L, NT, PLE])
    out_d = nc.dram_tensor("out", [NT, D], F32, kind="ExternalOutput").ap()
    wgu_d = [dr("ffn1_w_gu", [L, D, 2 * DFF]), dr("ffn2_w_gu", [L, D, 2 * DFF])]
    wdn_d = [dr("ffn1_w_down", [L, DFF, D]), dr("ffn2_w_down", [L, DFF, D])]
    win_d = dr("w_in", [L, D, 7840])
    wattn_d = dr("w_attn", [L, D, 768])
    wuqx_d = dr("w_uqx", [L, 384, 2048])
    wukv_d = dr("mla_w_ukv", [L, 256, 2048])
    wra_d = dr("w_read_a", [L, D, D])
    wrb_d = dr("w_read_b", [L, D, D])
    wrc_d = dr("w_read_c", [L, D, D])
    wout_d = dr("w_out", [L, D, D])
    wpg_d = dr("ple_w_gate", [L, D, D])
    wpp_d = dr("ple_w_proj", [L, PLE, D])
    lwa_d = dr("lru_w_a", [L, 16, 64, 64])
    lwx_d = dr("lru_w_x", [L, 16, 64, 64])
    swT_d = dr("sgu_wT", [L, 128, 8, 128])
    sbs_d = dr("sgu_b_s", [L, 1024])
    prm_d = dr("prm", [PRM_ROWS, 128])
    rope_d = dr("rope", [64, S])
    ident_d = dr("ident", [128, 128])

    st = ExitStack()
    fw = Fw(nc, st)
    sb = lambda n, s, d: st.enter_context(nc.sbuf_tensor(n, s, d))
    WAW = 18432
    WA = [sb("wa%d" % i, [128, WAW], BF16) for i in range(2)]
    WAb = [Buf(), Buf()]
    X = sb("X", [128, 8, TS], F32)
    Xb = [[Buf() for _ in range(2)] for _ in range(8)]
    N = sb("N", [128, 8, TS], BF16)
    Nb = [[Buf() for _ in range(2)] for _ in range(8)]
    CKV = [sb("ckv%d" % l, [128, 2, TS], BF16) for l in range(L)] + [sb("ckvc", [128, 2, TS], BF16)]
    CKVb = [Buf() for _ in range(L + 1)]
    KR = [sb("kr%d" % l, [128, TS], BF16) for l in range(L)] + [sb("krc", [128, TS], BF16)]
    KRb = [Buf() for _ in range(L + 1)]
    PRM = sb("prm_sb", [128, PRM_ROWS], F32); PRMb = Buf()
    IDENT = sb("ident_sb", [128, 128], F32); IDENTb = Buf()
    ONES = sb("ones", [128, 128], BF16); ONESb = Buf()
    CST = sb("cst", [128, 4], F32); CSTb = Buf()
    CLAM = sb("clam", [128, L, 2, 8], F32); CLAMb = Buf()
    HALO = sb("halo", [128, L, 8, 3], F32); HALOb = [[Buf() for _ in range(8)] for _ in range(L)]
    CAR = sb("car", [128, L, 8], F32); CARb = [[Buf() for _ in range(8)] for _ in range(L)]
    SQ = [sb("sq%d" % i, [128, TT], BF16) for i in range(2)]; SQb = [Buf(), Buf()]
    STD = sb("std", [128, TT], F32); STDb = Buf()
    RSTD = [sb("rstd%d" % i, [128, TT], F32) for i in range(2)]; RSTDb = [Buf(), Buf()]
    SCRW = 17152
    SCR = sb("scr", [128, SCRW], F32)
    PSP = [st.enter_context(nc.psum_tensor("psp%d" % i, [128, 2 * TT], F32)) for i in range(2)]
    PS = [PSP[i // 2][:, (i % 2) * TT:(i % 2 + 1) * TT] for i in range(4)]
    PS += [st.enter_context(nc.psum_tensor("ps%d" % i, [128, TT], F32))[:, :] for i in range(4, 8)]
    PSb = [Buf() for _ in range(8)]
    rot = {"a": [0, 1, 2, 3], "b": [4, 5], "c": [6, 7]}
    rotp = {"a": 0, "b": 0, "c": 0}

    def bank(role):
        lst = rot[role]
        i = lst[rotp[role] % len(lst)]
        rotp[role] += 1
        return i

    def mm(bk, out_ap, pairs, reads):
        fw.wait("pe", fw.deps(reads, [PSb[bk]]))
        n = len(pairs)
        tok = None
        for i, (lh, rh) in enumerate(pairs):
            tok = fw.raw("pe", lambda E, lh=lh, rh=rh, i=i: E.matmul(out=out_ap, lhsT=lh, rhs=rh,
                                                                    start=(i == 0), stop=(i == n - 1)))
        fw.mark(tok, reads, [PSb[bk]])
        return tok

    def prm_col(l, name, c):
        j = l * PRM_PER_LAYER + PRM_OFF[name] + c
        return PRM[:, j:j + 1]

    def tsl(tt):
        return slice(tt * TT, (tt + 1) * TT)

    eps_ap = CST[:, 0:1]
    one_ap = CST[:, 1:2]

    fw.dma("sp", "ld_id", IDENT[:], ident_d, writes=[IDENTb])
    fw.op("dve", lambda E: E.memset(ONES[:], 1.0), writes=[ONESb])
    fw.op("dve", lambda E: E.memset(CST[:, 0:1], EPS), writes=[CSTb])
    fw.op("dve", lambda E: E.memset(CST[:, 1:2], 1.0), writes=[CSTb])
    PTMP = SCR[:, 0:PRM_ROWS].rearrange("p (a b) -> p a b", a=3)
    PTMPb = Buf()
    fw.dma("sp", "ld_prm", PTMP, prm_d.rearrange("(a p) f -> p a f", p=128), writes=[PTMPb])
    bk = bank("a")
    for a in range(3):
        fw.op("pe", lambda E, a=a: E.transpose(out=PS[bk][:, a * 128:(a + 1) * 128], in_=PTMP[:, a, :],
                                               identity=IDENT[:]), reads=[PTMPb, IDENTb], writes=[PSb[bk]])
    fw.op("dve", lambda E: E.tensor_copy(out=PRM[:], in_=PS[bk][:, 0:PRM_ROWS]), reads=[PSb[bk]], writes=[PRMb])
    for l in range(n_layers):
        j = l * PRM_PER_LAYER + PRM_OFF["lru_lambda"]
        fw.op("act", lambda E, l=l, j=j: E.activation(out=CLAM[:, l, 0, :], in_=PRM[:, j:j + 8], func=AF.Exp, scale=-1.0),
              reads=[PRMb], writes=[CLAMb])
        fw.op("act", lambda E, l=l: E.activation(out=CLAM[:, l, 0, :], in_=CLAM[:, l, 0, :], func=AF.Ln, bias=one_ap),
              reads=[CLAMb, CSTb], writes=[CLAMb])
        fw.op("dve", lambda E, l=l: E.tensor_scalar(out=CLAM[:, l, 1, :], in0=CLAM[:, l, 0, :], scalar1=-16.0, scalar2=None,
                                                    op0=ALU.mult), reads=[CLAMb], writes=[CLAMb])
        fw.op("dve", lambda E, l=l: E.tensor_scalar(out=CLAM[:, l, 0, :], in0=CLAM[:, l, 0, :], scalar1=-8.0, scalar2=None,
                                                    op0=ALU.mult), reads=[CLAMb], writes=[CLAMb])

    NEGB = sb("negb", [128, L, 2, 8], F32); NEGBb = Buf()
    for l in range(n_layers):
        for jj, nm in enumerate(("lru_b_a", "lru_b_x")):
            j = l * PRM_PER_LAYER + PRM_OFF[nm]
            fw.op("dve", lambda E, l=l, jj=jj, j=j: E.tensor_scalar(out=NEGB[:, l, jj, :], in0=PRM[:, j:j + 8], scalar1=-1.0, scalar2=None,
                                                                    op0=ALU.mult), reads=[PRMb], writes=[NEGBb])

    rk = [0]

    def rmsnorm(src_aps, src_bufs, gain_aps, out_aps, out_bufs, dn, n=TT):
        nch = len(src_aps)
        bk = bank("c")
        fw.wait("pe", fw.deps([], [PSb[bk]]))
        tok = None
        for c in range(nch):
            k = c % 2
            fw.op("act", lambda E, c=c, k=k: E.activation(out=SQ[k][:, 0:n], in_=src_aps[c], func=AF.Square),
                  reads=[src_bufs[c]], writes=[SQb[k]])
            fw.wait("pe", fw.deps([SQb[k], ONESb], []))
            tok = fw.raw("pe", lambda E, c=c, k=k: E.matmul(out=PS[bk][:, 0:n], lhsT=ONES[:], rhs=SQ[k][:, 0:n],
                                                           start=(c == 0), stop=(c == nch - 1)))
            fw.mark(tok, [SQb[k], ONESb], [])
        fw.mark(tok, [], [PSb[bk]])
        fw.op("act", lambda E: E.activation(out=STD[:, 0:n], in_=PS[bk][:, 0:n], func=AF.Sqrt, scale=1.0 / dn, bias=eps_ap),
              reads=[PSb[bk], CSTb], writes=[STDb])
        r = rk[0] % 2
        rk[0] += 1
        fw.op("dve", lambda E: E.reciprocal(out=RSTD[r][:, 0:n], in_=STD[:, 0:n]), reads=[STDb], writes=[RSTDb[r]])
        for c in range(nch):
            fw.op("dve", lambda E, c=c: E.scalar_tensor_tensor(out=out_aps[c], in0=src_aps[c], scalar=gain_aps[c],
                                                               in1=RSTD[r][:, 0:n], op0=ALU.mult, op1=ALU.mult),
                  reads=[src_bufs[c], RSTDb[r], PRMb], writes=[out_bufs[c]])

    def norm_x(gname, l):
        for tt in range(2):
            rmsnorm([X[:, c, tsl(tt)] for c in range(8)], [Xb[c][tt] for c in range(8)],
                    [prm_col(l, gname, c) for c in range(8)],
                    [N[:, c, tsl(tt)] for c in range(8)], [Nb[c][tt] for c in range(8)], D)

    def wview(W, off, shape):
        n = 1
        for s_ in shape:
            n *= s_
        v = W[:, off:off + n]
        if len(shape) == 2:
            v = v.rearrange("p (a b) -> p a b", a=shape[0])
        elif len(shape) == 3:
            v = v.rearrange("p (a b c) -> p a b c", a=shape[0], b=shape[1])
        return v, off + n

    def kview(src2d):
        return src2d.rearrange("(c p) f -> p c f", p=128)

    stages = []

    def st_load_x(sq, seg):
        def f(W, Wb, Wo, Wob):
            def comp():
                fw.barrier()
                cv = Carver(SCR)
                XT = [cv.take([D], F32) for _ in range(2)]
                XTb = [Buf(), Buf()]
                for l in range(n_layers):
                    if seg == 0:
                        fw.op("dve", lambda E, l=l: E.memset(HALO[:, l, :, :], 0.0), writes=HALOb[l])
                        fw.op("dve", lambda E, l=l: E.memset(CAR[:, l, :], 0.0), writes=CARb[l])
                for tb in range(8):
                    k = tb % 2
                    r0 = sq * S + seg * TS + tb * 128
                    fw.dma("sp", "xl%d" % k, XT[k], x_d[r0:r0 + 128, :], writes=[XTb[k]])
                    for hh in range(2):
                        bk = bank("a")
                        fw.wait("pe", fw.deps([XTb[k], IDENTb], [PSb[bk]]))
                        tok = None
                        for c4 in range(4):
                            c = hh * 4 + c4
                            tok = fw.raw("pe", lambda E, c=c, c4=c4, k=k, bk=bk: E.transpose(
                                out=PS[bk][:, c4 * 128:(c4 + 1) * 128], in_=XT[k][:, c * 128:(c + 1) * 128], identity=IDENT[:]))
                        fw.mark(tok, [XTb[k], IDENTb], [PSb[bk]])
                        tt = tb // 4
                        dst = X[:, hh * 4:hh * 4 + 4, tb * 128:(tb + 1) * 128]
                        src = PS[bk][:, :].rearrange("p (a b) -> p a b", a=4)
                        wr = [Xb[c][tt] for c in range(hh * 4, hh * 4 + 4)]
                        if hh == 0:
                            fw.op("act", lambda E, dst=dst, src=src: E.activation(out=dst, in_=src, func=AF.Copy),
                                  reads=[PSb[bk]], writes=wr)
                        else:
                            fw.op("dve", lambda E, dst=dst, src=src: E.tensor_copy(out=dst, in_=src),
                                  reads=[PSb[bk]], writes=wr)
            return [], comp
        f._name = "load_x"
        return f

    out_toks = []

    def tap(name, ap, bufs):
        if name not in dbg or name in dbg_done:
            return
        dbg_done.add(name)
        dd = nc.dram_tensor("dbg_" + name, list(ap.shape), ap.dtype, kind="ExternalOutput").ap()
        out_toks.append(fw.dma("sp", "dbg", dd, ap, reads=bufs))

    def st_store(sq, seg):
        def f(W, Wb, Wo, Wob):
            def comp():
                fw.barrier()
                cv = Carver(SCR)
                Y = cv.take([8, TT], F32); Yb = [Buf() for _ in range(8)]
                OTK = [cv.take([D], F32) for _ in range(2)]; OTKb = [Buf(), Buf()]
                for tt in range(2):
                    rmsnorm([X[:, c, tsl(tt)] for c in range(8)], [Xb[c][tt] for c in range(8)],
                            [PRM[:, PRM_FINAL + c:PRM_FINAL + c + 1] for c in range(8)],
                            [Y[:, c, :] for c in range(8)], Yb, D)
                    for blk in range(4):
                        k = blk % 2
                        for hh in range(2):
                            bk = bank("a")
                            rd = [Yb[c] for c in range(hh * 4, hh * 4 + 4)] + [IDENTb]
                            fw.wait("pe", fw.deps(rd, [PSb[bk]]))
                            tok = None
                            for c4 in range(4):
                                c = hh * 4 + c4
                                tok = fw.raw("pe", lambda E, c=c, c4=c4, bk=bk, blk=blk: E.transpose(
                                    out=PS[bk][:, c4 * 128:(c4 + 1) * 128], in_=Y[:, c, blk * 128:(blk + 1) * 128], identity=IDENT[:]))
                            fw.mark(tok, rd, [PSb[bk]])
                            dst = OTK[k][:, hh * 512:(hh + 1) * 512]
                            if hh == 0:
                                fw.op("act", lambda E, dst=dst, bk=bk: E.activation(out=dst, in_=PS[bk][:, :], func=AF.Copy),
                                      reads=[PSb[bk]], writes=[OTKb[k]])
                            else:
                                fw.op("dve", lambda E, dst=dst, bk=bk: E.tensor_copy(out=dst, in_=PS[bk][:, :]),
                                      reads=[PSb[bk]], writes=[OTKb[k]])
                        r0 = sq * S + seg * TS + tt * TT + blk * 128
                        out_toks.append(fw.dma("sp", "st%d" % k, out_d[r0:r0 + 128, :], OTK[k], reads=[OTKb[k]]))
            return [], comp
        f._name = "store"
        return f

    def st_ffn(l, which, gi):
        f0, nf = FFN_GROUPS[gi]
        gname = "ffn1_norm" if which == 0 else "ffn2_norm"

        def f(W, Wb, Wo, Wob):
            wg, o = wview(W, 0, [8, nf * 128])
            wu, o = wview(W, o, [8, nf * 128])
            wd, o = wview(W, o, [nf, D])
            dmas = [(wg, kview(wgu_d[which][l, :, f0 * 128:(f0 + nf) * 128])),
                    (wu, kview(wgu_d[which][l, :, DFF + f0 * 128:DFF + (f0 + nf) * 128])),
                    (wd, wdn_d[which][l, f0 * 128:(f0 + nf) * 128, :].rearrange("(f p) d -> p f d", p=128))]

            def comp():
                cv = Carver(SCR)
                H = cv.take([2, 6, TT], BF16)
                SG = [cv.take([TT], F32) for _ in range(2)]
                if gi == 0:
                    norm_x(gname, l)
                    fw.barrier()
                    st_ffn.Hb = [[Buf() for _ in range(6)] for _ in range(2)]
                    st_ffn.SGb = [Buf(), Buf()]
                Hb, SGb = st_ffn.Hb, st_ffn.SGb
                k = 0
                for tt in range(2):
                    nrd = [Nb[c][tt] for c in range(8)] + [Wb]
                    for fi in range(nf):
                        bg, bu = bank("a"), bank("a")
                        mm(bg, PS[bg][:, :], [(wg[:, c, fi * 128:(fi + 1) * 128], N[:, c, tsl(tt)]) for c in range(8)], nrd)
                        mm(bu, PS[bu][:, :], [(wu[:, c, fi * 128:(fi + 1) * 128], N[:, c, tsl(tt)]) for c in range(8)], nrd)
                        kk = k % 2
                        k += 1
                        fw.op("act", lambda E, kk=kk, bg=bg: E.activation(out=SG[kk], in_=PS[bg][:, :], func=AF.Silu),
                              reads=[PSb[bg]], writes=[SGb[kk]])
                        fw.op("dve", lambda E, kk=kk, bu=bu, tt=tt, fi=fi: E.tensor_tensor(
                            out=H[:, tt, fi, :], in0=SG[kk], in1=PS[bu][:, :], op=ALU.mult),
                            reads=[SGb[kk], PSb[bu]], writes=[Hb[tt][fi]])
                for tt in range(2):
                    hrd = [Hb[tt][fi] for fi in range(nf)] + [Wb]
                    for d in range(8):
                        bd = bank("b")
                        mm(bd, PS[bd][:, :], [(wd[:, fi, d * 128:(d + 1) * 128], H[:, tt, fi, :]) for fi in range(nf)], hrd)
                        fw.op("dve", lambda E, d=d, tt=tt, bd=bd: E.scalar_tensor_tensor(
                            out=X[:, d, tsl(tt)], in0=PS[bd][:, :], scalar=0.5, in1=X[:, d, tsl(tt)],
                            op0=ALU.mult, op1=ALU.add), reads=[PSb[bd], Xb[d][tt]], writes=[Xb[d][tt]])
            return dmas, comp
        f._name = "ffn"
        return f

    MGcv = Carver(SCR)
    MG = MGcv.take([8, TS], BF16)
    MGb = [[Buf() for _ in range(2)] for _ in range(8)]
    MIX_BASE = MGcv.off
    mix_first = [True]

    def gate_merge(l, br, tt, yw, Y, Yb, glw, Wbs, TMPG, TMPGb, GT, GTb):
        first = mix_first[0]
        for d in range(8):
            by, bgl = bank("a"), bank("a")
            mm(by, PS[by][:, :], [(yw[:, c, d * 128:(d + 1) * 128], Y[:, c, :]) for c in range(8)], Yb + Wbs)
            mm(bgl, PS[bgl][:, :], [(glw[:, c, d * 128:(d + 1) * 128], N[:, c, tsl(tt)]) for c in range(8)],
               [Nb[c][tt] for c in range(8)] + Wbs)
            k = d % 2
            fw.op("act", lambda E, k=k, bgl=bgl, d=d: E.activation(out=GT[k], in_=PS[bgl][:, :], func=AF.Sigmoid,
                                                                   bias=prm_col(l, "gate_bias", br * 8 + d)),
                  reads=[PSb[bgl], PRMb], writes=[GTb[k]])
            if first:
                fw.op("dve", lambda E, k=k, by=by, d=d: E.tensor_tensor(out=MG[:, d, tsl(tt)], in0=GT[k], in1=PS[by][:, :], op=ALU.mult),
                      reads=[GTb[k], PSb[by]], writes=[MGb[d][tt]])
            else:
                fw.op("dve", lambda E, k=k, by=by: E.tensor_tensor(out=TMPG[k], in0=GT[k], in1=PS[by][:, :], op=ALU.mult),
                      reads=[GTb[k], PSb[by]], writes=[TMPGb[k]])
                fw.op("pool", lambda E, k=k, d=d: E.tensor_tensor(out=MG[:, d, tsl(tt)], in0=MG[:, d, tsl(tt)], in1=TMPG[k], op=ALU.add),
                      reads=[TMPGb[k], MGb[d][tt]], writes=[MGb[d][tt]])

    def st_B1(l, sq, seg):
        def f(W, Wb, Wo, Wob):
            wat, o = wview(W, 0, [8, 768])
            wuq, o = wview(W, o, [3, 2048])
            wkv, o = wview(W, o, [2, 2048])
            dmas = [(wat, kview(wattn_d[l])), (wuq, kview(wuqx_d[l])), (wkv, kview(wukv_d[l]))]
            wrb, o2 = wview(Wo, 0, [8, D])
            wgl, o2 = wview(Wo, o2, [8, D])

            def comp():
                norm_x("mix_norm", l)
                fw.barrier()
                cv = Carver(SCR, MIX_BASE)
                CQN = cv.take([3, TS], BF16); CQNb = [[Buf(), Buf()] for _ in range(3)]
                Q = [cv.take([TT], BF16) for _ in range(2)]; Qb = [Buf(), Buf()]
                K = [cv.take([S], BF16) for _ in range(2)]; Kb = [Buf(), Buf()]
                VE = cv.take([16, 2, 128], BF16); VEb = [Buf(), Buf()]
                PT = [cv.take([2 * TT], BF16) for _ in range(2)]; PTb = [Buf() for _ in range(2)]
                PTD = [cv.take([TT], BF16) for _ in range(2)]; PTDb = [Buf(), Buf()]
                OT = cv.take([8, TT], BF16); OTb = [Buf() for _ in range(8)]
                TAB = cv.take([TS], F32); TABb = Buf()
                T1 = cv.take([TT], F32); T1b = Buf()
                T2 = cv.take([TT], F32); T2b = Buf()
                TQ1 = [T1, T1]; TQ1b = [T1b, T1b]
                TQ2 = [T2, T2]; TQ2b = [T2b, T2b]
                RC = cv.take([TT], F32); RCb = Buf()
                MK = cv.take([128], BF16); MQ = cv.take([TT], BF16); MKb = Buf()
                assert cv.off <= SCRW * 4, cv.off
                fw.op("dve", lambda E: E.memset(MK[:, :], 0.0), writes=[MKb])
                fw.op("dve", lambda E: E.memset(MQ[:, :], 0.0), writes=[MKb])
                fw.op("dve", lambda E: E.memset(MK[0:1, 64:128], -30000.0), writes=[MKb])
                fw.op("dve", lambda E: E.memset(MQ[0:1, 0:64], 1.0), writes=[MKb])
                cdst = l if seg == 0 else L
                scale = float((64 + 32) ** -0.5)
                fw.dma("sp", "ld_tab", TAB[64:128, :], rope_d[:, seg * TS:(seg + 1) * TS], writes=[TABb])
                fw.op("pool", lambda E: E.memset(VE[:, :, :, 64:128], 1.0), writes=VEb)
                for k in range(2):
                    fw.op("pool", lambda E, k=k: E.memset(Q[k][64:128, :], 0.0), writes=[Qb[k]])
                    fw.op("pool", lambda E, k=k: E.memset(K[k][64:128, :], 0.0), writes=[Kb[k]])
                for k in range(2):
                    fw.op("pool", lambda E, k=k: E.memset(PTD[k][64:128, 0:64], 0.0), writes=[PTDb[k]])
                for tt in range(2):
                    nrd = [Nb[c][tt] for c in range(8)] + [Wb]
                    bks = [bank("a") for _ in range(3)]
                    for j in range(3):
                        mm(bks[j], PS[bks[j]][:, :], [(wat[:, c, j * 128:(j + 1) * 128], N[:, c, tsl(tt)]) for c in range(8)], nrd)
                    rmsnorm([PS[bks[j]][:, :] for j in range(3)], [PSb[bks[j]] for j in range(3)],
                            [prm_col(l, "mla_q_norm", j) for j in range(3)],
                            [CQN[:, j, tsl(tt)] for j in range(3)], [CQNb[j][tt] for j in range(3)], 384)
                    bks = [bank("a") for _ in range(2)]
                    for j in range(2):
                        mm(bks[j], PS[bks[j]][:, :], [(wat[:, c, 384 + j * 128:384 + (j + 1) * 128], N[:, c, tsl(tt)]) for c in range(8)], nrd)
                    rmsnorm([PS[bks[j]][:, :] for j in range(2)], [PSb[bks[j]] for j in range(2)],
                            [prm_col(l, "mla_kv_norm", j) for j in range(2)],
                            [CKV[cdst][:, j, tsl(tt)] for j in range(2)], [CKVb[cdst], CKVb[cdst]], 256)
                    bkr = bank("a")
                    mm(bkr, PS[bkr][:, :], [(wat[:, c, 640:768], N[:, c, tsl(tt)]) for c in range(8)], nrd)
                    fw.op("dve", lambda E, bkr=bkr, tt=tt: E.tensor_tensor(out=T1[64:96, :], in0=PS[bkr][64:96, :], in1=TAB[64:96, tsl(tt)], op=ALU.mult),
                          reads=[PSb[bkr], TABb], writes=[T1b])
                    fw.op("dve", lambda E, bkr=bkr, tt=tt: E.tensor_tensor(out=T2[64:96, :], in0=PS[bkr][96:128, :], in1=TAB[96:128, tsl(tt)], op=ALU.mult),
                          reads=[PSb[bkr], TABb], writes=[T2b])
                    fw.op("pool", lambda E, tt=tt: E.tensor_tensor(out=KR[cdst][64:96, tsl(tt)], in0=T1[64:96, :], in1=T2[64:96, :], op=ALU.add),
                          reads=[T1b, T2b], writes=[KRb[cdst]])
                tap("cqn", CQN, [b for bb in CQNb for b in bb])
                tap("ckv", CKV[cdst][:], [CKVb[cdst]])
                tap("kr", KR[cdst][64:96, :], [KRb[cdst]])
                ksrc = [l] if seg == 0 else [l, L]
                nhalf = len(ksrc)
                for kb_ in range(2):
                    for hf in range(nhalf):
                        fw.op("act", lambda E, kb_=kb_, hf=hf: E.activation(out=K[kb_][64:96, hf * TS:(hf + 1) * TS],
                                                                            in_=KR[ksrc[hf]][64:96, :], func=AF.Copy),
                              reads=[KRb[ksrc[hf]]], writes=[Kb[kb_]])

                def lat(kc, k0, n):
                    hf = k0 // TS
                    return CKV[ksrc[hf]][:, kc, k0 - hf * TS:k0 - hf * TS + n], CKVb[ksrc[hf]]

                ptc = [0]
                ptdc = [0]
                pairc = [0]
                order = [(tt, h) for tt in range(2) for h in range(NH)]

                def produce(i):
                    pieces = []
                    tt, h = order[i]
                    kk = i % 2
                    gq = seg * 2 + tt
                    kend = (gq + 1) * TT
                    nkb = kend // 128
                    lrd = [CKVb[j] for j in ksrc] + [Wb]
                    def piece_v(kb0):
                        nb8 = min(8, nkb - kb0)
                        bv = bank("c")
                        fw.wait("pe", fw.deps(lrd, [PSb[bv]]))
                        tok = None
                        for q8 in range(nb8):
                            kb = kb0 + q8
                            for kc in range(2):
                                la, lab = lat(kc, kb * 128, 128)
                                tok = fw.raw("pe", lambda E, la=la, kc=kc, q8=q8, bv=bv, h=h: E.matmul(
                                    out=PS[bv][:, q8 * 64:(q8 + 1) * 64], lhsT=la, rhs=wkv[:, kc, h * 128 + 64:(h + 1) * 128],
                                    start=(kc == 0), stop=(kc == 1)))
                        fw.mark(tok, lrd, [PSb[bv]])
                        fw.op("dve", lambda E, kb0=kb0, nb8=nb8, bv=bv, kk=kk: E.tensor_copy(
                            out=VE[:, kb0:kb0 + nb8, kk, 0:64],
                            in_=PS[bv][:, 0:nb8 * 64].rearrange("p (a e) -> p a e", a=nb8)), reads=[PSb[bv]], writes=[VEb[kk]])
                    for kb0 in range(0, nkb, 8):
                        pieces.append(lambda kb0=kb0: piece_v(kb0))

                    def piece_k(kt):
                        bkk = bank("c")
                        pairs = []
                        for kc in range(2):
                            la, lab = lat(kc, kt * TT, TT)
                            pairs.append((wkv[:, kc, h * 128:(h + 1) * 128], la))
                        mm(bkk, PS[bkk][:, :], pairs, lrd)
                        fw.op("dve", lambda E, kk=kk, kt=kt, bkk=bkk: E.tensor_copy(out=K[kk][0:64, kt * TT:(kt + 1) * TT],
                                                                               in_=PS[bkk][0:64, :]),
                              reads=[PSb[bkk]], writes=[Kb[kk]])
                    for kt in range(kend // TT):
                        pieces.append(lambda kt=kt: piece_k(kt))

                    def piece_q():
                        bq = bank("c")
                        mm(bq, PS[bq][:, :], [(wuq[:, kc, h * 128:(h + 1) * 128], CQN[:, kc, tsl(tt)]) for kc in range(3)],
                           [CQNb[kc][tt] for kc in range(3)] + [Wb])
                        piece_q2(bq)

                    def piece_q2(bq):
                        fw.op("dve", lambda E, kk=kk, bq=bq: E.tensor_copy(out=Q[kk][0:64, :], in_=PS[bq][0:64, :]),
                              reads=[PSb[bq]], writes=[Qb[kk]])
                        fw.op("dve", lambda E, bq=bq, tt=tt, kk=kk: E.tensor_tensor(out=TQ1[kk][64:96, :], in0=PS[bq][64:96, :], in1=TAB[64:96, tsl(tt)], op=ALU.mult),
                              reads=[PSb[bq], TABb], writes=[TQ1b[kk]])
                        fw.op("dve", lambda E, bq=bq, tt=tt, kk=kk: E.tensor_tensor(out=TQ2[kk][64:96, :], in0=PS[bq][96:128, :], in1=TAB[96:128, tsl(tt)], op=ALU.mult),
                              reads=[PSb[bq], TABb], writes=[TQ2b[kk]])
                        fw.op("pool", lambda E, kk=kk: E.tensor_tensor(out=Q[kk][64:96, :], in0=TQ1[kk][64:96, :], in1=TQ2[kk][64:96, :], op=ALU.add),
                              reads=[TQ1b[kk], TQ2b[kk]], writes=[Qb[kk]])
                    pieces = [piece_q] + pieces[len(range(0, nkb, 8)):] + pieces[:len(range(0, nkb, 8))]
                    return pieces

                def attend(i, pieces, prev_norm):
                    tt, h = order[i]
                    kk = i % 2
                    gq = seg * 2 + tt
                    nkb = (gq + 1) * TT // 128
                    hp, hh = h // 2, h % 2
                    bo = bank("b")

                    def issue_pv(item, last):
                        kb, q_lo, pt, ptb, first = item
                        fw.wait("pe", fw.deps([ptb, VEb[kk]], [PSb[bo]] if first else []))
                        tok = fw.raw("pe", lambda E, kb=kb, q_lo=q_lo, pt=pt, first=first, last=last: E.matmul(
                            out=PS[bo][:, q_lo:TT], lhsT=VE[:, kb, kk, :], rhs=pt, start=first, stop=last))
                        fw.mark(tok, [ptb, VEb[kk]], [PSb[bo]])

                    nd = gq * 4
                    units = [("pair", j) for j in range(0, nd, 2)] + [("diag", kb) for kb in range(nd, nkb)]
                    prev = None
                    for u in units:
                        if u[0] == "pair":
                            kb = u[1]
                            p = pairc[0] % 2
                            pairc[0] += 1
                            for j in range(2):
                                mm(2 * p + j, PS[2 * p + j][:, :], [(K[kk][:, (kb + j) * 128:(kb + j + 1) * 128], Q[kk][:, :])], [Kb[kk], Qb[kk]])
                            pi = ptc[0] % 2
                            ptc[0] += 1
                            pt, ptb = PT[pi], PTb[pi]
                            fw.op("act", lambda E, pt=pt, p=p: E.activation(out=pt[:, :], in_=PSP[p][:, :], func=AF.Exp, scale=scale),
                                  reads=[PSb[2 * p], PSb[2 * p + 1]], writes=[ptb])
                            items = [(kb, 0, pt[:, 0:TT], ptb, kb == 0), (kb + 1, 0, pt[:, TT:2 * TT], ptb, False)]
                        else:
                            kb = u[1]
                            q_lo = kb * 128 - gq * TT
                            nq = TT - q_lo
                            bs = bank("a")
                            mm(bs, PS[bs][:, 0:nq], [(K[kk][:, kb * 128:(kb + 1) * 128], Q[kk][:, q_lo:TT]), (MK[:, :], MQ[:, 0:nq])],
                               [Kb[kk], Qb[kk], MKb])
                            j = ptdc[0] % 2
                            ptdc[0] += 1
                            pt, ptb = PTD[j], PTDb[j]
                            fw.op("act", lambda E, pt=pt, bs=bs, nq=nq: E.activation(out=pt[:, 0:nq], in_=PS[bs][:, 0:nq], func=AF.Exp, scale=scale),
                                  reads=[PSb[bs]], writes=[ptb])
                            items = [(kb, q_lo, pt[:, 0:nq], ptb, kb == 0)]
                        if pieces:
                            pieces.pop(0)()
                        if prev is not None:
                            for it in prev:
                                issue_pv(it, False)
                            if prev_norm is not None:
                                prev_norm()
                                prev_norm = None
                        prev = items
                    while pieces:
                        pieces.pop(0)()
                    if prev_norm is not None:
                        prev_norm()
                    for idx, it in enumerate(prev):
                        issue_pv(it, idx == len(prev) - 1)
                    def norm_fn():
                        fw.op("act", lambda E: E.activation(out=RC[64:128, :], in_=PS[bo][64:128, :], func=AF.Ln),
                              reads=[PSb[bo]], writes=[RCb])
                        fw.op("act", lambda E: E.activation(out=RC[64:128, :], in_=RC[64:128, :], func=AF.Exp, scale=-1.0),
                              reads=[RCb], writes=[RCb])
                        fw.op("dve", lambda E: E.tensor_copy(out=RC[0:64, :], in_=RC[64:128, :]), reads=[RCb], writes=[RCb])
                        fw.op("dve", lambda E: E.tensor_tensor(out=OT[hh * 64:(hh + 1) * 64, hp, :], in0=PS[bo][0:64, :],
                                                               in1=RC[0:64, :], op=ALU.mult),
                              reads=[PSb[bo], RCb], writes=[OTb[hp]])
                    return norm_fn

                for pc in produce(0):
                    pc()
                pnorm = None
                for i in range(len(order)):
                    pnorm = attend(i, produce(i + 1) if i + 1 < len(order) else [], pnorm)
                    tt, h = order[i]
                    if h == NH - 1:
                        pnorm()
                        pnorm = None
                        assert mix_first[0]
                        gate_merge(l, 1, tt, wrb, OT, OTb, wgl, [Wob], None, None, [T1, T2], [T1b, T2b])
                mix_first[0] = False
            return dmas, comp
        f._name = "B1"
        return f

    def st_B2(l):
        def f(W, Wb, Wo, Wob):
            wrb, o = wview(W, 0, [8, D])
            wgl, o = wview(W, o, [8, D])
            dmas = [(wrb, kview(wrb_d[l])), (wgl, kview(win_d[l, :, C_GL + D:C_GL + 2 * D]))]
            return dmas, (lambda: None)
        f._name = "B2"
        return f

    def st_A1(l, sq, seg):
        def f(W, Wb, Wo, Wob):
            wxa, o = wview(W, 0, [8, D])
            wga, o = wview(W, o, [8, D])
            lwa, o = wview(W, o, [8, 128])
            lwx, o = wview(W, o, [8, 128])
            dmas = [(wxa, kview(win_d[l, :, C_XA:C_XA + D])), (wga, kview(win_d[l, :, C_GA:C_GA + D]))]
            for j in range(2):
                dmas.append((lwa[64 * j:64 * j + 64, :, 64 * j:64 * j + 64],
                             lwa_d[l].rearrange("(n two) c d -> two c n d", two=2)[j]))
                dmas.append((lwx[64 * j:64 * j + 64, :, 64 * j:64 * j + 64],
                             lwx_d[l].rearrange("(n two) c d -> two c n d", two=2)[j]))
            wra, o2 = wview(Wo, 0, [8, D])
            wgl, o2 = wview(Wo, o2, [8, D])
            pre = [("z", lwa[0:64, :, 64:128]), ("z", lwa[64:128, :, 0:64]), ("z", lwx[0:64, :, 64:128]), ("z", lwx[64:128, :, 0:64])]

            def comp():
                fw.barrier()
                cv = Carver(SCR, MIX_BASE)
                NS = 3
                XA = [cv.take([TT + 3], F32) for _ in range(NS)]; XAb = [Buf() for _ in range(NS)]
                XC = [cv.take([TT], F32) for _ in range(NS)]; XCb = [Buf() for _ in range(NS)]
                XCH = [cv.take([TT], BF16) for _ in range(NS)]; XCHb = [Buf() for _ in range(NS)]
                RI = [cv.take([2, TT], F32) for _ in range(NS)]; RIb = [Buf() for _ in range(NS)]
                T = [cv.take([TT], F32) for _ in range(NS)]; Tb = [Buf() for _ in range(NS)]
                HH, HHb = T, Tb
                GG8 = cv.take([8, TT], BF16); GG8b = [Buf() for _ in range(8)]
                AT = cv.take([8, TT], BF16); ATb = [Buf() for _ in range(8)]
                GT, GTb = XC[0:2], XCb[0:2]
                TM, TMb = [RI[0][:, 0, :], RI[0][:, 1, :]], [RIb[0], RIb[0]]
                slot = [0]
                assert cv.off <= SCRW * 4, cv.off
                if "B" not in phases:
                    norm_x("mix_norm", l)
                for tt in range(2):
                    nrd = [Nb[c][tt] for c in range(8)] + [Wb]
                    for c in range(8):
                        bg = bank("a")
                        mm(bg, PS[bg][:, :], [(wga[:, kc, c * 128:(c + 1) * 128], N[:, kc, tsl(tt)]) for kc in range(8)], nrd)
                        fw.op("act", lambda E, c=c, bg=bg: E.activation(out=GG8[:, c, :], in_=PS[bg][:, :], func=AF.Gelu_apprx_tanh),
                              reads=[PSb[bg]], writes=[GG8b[c]])
                    base = slot[0]
                    slot[0] += 8

                    def stage_a(c):
                        k = (base + c) % NS
                        bx = bank("a")
                        mm(bx, PS[bx][:, :], [(wxa[:, kc, c * 128:(c + 1) * 128], N[:, kc, tsl(tt)]) for kc in range(8)], nrd)
                        fw.op("pool", lambda E, k=k, c=c: E.tensor_copy(out=XA[k][:, 0:3], in_=HALO[:, l, c, :]),
                              reads=[HALOb[l][c]], writes=[XAb[k]])
                        fw.op("dve", lambda E, k=k, bx=bx: E.tensor_copy(out=XA[k][:, 3:TT + 3], in_=PS[bx][:, :]),
                              reads=[PSb[bx]], writes=[XAb[k]])
                        fw.op("pool", lambda E, k=k, c=c: E.tensor_copy(out=HALO[:, l, c, :], in_=XA[k][:, TT:TT + 3]),
                              reads=[XAb[k]], writes=[HALOb[l][c]])
                        cw = lambda tap, c=c: prm_col(l, "conv_w", tap * 8 + c)
                        fw.op("dve", lambda E, k=k, c=c, cw=cw: E.tensor_scalar(out=XC[k], in0=XA[k][:, 0:TT], scalar1=cw(0),
                                                                             scalar2=prm_col(l, "conv_b", c), op0=ALU.mult, op1=ALU.add),
                              reads=[XAb[k], PRMb], writes=[XCb[k]])
                        for tap in range(1, 4):
                            fw.op("dve", lambda E, k=k, tap=tap, cw=cw: E.scalar_tensor_tensor(
                                out=XC[k], in0=XA[k][:, tap:tap + TT], scalar=cw(tap), in1=XC[k], op0=ALU.mult, op1=ALU.add),
                                reads=[XAb[k], XCb[k], PRMb], writes=[XCb[k]])
                        fw.op("dve", lambda E, k=k: E.tensor_copy(out=XCH[k], in_=XC[k]), reads=[XCb[k]], writes=[XCHb[k]])

                    def stage_b(c):
                        k = (base + c) % NS
                        R = RI[k][:, 0, :]
                        I = RI[k][:, 1, :]
                        RI2 = RI[k][:, :, :].rearrange("p a t -> p (a t)")
                        br_, bi_ = bank("c"), bank("c")
                        mm(br_, PS[br_][:, :], [(lwa[:, c, :], XCH[k])], [XCHb[k], Wb])
                        mm(bi_, PS[bi_][:, :], [(lwx[:, c, :], XCH[k])], [XCHb[k], Wb])
                        fw.op("act", lambda E, R=R, br_=br_, c=c: E.activation(out=R, in_=PS[br_][:, :], func=AF.Exp, scale=-1.0, bias=NEGB[:, l, 0, c:c + 1]),
                              reads=[PSb[br_], NEGBb], writes=[RIb[k]])
                        fw.op("act", lambda E, I=I, bi_=bi_, c=c: E.activation(out=I, in_=PS[bi_][:, :], func=AF.Exp, scale=-1.0, bias=NEGB[:, l, 1, c:c + 1]),
                              reads=[PSb[bi_], NEGBb], writes=[RIb[k]])
                        fw.op("act", lambda E, RI2=RI2: E.activation(out=RI2, in_=RI2, func=AF.Ln, bias=one_ap), reads=[RIb[k], CSTb], writes=[RIb[k]])
                        fw.op("act", lambda E, RI2=RI2: E.activation(out=RI2, in_=RI2, func=AF.Exp, scale=-1.0), reads=[RIb[k]], writes=[RIb[k]])
                        fw.op("act", lambda E, R=R, c=c: E.activation(out=R, in_=R, func=AF.Exp, scale=CLAM[:, l, 0, c:c + 1]),
                              reads=[RIb[k], CLAMb], writes=[RIb[k]])
                        fw.op("act", lambda E, k=k, R=R: E.activation(out=T[k], in_=R, func=AF.Square),
                              reads=[RIb[k]], writes=[Tb[k]])
                        fw.op("act", lambda E, k=k: E.activation(out=T[k], in_=T[k], func=AF.Ln, scale=-1.0, bias=one_ap),
                              reads=[Tb[k], CSTb], writes=[Tb[k]])
                        fw.op("act", lambda E, k=k: E.activation(out=T[k], in_=T[k], func=AF.Exp, scale=0.5),
                              reads=[Tb[k]], writes=[Tb[k]])

                    def stage_b2(c):
                        k = (base + c) % NS
                        R = RI[k][:, 0, :]
                        I = RI[k][:, 1, :]
                        if seg == 0 and tt == 0:
                            fw.op("dve", lambda E, k=k: E.memset(T[k][:, 0:1], 1.0), reads=[], writes=[Tb[k]])
                        fw.op("pool", lambda E, k=k, I=I: E.tensor_tensor(out=I, in0=I, in1=T[k], op=ALU.mult),
                              reads=[RIb[k], Tb[k]], writes=[RIb[k]])
                        fw.op("pool", lambda E, k=k, I=I: E.tensor_tensor(out=I, in0=I, in1=XC[k], op=ALU.mult),
                              reads=[RIb[k], XCb[k]], writes=[RIb[k]])
                        fw.op("dve", lambda E, k=k, c=c, R=R, I=I: E.tensor_tensor_scan(out=HH[k], data0=R, data1=I, initial=CAR[:, l, c:c + 1],
                                                                                      op0=ALU.mult, op1=ALU.add),
                              reads=[RIb[k], CARb[l][c], Tb[k]], writes=[HHb[k]])
                        fw.op("dve", lambda E, k=k, c=c: E.tensor_copy(out=CAR[:, l, c:c + 1], in_=HH[k][:, TT - 1:TT]),
                              reads=[HHb[k]], writes=[CARb[l][c]])
                        fw.op("pool", lambda E, k=k, c=c: E.tensor_tensor(out=AT[:, c, :], in0=HH[k], in1=GG8[:, c, :], op=ALU.mult),
                              reads=[HHb[k], GG8b[c]], writes=[ATb[c]])

                    for i in range(8 + 2):
                        if i < 8:
                            stage_a(i)
                        if 1 <= i < 9:
                            stage_b(i - 1)
                        if i >= 2:
                            stage_b2(i - 2)
                    gate_merge(l, 0, tt, wra, AT, ATb, wgl, [Wob], TM, TMb, GT, GTb)
                mix_first[0] = False
            return dmas, comp, pre
        f._name = "A1"
        return f

    def st_A2(l):
        def f(W, Wb, Wo, Wob):
            wra, o = wview(W, 0, [8, D])
            wgl, o = wview(W, o, [8, D])
            dmas = [(wra, kview(wra_d[l])), (wgl, kview(win_d[l, :, C_GL:C_GL + D]))]
            return dmas, (lambda: None)
        f._name = "A2"
        return f

    def st_C1(l, sq, seg):
        def f(W, Wb, Wo, Wob):
            wzu, o = wview(W, 0, [8, D])
            wzv, o = wview(W, o, [8, D])
            wmT, o = wview(W, o, [8, 128])
            dmas = [(wzu, kview(win_d[l, :, C_ZU:C_ZU + D])), (wzv, kview(win_d[l, :, C_ZV:C_ZV + D])), (wmT, swT_d[l])]
            wrc, o2 = wview(Wo, 0, [8, D])
            wgl, o2 = wview(Wo, o2, [8, D])
            post = [("z", wmT[64:128, :, 0:64])]

            def comp():
                fw.barrier()
                cv = Carver(SCR, MIX_BASE)
                UT = cv.take([8, TT], BF16); UTb = [Buf() for _ in range(8)]
                VT = [cv.take([D], F32) for _ in range(2)]; VTb = [Buf(), Buf()]
                VH = [cv.take([D], BF16) for _ in range(2)]; VHb = [Buf(), Buf()]
                CT = cv.take([8, TT], BF16); CTb = [Buf() for _ in range(8)]
                TC1 = cv.take([8, 128], F32); TC = [TC1, TC1]; TCb1 = Buf(); TCb = [TCb1, TCb1]
                B2 = cv.take([8, 128], F32); B2b = Buf()
                STA = [cv.take([16], F32) for _ in range(2)]; STAb = [Buf(), Buf()]
                GT = [cv.take([TT], F32) for _ in range(2)]; GTb = [Buf(), Buf()]
                TMall = cv.take([2, TT], F32); TM = [TMall[:, 0, :], TMall[:, 1, :]]; TMb = [Buf(), Buf()]
                BSB = TMall.rearrange("p a (b c) -> p (a b) c", c=128)
                assert cv.off <= SCRW * 4, cv.off
                if "B" not in phases and "A" not in phases:
                    norm_x("mix_norm", l)
                fw.dma("sp", "ld_bsb", BSB, sbs_d[l].rearrange("(g t) -> g t", g=8).partition_broadcast(128), writes=TMb)
                for hh in range(2):
                    bb = bank("c")
                    mm(bb, PS[bb][:, :], [(ONES[:], wmT[:, hh * 4:hh * 4 + 4, :])], [ONESb, Wb])
                    for g4 in range(4):
                        g = hh * 4 + g4
                        fw.op("dve", lambda E, g=g, g4=g4, bb=bb: E.scalar_tensor_tensor(
                            out=B2[:, g, :], in0=PS[bb][:, g4 * 128:(g4 + 1) * 128], scalar=prm_col(l, "sgu_norm_b", g),
                            in1=BSB[:, g, :], op0=ALU.mult, op1=ALU.add), reads=[PSb[bb], PRMb] + TMb, writes=[B2b])
                for tt in range(2):
                    nrd = [Nb[c][tt] for c in range(8)] + [Wb]
                    for g in range(8):
                        bu = bank("a")
                        mm(bu, PS[bu][:, :], [(wzu[:, kc, g * 128:(g + 1) * 128], N[:, kc, tsl(tt)]) for kc in range(8)], nrd)
                        fw.op("act", lambda E, g=g, bu=bu: E.activation(out=UT[:, g, :], in_=PS[bu][:, :], func=AF.Gelu_apprx_tanh),
                              reads=[PSb[bu]], writes=[UTb[g]])
                    def stage_v(blk):
                        k = blk % 2
                        t0 = tt * TT + blk * 128
                        for hv in range(2):
                            bv = bank("a")
                            mm(bv, PS[bv][:, :], [(N[:, kc, t0:t0 + 128], wzv[:, kc, hv * 512:(hv + 1) * 512]) for kc in range(8)], nrd)
                            fw.op("act", lambda E, k=k, hv=hv, bv=bv: E.activation(out=VT[k][:, hv * 512:(hv + 1) * 512], in_=PS[bv][:, :],
                                                                                  func=AF.Gelu_apprx_tanh), reads=[PSb[bv]], writes=[VTb[k]])
                        for hv in range(2):
                            fw.op("dve", lambda E, k=k, hv=hv: E.bn_stats(out=STA[k][:, hv * 6:(hv + 1) * 6], in_=VT[k][:, hv * 512:(hv + 1) * 512]),
                                  reads=[VTb[k]], writes=[STAb[k]])
                        fw.op("dve", lambda E, k=k: E.bn_aggr(out=STA[k][:, 12:14], in_=STA[k][:, 0:12]), reads=[STAb[k]], writes=[STAb[k]])
                        fw.op("act", lambda E, k=k: E.activation(out=STA[k][:, 14:15], in_=STA[k][:, 13:14], func=AF.Sqrt, bias=eps_ap),
                              reads=[STAb[k], CSTb], writes=[STAb[k]])
                        fw.op("dve", lambda E, k=k: E.reciprocal(out=STA[k][:, 15:16], in_=STA[k][:, 14:15]), reads=[STAb[k]], writes=[STAb[k]])
                        fw.op("dve", lambda E, k=k: E.tensor_scalar(out=VH[k], in0=VT[k], scalar1=STA[k][:, 12:13], scalar2=STA[k][:, 15:16],
                                                                   op0=ALU.subtract, op1=ALU.mult), reads=[VTb[k], STAb[k]], writes=[VHb[k]])

                    def stage_p(blk):
                        k = blk % 2
                        for hg in range(2):
                            bp = bank("c")
                            fw.wait("pe", fw.deps([VHb[k], Wb], [PSb[bp]]))
                            tok = None
                            for g4 in range(4):
                                g = hg * 4 + g4
                                tok = fw.raw("pe", lambda E, k=k, g=g, g4=g4, bp=bp: E.matmul(
                                    out=PS[bp][:, g4 * 128:(g4 + 1) * 128], lhsT=VH[k][:, g * 128:(g + 1) * 128], rhs=wmT[:, g, :],
                                    start=True, stop=True))
                            fw.mark(tok, [VHb[k], Wb], [PSb[bp]])
                            for g4 in range(4):
                                g = hg * 4 + g4
                                fw.op("dve", lambda E, k=k, g=g, g4=g4, bp=bp: E.scalar_tensor_tensor(
                                    out=TC[k][:, g, :], in0=PS[bp][:, g4 * 128:(g4 + 1) * 128], scalar=prm_col(l, "sgu_norm_g", g),
                                    in1=B2[:, g, :], op0=ALU.mult, op1=ALU.add), reads=[PSb[bp], B2b, PRMb], writes=[TCb[k]])
                        fw.op("pool", lambda E, k=k, blk=blk: E.tensor_tensor(out=CT[:, :, blk * 128:(blk + 1) * 128], in0=TC[k][:],
                                                                             in1=UT[:, :, blk * 128:(blk + 1) * 128], op=ALU.mult),
                              reads=[TCb[k]] + UTb, writes=CTb)

                    for i in range(4 + 1):
                        if i < 4:
                            stage_v(i)
                        if i >= 1:
                            stage_p(i - 1)
                    gate_merge(l, 2, tt, wrc, CT, CTb, wgl, [Wob], TM, TMb, GT, GTb)
                mix_first[0] = False
            return dmas, comp, [], post
        f._name = "C1"
        return f

    def st_C2(l):
        def f(W, Wb, Wo, Wob):
            wrc, o = wview(W, 0, [8, D])
            wgl, o = wview(W, o, [8, D])
            dmas = [(wrc, kview(wrc_d[l])), (wgl, kview(win_d[l, :, C_GL + 2 * D:C_GL + 3 * D]))]
            return dmas, (lambda: None)
        f._name = "C2"
        return f

    def st_wout(l):
        def f(W, Wb, Wo, Wob):
            wo, o = wview(W, 0, [8, D])
            dmas = [(wo, kview(wout_d[l]))]

            def comp():
                for tt in range(2):
                    rd = [MGb[c][tt] for c in range(8)] + [Wb]
                    for d in range(8):
                        bd = bank("b")
                        mm(bd, PS[bd][:, :], [(wo[:, c, d * 128:(d + 1) * 128], MG[:, c, tsl(tt)]) for c in range(8)], rd)
                        fw.op("dve", lambda E, d=d, tt=tt, bd=bd: E.tensor_tensor(out=X[:, d, tsl(tt)], in0=X[:, d, tsl(tt)], in1=PS[bd][:, :], op=ALU.add),
                              reads=[PSb[bd], Xb[d][tt]], writes=[Xb[d][tt]])
                mix_first[0] = True
            return dmas, comp
        f._name = "wout"
        return f

    def st_ple(l, sq, seg):
        def f(W, Wb, Wo, Wob):
            wpg, o = wview(W, 0, [8, D])
            wpp, o = wview(W, o, [2, D])
            dmas = [(wpg, kview(wpg_d[l])), (wpp, kview(wpp_d[l]))]

            def comp():
                norm_x("ple_norm", l)
                fw.barrier()
                cv = Carver(SCR)
                PTK = [cv.take([PLE], F32) for _ in range(2)]; PTKb = [Buf(), Buf()]
                PTT = cv.take([2, TS], BF16); PTTb = [Buf(), Buf()]
                GT = [cv.take([TT], F32) for _ in range(2)]; GTb = [Buf(), Buf()]
                TM = [cv.take([TT], F32) for _ in range(2)]; TMb = [Buf(), Buf()]
                for tb in range(8):
                    k = tb % 2
                    r0 = sq * S + seg * TS + tb * 128
                    fw.dma("sp", "pl%d" % k, PTK[k], p_d[l, r0:r0 + 128, :], writes=[PTKb[k]])
                    bk = bank("c")
                    fw.wait("pe", fw.deps([PTKb[k], IDENTb], [PSb[bk]]))
                    tok = None
                    for j in range(2):
                        tok = fw.raw("pe", lambda E, j=j, k=k, bk=bk: E.transpose(out=PS[bk][:, j * 128:(j + 1) * 128],
                                                                              in_=PTK[k][:, j * 128:(j + 1) * 128], identity=IDENT[:]))
                    fw.mark(tok, [PTKb[k], IDENTb], [PSb[bk]])
                    fw.op("act", lambda E, tb=tb, bk=bk: E.activation(out=PTT[:, :, tb * 128:(tb + 1) * 128],
                                                                  in_=PS[bk][:, 0:256].rearrange("p (a b) -> p a b", a=2), func=AF.Copy),
                          reads=[PSb[bk]], writes=[PTTb[tb // 4]])
                for tt in range(2):
                    nrd = [Nb[c][tt] for c in range(8)] + [Wb]
                    for d in range(8):
                        bg, bp = bank("a"), bank("a")
                        mm(bg, PS[bg][:, :], [(wpg[:, c, d * 128:(d + 1) * 128], N[:, c, tsl(tt)]) for c in range(8)], nrd)
                        mm(bp, PS[bp][:, :], [(wpp[:, j, d * 128:(d + 1) * 128], PTT[:, j, tsl(tt)]) for j in range(2)], [PTTb[tt], Wb])
                        k = d % 2
                        fw.op("act", lambda E, k=k, bg=bg: E.activation(out=GT[k], in_=PS[bg][:, :], func=AF.Sigmoid),
                              reads=[PSb[bg]], writes=[GTb[k]])
                        fw.op("dve", lambda E, k=k, bp=bp: E.tensor_tensor(out=TM[k], in0=GT[k], in1=PS[bp][:, :], op=ALU.mult),
                              reads=[GTb[k], PSb[bp]], writes=[TMb[k]])
                        fw.op("pool", lambda E, k=k, d=d, tt=tt: E.tensor_tensor(out=X[:, d, tsl(tt)], in0=X[:, d, tsl(tt)], in1=TM[k], op=ALU.add),
                              reads=[TMb[k], Xb[d][tt]], writes=[Xb[d][tt]])
            return dmas, comp
        f._name = "ple"
        return f

    for sq in range(n_seq):
        for seg in range(n_seg):
            stages.append(st_load_x(sq, seg))
            for l in range(n_layers):
                if "ffn1" in phases:
                    for gi in range(len(FFN_GROUPS)):
                        stages.append(st_ffn(l, 0, gi))
                if "B" in phases:
                    stages.append(st_B1(l, sq, seg))
                    stages.append(st_B2(l))
                if "A" in phases:
                    stages.append(st_A1(l, sq, seg))
                    stages.append(st_A2(l))
                if "C" in phases:
                    stages.append(st_C1(l, sq, seg))
                    stages.append(st_C2(l))
                if "wout" in phases:
                    stages.append(st_wout(l))
                if "ffn2" in phases:
                    for gi in range(len(FFN_GROUPS)):
                        stages.append(st_ffn(l, 1, gi))
                if "ple" in phases:
                    stages.append(st_ple(l, sq, seg))
            stages.append(st_store(sq, seg))

    half = 0
    pending = None
    marks = []
    nc._marks = marks

    def pe_count():
        return sum(1 for it in fw.q["pe"] if it[0] == "o")

    for sfn in stages:
        res = sfn(WA[half], WAb[half], WA[1 - half], WAb[1 - half])
        dmas, comp = res[0], res[1]
        pre = res[2] if len(res) > 2 else []
        post = res[3] if len(res) > 3 else []
        if dmas:
            hb = WAb[half]
            fw.wait("pool", fw.deps([], [hb]))
            tok = None
            for kind, ap in pre:
                tok = fw.raw("pool", lambda E, ap=ap: E.memset(ap, 0.0))
            if tok is not None:
                fw.wait("pool", {tok[0]: tok[1]})
            for dst, src in dmas:
                tok = fw.dma_raw("pool", "wl%d" % half, dst, src)
            hb.lw = tok
            hb.rd = {}
            for kind, ap in post:
                fw.op("pool", lambda E, ap=ap: E.memset(ap, 0.0), reads=[], writes=[hb])
            half = 1 - half
        if pending is not None:
            marks.append((pending_name, pe_count()))
            pending()
        pending = comp
        pending_name = getattr(sfn, "_name", "?")
    marks.append((pending_name, pe_count()))
    pending()
    marks.append(("end", pe_count()))
    fw.wait("sp", fw.deps([], [], extra=out_toks))
    fw.emit()
    st.close()
    return nc


_CACHE = {}


def _host_weights(inp):
    f = lambda k: np.ascontiguousarray(np.asarray(inp[k], dtype=np.float32))
    w_in = f("w_in")
    w = {}
    for k in ("ffn1_w_gu", "ffn2_w_gu", "ffn1_w_down", "ffn2_w_down", "mla_w_ukv", "w_read_a", "w_read_b", "w_read_c",
              "w_out", "ple_w_gate", "ple_w_proj", "lru_w_a", "lru_w_x"):
        w[k] = f(k)
    w["w_in"] = w_in
    kr = w_in[:, :, C_KR:C_KR + 32]
    kr_sw = np.concatenate([kr[:, :, 16:32], kr[:, :, 0:16]], axis=-1)
    w["w_attn"] = np.ascontiguousarray(np.concatenate(
        [w_in[:, :, C_CQ:C_CQ + 384], w_in[:, :, C_CKV:C_CKV + 256], w_in[:, :, C_CKV:C_CKV + 64], kr, kr_sw], axis=-1))
    uq = f("mla_w_uq").reshape(L, 384, NH, 96)
    uqx = np.concatenate([uq, uq[..., 80:96], uq[..., 64:80]], axis=-1)
    w["w_uqx"] = np.ascontiguousarray(uqx.reshape(L, 384, NH * 128))
    w["sgu_wT"] = np.ascontiguousarray(f("sgu_w_s").transpose(0, 3, 1, 2))
    w["sgu_b_s"] = np.ascontiguousarray(f("sgu_b_s").reshape(L, 1024))
    prm = np.zeros((PRM_ROWS, 128), np.float32)
    for l in range(L):
        for name, k in PRM_LAYOUT:
            r0 = l * PRM_PER_LAYER + PRM_OFF[name]
            prm[r0:r0 + k] = f(name)[l].reshape(k, 128)
    prm[PRM_FINAL:PRM_FINAL + 8] = f("final_norm").reshape(8, 128)
    w["prm"] = prm
    half = 16
    inv_freq = (10000.0 ** (-np.arange(half, dtype=np.float32) / half)).astype(np.float32)
    ang = np.arange(S, dtype=np.float32)[None, :] * inv_freq[:, None]
    cos, sin = np.cos(ang).astype(np.float32), np.sin(ang).astype(np.float32)
    w["rope"] = np.ascontiguousarray(np.concatenate([cos, cos, -sin, sin], axis=0))
    w["ident"] = np.eye(128, dtype=np.float32)
    return w


def kernel(**inputs):
    x = np.asarray(inputs["x"], dtype=np.float32)
    p = np.asarray(inputs["p"], dtype=np.float32)
    w = _host_weights(inputs)
    if "nc" not in _CACHE:
        _CACHE["nc"] = build()
    nc = _CACHE["nc"]
    in_maps = []
    for c in range(8):
        m = dict(w)
        m["x"] = np.ascontiguousarray(x[2 * c:2 * c + 2].reshape(2 * S, D))
        m["p"] = np.ascontiguousarray(p[:, 2 * c:2 * c + 2].reshape(L, 2 * S, PLE))
        in_maps.append(m)
    res = run_bass_kernel_spmd(nc, in_maps, core_ids=list(range(8)))
    out = np.stack([res.results[c]["out"].reshape(2, S, D) for c in range(8)], axis=0).reshape(16, S, D)
    return np.ascontiguousarray(out.astype(np.float32))
```
